# Optimizing a Trainium2 kernel written in Bass

```python
import math
import jax, jax.numpy as jnp
from jax import lax
import numpy as np

D_MODEL = 2048
BATCH = 32
SEQ = 256
DEPTH = 2
DEC_BATCH = 8
DEC_SEQ = 2048
PAST_LEN = 256

GRID_W = 64
S5_WIDTH = D_MODEL // 4
S5_CH = 16
S5_GROUPS = S5_WIDTH // S5_CH
S5_STATE = 64
RWKV_WIDTH = D_MODEL // 4
RWKV_HEAD_DIM = 64
RWKV_HEADS = RWKV_WIDTH // RWKV_HEAD_DIM
DECAY_LORA = 64
ICLR_LORA = 64
GATE_LORA = 128
RWKV_IN = 3 * RWKV_WIDTH + DECAY_LORA + ICLR_LORA + GATE_LORA
MLA_WIDTH = D_MODEL - S5_WIDTH - RWKV_WIDTH
MLA_V_DIM = 128
MLA_HEADS = MLA_WIDTH // MLA_V_DIM
MLA_NOPE_DIM = 128
MLA_ROPE_DIM = 64
MLA_Q_RANK = D_MODEL // 4
MLA_KV_RANK = D_MODEL // 8
MLA_IN = MLA_Q_RANK + MLA_KV_RANK + MLA_ROPE_DIM
IN_COLS = S5_WIDTH + RWKV_IN + MLA_IN
MIX_WIDTH = S5_WIDTH + RWKV_WIDTH + MLA_WIDTH
D_FF = 4 * D_MODEL
N_MOD = 6
Q_BLOCK = 128
ROPE_THETA = 10000.0
NORM_EPS = 1e-6
GN_EPS = 64e-5

kernel_name = 'hybrid_s5_rwkv7_mla_prefix_dit_step'


def rms_norm(x, g):
    xf = x.astype(jnp.float32)
    y = xf * lax.rsqrt(jnp.mean(xf * xf, axis=-1, keepdims=True) + NORM_EPS)
    return (y * g.astype(jnp.float32)).astype(x.dtype)


def modulation(cond, w_ada, b_ada):
    m = jax.nn.silu(cond) @ w_ada + b_ada
    return jnp.split(m[..., None, :], N_MOD, axis=-1)


def centred_shift(p):
    prev = jnp.pad(p[:, :-1], ((0, 0), (1, 0), (0, 0)))
    nxt = jnp.pad(p[:, 1:], ((0, 0), (0, 1), (0, 0)))
    return 0.5 * (prev + nxt)


def rotate_half(x):
    x1, x2 = jnp.split(x, 2, axis=-1)
    return jnp.concatenate([-x2, x1], axis=-1)


def axial_rope_tables(length):
    rows = length // GRID_W
    row_pos = jnp.repeat(jnp.arange(rows, dtype=jnp.float32), GRID_W)
    col_pos = jnp.tile(jnp.arange(GRID_W, dtype=jnp.float32), rows)
    axis_dim = MLA_ROPE_DIM // 2
    inv_freq = 1.0 / (ROPE_THETA ** (jnp.arange(0, axis_dim, 2, dtype=jnp.float32) / axis_dim))
    ang_r = row_pos[:, None] * inv_freq[None, :]
    ang_c = col_pos[:, None] * inv_freq[None, :]
    ang = jnp.concatenate([ang_r, ang_r, ang_c, ang_c], axis=-1)
    return jnp.cos(ang), jnp.sin(ang)


def apply_axial_rope(x, cos, sin):
    xr, xc = jnp.split(x, 2, axis=-1)
    rot = jnp.concatenate([rotate_half(xr), rotate_half(xc)], axis=-1)
    return (x * cos + rot * sin).astype(x.dtype)


def _ssm_combine(e1, e2):
    a1, b1 = e1
    a2, b2 = e2
    return a2 * a1, a2 * b1 + b2


def s5_mixer(u, a_re, a_im, log_dt, b_re, b_im, c_re, c_im, d_skip, w_glu, state0):
    bsz, seq, _ = u.shape
    f32 = jnp.float32
    uf = u.astype(f32).reshape(bsz, seq, S5_GROUPS, S5_CH)
    h0 = lax.complex(state0[..., 0].astype(f32), state0[..., 1].astype(f32))
    lam = lax.complex(a_re.astype(f32), a_im.astype(f32))
    dt = jnp.exp(log_dt.astype(f32))[..., None]
    lam_bar = jnp.exp(lam * dt)
    b_mat = lax.complex(b_re.astype(f32), b_im.astype(f32))
    b_bar = ((lam_bar - 1.0) / lam)[..., None] * b_mat
    c_mat = lax.complex(c_re.astype(f32), c_im.astype(f32))
    uc = uf.astype(jnp.complex64)
    outs = []
    finals = []
    for d, rev in enumerate((False, True)):
        bu = jnp.einsum('blgc,gpc->blgp', uc, b_bar[d])
        edge = seq - 1 if rev else 0
        bu = bu.at[:, edge].add(lam_bar[d] * h0[:, d])
        lam_seq = jnp.broadcast_to(lam_bar[d], bu.shape)
        _, h = lax.associative_scan(_ssm_combine, (lam_seq, bu), reverse=rev, axis=1)
        finals.append(h[:, 0] if rev else h[:, -1])
        outs.append(jnp.einsum('blgp,gcp->blgc', h, c_mat[d]).real)
    y = outs[0] + outs[1] + uf * d_skip.astype(f32).reshape(S5_GROUPS, S5_CH)
    y = jax.nn.gelu(y.reshape(bsz, seq, S5_WIDTH))
    val, gate = jnp.split(y @ w_glu.astype(f32), 2, axis=-1)
    h_fin = jnp.stack(finals, axis=1)
    return (val * jax.nn.sigmoid(gate)).astype(u.dtype), jnp.stack([h_fin.real, h_fin.imag], axis=-1)


def rwkv_scan(r, w, k, v, a, b, s0, reverse):
    xs = tuple(jnp.moveaxis(t, 1, 0) for t in (r, w, k, v, a, b))

    def step(s, inp):
        r_t, w_t, k_t, v_t, a_t, b_t = inp
        sa = jnp.einsum('bhij,bhj->bhi', s, a_t)
        s = s * w_t[:, :, None, :] + sa[..., None] * b_t[:, :, None, :] + v_t[..., None] * k_t[:, :, None, :]
        return s, jnp.einsum('bhij,bhj->bhi', s, r_t)

    s_fin, ys = lax.scan(step, s0.astype(jnp.float32), xs, reverse=reverse)
    return jnp.moveaxis(ys, 0, 1), s_fin


def rwkv_mixer(p, mu, w0, w2, a0, a2, g2, k_k, k_a, r_k, ln_w, ln_b, s0):
    bsz, seq, _ = p.shape
    f32 = jnp.float32
    hd = (bsz, seq, RWKV_HEADS, RWKV_HEAD_DIM)
    p = p + mu * (centred_shift(p) - p)
    cw = RWKV_WIDTH
    r, k, v, wl, al, gl = jnp.split(p, [cw, 2 * cw, 3 * cw, 3 * cw + DECAY_LORA, 3 * cw + DECAY_LORA + ICLR_LORA], axis=-1)
    rf = r.astype(f32).reshape(hd)
    vf = v.astype(f32).reshape(hd)
    kk = (k * k_k).astype(f32).reshape(hd)
    kk = kk * lax.rsqrt(jnp.sum(kk * kk, axis=-1, keepdims=True) + 1e-12)
    g = (jax.nn.sigmoid(gl) @ g2).astype(f32)
    tw = jnp.tanh(wl)
    ys, ks, finals = [], [], []
    for d, rev in enumerate((False, True)):
        w = -jax.nn.softplus(-(w0[d] + tw @ w2[d]).astype(f32)) - 0.5
        decay = jnp.exp(-jnp.exp(w)).reshape(hd)
        a = jax.nn.sigmoid((a0[d] + al @ a2[d]).astype(f32))
        kd = (k.astype(f32) * (1.0 + (a - 1.0) * k_a.astype(f32))).reshape(hd)
        yd, sf = rwkv_scan(rf, decay, kd, vf, -kk, kk * a.reshape(hd), s0[:, d], rev)
        ys.append(yd)
        ks.append(kd)
        finals.append(sf)
    y = ys[0] + ys[1]
    mean = jnp.mean(y, axis=-1, keepdims=True)
    var = jnp.mean(jnp.square(y - mean), axis=-1, keepdims=True)
    yn = (y - mean) * lax.rsqrt(var + GN_EPS)
    yn = yn * ln_w.astype(f32).reshape(RWKV_HEADS, RWKV_HEAD_DIM) + ln_b.astype(f32).reshape(RWKV_HEADS, RWKV_HEAD_DIM)
    bonus = jnp.sum(rf * (ks[0] + ks[1]) * r_k.astype(f32), axis=-1, keepdims=True) * vf
    out = (yn + bonus).reshape(bsz, seq, cw) * g
    return out.astype(p.dtype), jnp.stack(finals, axis=1)


def block_attention(q, k, v):
    bsz, lq, nh, dq = q.shape
    nb = lq // Q_BLOCK
    qb = q.reshape(bsz, nb, Q_BLOCK, nh, dq).transpose(1, 0, 2, 3, 4)
    scale = dq ** -0.5

    def one_block(q_blk):
        s = jnp.einsum('bqhd,bkhd->bhqk', q_blk, k, preferred_element_type=jnp.float32) * scale
        pr = jax.nn.softmax(s, axis=-1)
        return jnp.einsum('bhqk,bkhd->bqhd', pr.astype(v.dtype), v)

    o = lax.map(one_block, qb)
    return o.transpose(1, 0, 2, 3, 4).reshape(bsz, lq, nh, v.shape[-1])


def mla_mixer(p, q_norm, w_uq, kv_norm, w_ukv, ctx_ckv, ctx_krope, rope):
    bsz, seq, _ = p.shape
    c_q, c_kv, k_rope = jnp.split(p, [MLA_Q_RANK, MLA_Q_RANK + MLA_KV_RANK], axis=-1)
    q = (rms_norm(c_q, q_norm) @ w_uq).reshape(bsz, seq, MLA_HEADS, MLA_NOPE_DIM + MLA_ROPE_DIM)
    q_nope, q_rope = jnp.split(q, [MLA_NOPE_DIM], axis=-1)
    ckv_n = rms_norm(c_kv, kv_norm)
    if rope is None:
        keys_ckv, keys_rope = ckv_n, k_rope
    else:
        cos, sin = rope
        q_rope = apply_axial_rope(q_rope, cos[:, None, :], sin[:, None, :])
        keys_ckv = jnp.concatenate([ckv_n, ctx_ckv.astype(ckv_n.dtype)], axis=1)
        keys_rope = jnp.concatenate([apply_axial_rope(k_rope, cos, sin), ctx_krope.astype(k_rope.dtype)], axis=1)
    n_keys = keys_ckv.shape[1]
    kv = (keys_ckv @ w_ukv).reshape(bsz, n_keys, MLA_HEADS, MLA_NOPE_DIM + MLA_V_DIM)
    k_nope, v = jnp.split(kv, [MLA_NOPE_DIM], axis=-1)
    k_pe = jnp.broadcast_to(keys_rope[:, :, None, :], (bsz, n_keys, MLA_HEADS, MLA_ROPE_DIM))
    k = jnp.concatenate([k_nope, k_pe], axis=-1)
    qf = jnp.concatenate([q_nope, q_rope], axis=-1)
    o = block_attention(qf, k, v)
    return o.reshape(bsz, seq, MLA_WIDTH), ckv_n, k_rope


def trunk_layer(x, cond, lw, s5_state0, rwkv_state0, ctx_ckv, ctx_krope, rope):
    sh_m, sc_m, g_m, sh_f, sc_f, g_f = modulation(cond, lw['w_ada'], lw['b_ada'])
    h = rms_norm(x, lw['norm_mix']) * (1.0 + sc_m) + sh_m
    proj = h @ lw['w_in']
    u_s5, p_rwkv, p_mla = jnp.split(proj, [S5_WIDTH, S5_WIDTH + RWKV_IN], axis=-1)
    y_s5, s5_fin = s5_mixer(u_s5, lw['s5_a_re'], lw['s5_a_im'], lw['s5_log_dt'], lw['s5_b_re'], lw['s5_b_im'],
                            lw['s5_c_re'], lw['s5_c_im'], lw['s5_d'], lw['s5_w_glu'], s5_state0)
    y_rwkv, rwkv_fin = rwkv_mixer(p_rwkv, lw['rwkv_mu'], lw['rwkv_w0'], lw['rwkv_w2'], lw['rwkv_a0'], lw['rwkv_a2'],
                                  lw['rwkv_g2'], lw['rwkv_k_k'], lw['rwkv_k_a'], lw['rwkv_r_k'], lw['rwkv_ln_w'],
                                  lw['rwkv_ln_b'], rwkv_state0)
    y_mla, ckv_n, k_rope = mla_mixer(p_mla, lw['mla_q_norm'], lw['mla_w_uq'], lw['mla_kv_norm'], lw['mla_w_ukv'],
                                     ctx_ckv, ctx_krope, rope)
    merged = jnp.concatenate([rms_norm(y_s5, lw['s5_out_norm']), y_rwkv, rms_norm(y_mla, lw['mla_out_norm'])], axis=-1)
    x = x + g_m * (merged @ lw['w_out'])
    h = rms_norm(x, lw['norm_mlp']) * (1.0 + sc_f) + sh_f
    x = x + g_f * (jnp.square(jax.nn.relu(h @ lw['mlp_w1'])) @ lw['mlp_w2'])
    return x, (ckv_n, k_rope, s5_fin, rwkv_fin)


def setup_inputs(seed: int = 0) -> dict:
    key = jax.random.key(seed)
    keys = iter(jax.random.split(key, 64))
    f32 = jnp.float32

    def nrm(shape, scale):
        return scale * jax.random.normal(next(keys), shape, f32)

    def gain(shape):
        return 1.0 + nrm(shape, 0.02)

    L, G, P, H, N = DEPTH, S5_GROUPS, S5_STATE, RWKV_HEADS, RWKV_HEAD_DIM
    ratio = jnp.arange(RWKV_WIDTH, dtype=f32) / (RWKV_WIDTH - 1)
    return {
        'x_prompt': nrm((BATCH, SEQ, D_MODEL), 1.0),
        'x_sample': nrm((DEC_BATCH, DEC_SEQ, D_MODEL), 1.0),
        'cache_mla_ckv': nrm((DEC_BATCH, L, PAST_LEN, MLA_KV_RANK), 1.0),
        'cache_mla_krope': nrm((DEC_BATCH, L, PAST_LEN, MLA_ROPE_DIM), 1.0),
        'state_s5': nrm((DEC_BATCH, L, 2, G, P, 2), 0.1),
        'state_rwkv': nrm((DEC_BATCH, L, 2, H, N, N), 0.1),
        'c': nrm((DEC_BATCH, D_MODEL), 1.0),
        'c_ctx': nrm((D_MODEL,), 1.0),
        'norm_mix': gain((L, D_MODEL)),
        'norm_mlp': gain((L, D_MODEL)),
        'norm_final': gain((D_MODEL,)),
        'w_ada': nrm((L, D_MODEL, N_MOD * D_MODEL), 0.3 * D_MODEL ** -0.5),
        'b_ada': nrm((L, N_MOD * D_MODEL), 0.02),
        'w_in': nrm((L, D_MODEL, IN_COLS), D_MODEL ** -0.5),
        'w_out': nrm((L, MIX_WIDTH, D_MODEL), MIX_WIDTH ** -0.5),
        's5_a_re': -0.5 + nrm((L, 2, G, P), 0.01),
        's5_a_im': math.pi * jnp.arange(P, dtype=f32) + nrm((L, 2, G, P), 0.01),
        's5_log_dt': jax.random.uniform(next(keys), (L, 2, G), dtype=f32, minval=math.log(1e-3), maxval=math.log(1e-1)),
        's5_b_re': nrm((L, 2, G, P, S5_CH), (2 * S5_CH) ** -0.5),
        's5_b_im': nrm((L, 2, G, P, S5_CH), (2 * S5_CH) ** -0.5),
        's5_c_re': nrm((L, 2, G, S5_CH, P), (2 * P) ** -0.5),
        's5_c_im': nrm((L, 2, G, S5_CH, P), (2 * P) ** -0.5),
        's5_d': nrm((L, S5_WIDTH), 1.0),
        's5_w_glu': nrm((L, S5_WIDTH, 2 * S5_WIDTH), S5_WIDTH ** -0.5),
        's5_out_norm': gain((L, S5_WIDTH)),
        'rwkv_mu': jax.random.uniform(next(keys), (L, RWKV_IN), dtype=f32),
        'rwkv_w0': -6.0 + 5.0 * ratio ** 0.9 + nrm((L, 2, RWKV_WIDTH), 0.01),
        'rwkv_w2': nrm((L, 2, DECAY_LORA, RWKV_WIDTH), 0.1 * DECAY_LORA ** -0.5),
        'rwkv_a0': nrm((L, 2, RWKV_WIDTH), 0.1),
        'rwkv_a2': nrm((L, 2, ICLR_LORA, RWKV_WIDTH), 0.1 * ICLR_LORA ** -0.5),
        'rwkv_g2': nrm((L, GATE_LORA, RWKV_WIDTH), GATE_LORA ** -0.5),
        'rwkv_k_k': 0.85 + nrm((L, RWKV_WIDTH), 0.02),
        'rwkv_k_a': 1.0 + nrm((L, RWKV_WIDTH), 0.02),
        'rwkv_r_k': nrm((L, H, N), 0.1),
        'rwkv_ln_w': gain((L, RWKV_WIDTH)),
        'rwkv_ln_b': nrm((L, RWKV_WIDTH), 0.01),
        'mla_q_norm': gain((L, MLA_Q_RANK)),
        'mla_w_uq': nrm((L, MLA_Q_RANK, MLA_HEADS * (MLA_NOPE_DIM + MLA_ROPE_DIM)), MLA_Q_RANK ** -0.5),
        'mla_kv_norm': gain((L, MLA_KV_RANK)),
        'mla_w_ukv': nrm((L, MLA_KV_RANK, MLA_HEADS * (MLA_NOPE_DIM + MLA_V_DIM)), MLA_KV_RANK ** -0.5),
        'mla_out_norm': gain((L, MLA_WIDTH)),
        'mlp_w1': nrm((L, D_MODEL, D_FF), D_MODEL ** -0.5),
        'mlp_w2': nrm((L, D_FF, D_MODEL), D_FF ** -0.5),
    }


def reference(x_prompt, x_sample, cache_mla_ckv, cache_mla_krope, state_s5, state_rwkv, c, c_ctx,
              norm_mix, norm_mlp, norm_final, w_ada, b_ada, w_in, w_out,
              s5_a_re, s5_a_im, s5_log_dt, s5_b_re, s5_b_im, s5_c_re, s5_c_im, s5_d, s5_w_glu, s5_out_norm,
              rwkv_mu, rwkv_w0, rwkv_w2, rwkv_a0, rwkv_a2, rwkv_g2, rwkv_k_k, rwkv_k_a, rwkv_r_k,
              rwkv_ln_w, rwkv_ln_b,
              mla_q_norm, mla_w_uq, mla_kv_norm, mla_w_ukv, mla_out_norm,
              mlp_w1, mlp_w2):
    rope = axial_rope_tables(x_sample.shape[1])
    bsz_ctx = x_prompt.shape[0]
    zero_s5 = jnp.zeros((bsz_ctx, 2, S5_GROUPS, S5_STATE, 2), jnp.float32)
    zero_rwkv = jnp.zeros((bsz_ctx, 2, RWKV_HEADS, RWKV_HEAD_DIM, RWKV_HEAD_DIM), jnp.float32)
    xp, xs = x_prompt, x_sample
    new_ckv, new_krope, new_s5, new_rwkv = [], [], [], []
    for l in range(DEPTH):
        lw = {
            'norm_mix': norm_mix[l], 'norm_mlp': norm_mlp[l], 'w_ada': w_ada[l], 'b_ada': b_ada[l],
            'w_in': w_in[l], 'w_out': w_out[l],
            's5_a_re': s5_a_re[l], 's5_a_im': s5_a_im[l], 's5_log_dt': s5_log_dt[l],
            's5_b_re': s5_b_re[l], 's5_b_im': s5_b_im[l], 's5_c_re': s5_c_re[l], 's5_c_im': s5_c_im[l],
            's5_d': s5_d[l], 's5_w_glu': s5_w_glu[l], 's5_out_norm': s5_out_norm[l],
            'rwkv_mu': rwkv_mu[l], 'rwkv_w0': rwkv_w0[l], 'rwkv_w2': rwkv_w2[l], 'rwkv_a0': rwkv_a0[l],
            'rwkv_a2': rwkv_a2[l], 'rwkv_g2': rwkv_g2[l], 'rwkv_k_k': rwkv_k_k[l], 'rwkv_k_a': rwkv_k_a[l],
            'rwkv_r_k': rwkv_r_k[l], 'rwkv_ln_w': rwkv_ln_w[l], 'rwkv_ln_b': rwkv_ln_b[l],
            'mla_q_norm': mla_q_norm[l], 'mla_w_uq': mla_w_uq[l], 'mla_kv_norm': mla_kv_norm[l],
            'mla_w_ukv': mla_w_ukv[l], 'mla_out_norm': mla_out_norm[l],
            'mlp_w1': mlp_w1[l], 'mlp_w2': mlp_w2[l],
        }
        xp, (ckv_l, krope_l, s5_l, rwkv_l) = trunk_layer(xp, c_ctx, lw, zero_s5, zero_rwkv, None, None, None)
        new_ckv.append(ckv_l)
        new_krope.append(krope_l)
        new_s5.append(s5_l)
        new_rwkv.append(rwkv_l)
        xs, _ = trunk_layer(xs, c, lw, state_s5[:, l], state_rwkv[:, l], cache_mla_ckv[:, l], cache_mla_krope[:, l], rope)
    y_prompt = rms_norm(xp, norm_final)
    y_sample = rms_norm(xs, norm_final)
    return (y_prompt, y_sample, jnp.stack(new_ckv, axis=1), jnp.stack(new_krope, axis=1),
            jnp.stack(new_s5, axis=1), jnp.stack(new_rwkv, axis=1))
```

```python
import numpy as np
import concourse.bass as bass
import concourse.mybir as mybir

F32 = mybir.dt.float32
BF16 = mybir.dt.bfloat16
AF = mybir.ActivationFunctionType
ALU = mybir.AluOpType
AX = mybir.AxisListType

COMPUTE = ("pe", "act", "dve", "pool", "sp")
EPOCH = 30000
N_DMA_SLOTS = {"sp": 20, "pool": 12, "act": 8}


class Op:
    __slots__ = ("eng", "fn", "reads", "writes", "dma", "deps", "sig", "slot", "slot_cnt", "idx", "kind")

    def __init__(self, eng, fn, reads, writes, dma):
        self.eng = eng
        self.fn = fn
        self.reads = tuple(reads)
        self.writes = tuple(writes)
        self.dma = dma
        self.deps = ()
        self.sig = None
        self.slot = None
        self.slot_cnt = None
        self.kind = None


class Prog:
    def __init__(self, nc, same_engine_sync=True):
        self.nc = nc
        self.ops = []
        self.same_engine_sync = same_engine_sync

    def op(self, eng, fn, reads=(), writes=()):
        self.ops.append(Op(eng, fn, reads, writes, False))

    def dma(self, q, fn, reads=(), writes=()):
        self.ops.append(Op(q, fn, reads, writes, True))

    def barrier(self):
        n = getattr(self, "_nbar", 0)
        self._nbar = n + 1
        engs = ("pe", "act", "dve", "pool", "sp")
        for e in engs:
            o = Op(e, (lambda en: en.nop()) if e == "sp" else (lambda en: en.drain()), (), [("bar", n, e)], False)
            o.kind = "arrive"
            self.ops.append(o)
        for e in engs:
            o = Op(e, lambda en: en.nop(), [("bar", n, x) for x in engs], (), False)
            o.kind = "depart"
            self.ops.append(o)

    def emit(self, stack):
        nc = self.nc
        ops = self.ops
        last_writer = {}
        readers = {}
        for i, o in enumerate(ops):
            o.idx = i
            deps = set()
            for k in o.reads:
                w = last_writer.get(k)
                if w is not None:
                    deps.add(w)
            for k in o.writes:
                w = last_writer.get(k)
                if w is not None:
                    deps.add(w)
                for r in readers.get(k, ()):
                    deps.add(r)
            deps.discard(i)
            o.deps = deps
            for k in o.reads:
                readers.setdefault(k, []).append(i)
            for k in o.writes:
                last_writer[k] = i
                readers[k] = []
            if o.kind == "depart" and o.eng == "sp":
                last_writer = {}
                readers = {}
        need_sig = set()
        for o in ops:
            for d in o.deps:
                p = ops[d]
                if p.dma:
                    continue
                if p.eng != o.eng or o.dma:
                    need_sig.add(d)
                elif self.same_engine_sync and p.eng != "pe":
                    need_sig.add(d)
        cnt = {e: 0 for e in COMPUTE}
        for o in ops:
            if o.dma:
                continue
            if o.idx in need_sig:
                cnt[o.eng] += 1
                o.sig = cnt[o.eng]
        dcnt = {q: 0 for q in N_DMA_SLOTS}
        slot_uses = {}
        for o in ops:
            if o.dma:
                n = N_DMA_SLOTS[o.eng]
                s = dcnt[o.eng] % n
                dcnt[o.eng] += 1
                o.slot = (o.eng, s)
                slot_uses[o.slot] = slot_uses.get(o.slot, 0) + 1
                o.slot_cnt = slot_uses[o.slot]
        sems = {}
        for e in COMPUTE:
            for ep in range(max(0, cnt[e] - 1) // EPOCH + 1):
                sems[(e, ep)] = stack.enter_context(nc.semaphore(f"s_{e}_{ep}"))
        dsems = {}
        for q, n in N_DMA_SLOTS.items():
            for s in range(min(n, dcnt[q])):
                dsems[(q, s)] = stack.enter_context(nc.semaphore(f"d_{q}_{s}"))
        self.stats = dict(cnt=cnt, dcnt=dcnt, nops=len(ops))

        per_eng = {e: [] for e in ("pe", "act", "dve", "pool", "sp")}
        for o in ops:
            per_eng[o.eng].append(o)

        def run_engine(ename, eng):
            waited = {}
            dwaited = {}
            issued = {}
            for o in per_eng[ename]:
                if o.dma:
                    issued[o.slot] = o.slot_cnt
                if o.kind == "arrive":
                    for sl, v in issued.items():
                        if dwaited.get(sl, 0) < v:
                            eng.wait_ge(dsems[sl], 16 * v)
                            dwaited[sl] = v
                cw = {}
                dw = {}
                for d in o.deps:
                    p = ops[d]
                    if p.dma:
                        if dwaited.get(p.slot, 0) < p.slot_cnt:
                            dw[p.slot] = max(dw.get(p.slot, 0), p.slot_cnt)
                    else:
                        if p.sig is None:
                            continue
                        if p.eng == ename and not o.dma and not (self.same_engine_sync and ename != "pe"):
                            continue
                        if waited.get(p.eng, 0) < p.sig:
                            cw[p.eng] = max(cw.get(p.eng, 0), p.sig)
                if o.dma and o.slot_cnt > 1:
                    if dwaited.get(o.slot, 0) < o.slot_cnt - 1:
                        dw[o.slot] = max(dw.get(o.slot, 0), o.slot_cnt - 1)
                for pe_, v in cw.items():
                    ep, r = divmod(v - 1, EPOCH)
                    eng.wait_ge(sems[(pe_, ep)], r + 1)
                    waited[pe_] = v
                for sl, v in dw.items():
                    eng.wait_ge(dsems[sl], 16 * v)
                    dwaited[sl] = v
                ins = o.fn(eng)
                if o.dma:
                    ins.then_inc(dsems[o.slot], 16)
                elif o.sig is not None:
                    ins.then_inc(sems[(o.eng, (o.sig - 1) // EPOCH)], 1)
            if ename in N_DMA_SLOTS:
                for (q, s), v in slot_uses.items():
                    if q == ename and dwaited.get((q, s), 0) < v:
                        eng.wait_ge(dsems[(q, s)], 16 * v)

        with nc.Block() as block:
            @block.tensor
            def _(e):
                run_engine("pe", e)

            @block.scalar
            def _(e):
                run_engine("act", e)

            @block.vector
            def _(e):
                run_engine("dve", e)

            @block.gpsimd
            def _(e):
                run_engine("pool", e)

            @block.sync
            def _(e):
                run_engine("sp", e)


import math
import numpy as np

PAST = 256


def _load_bf(g, dst, src, shape2, key):
    DMA, CP = g["DMA"], g["CP"]
    bv, bk = g["load_w"](src, shape2)
    np_ = dst.shape[0]
    CP("dve", dst, bv[0:np_], [bk], [key])


def kmix_mla(g, l):
    P, AR, ps, psk, W, I, O, C, VB, VBI = g["P"], g["AR"], g["ps"], g["psk"], g["W"], g["I"], g["O"], g["C"], g["VB"], g["VBI"]
    MM, TRN, ACTF, TT_, TS_, STT, CP, DMA, MSET, RSTD = (g[k] for k in ("MM", "TRN", "ACTF", "TT_", "TS_", "STT", "CP", "DMA", "MSET", "RSTD"))
    ident, ones_bf, rm = g["ident"], g["ones_bf"], g["rm"]
    scr_p, scr_m, scr_r = g["scr_p"], g["scr_m"], g["scr_r"]
    SCALE = 192.0 ** -0.5
    cfg_ = g["cfg"]
    for (t0, T, smp, nsub, Tsub) in ((0, cfg_.TS, True, 1, cfg_.TS), (cfg_.TS, cfg_.NP * cfg_.TP, False, cfg_.NP, cfg_.TP)):
        AR.reset()
        nk = T + (PAST if smp else 0); nkc = nk // 128
        TL = min(512, T); ntl = T // TL
        cqn = AR.get([128, 4, T], BF16); keysT = AR.get([128, 2, nk], BF16); krR = AR.get([64, nk], BF16)
        wukv = AR.get([128, 2, 2048], BF16); wuq = AR.get([128, 4, 1536], BF16)
        kn = AR.get([128, nk], BF16); Vh = AR.get([128, nkc, 128], BF16); qn = AR.get([128, T], BF16)
        qrb = AR.get([64, T], BF16); qrf = AR.get([64, TL]); pT = [AR.get([128, TL], BF16) for _ in range(4)]
        dacc = AR.get([128, TL]); daccb = AR.get([128, TL], BF16)
        rden = AR.get([128, TL]); ost = [AR.get([128, TL]) for _ in range(2)]; sqt = AR.get([128, TL])
        ssacc = AR.get([128, T]); st_cq = AR.get([128, 4, TL]); sq4 = AR.get([128, 4, TL], BF16)
        st_kv = AR.get([128, 2, TL]); ckvn = AR.get([128, 2, TL]); st_kr = AR.get([64, TL])
        cs = AR.get([64, TL]); sn = AR.get([64, TL]); rstd = AR.get([128, TL]); ot = AR.get([128, 256])
        ck = AR.get([128, 2, 256]); kk_ = AR.get([128, 2, 64]); t64 = AR.get([64, TL]); ssb = AR.get([128, TL], BF16)
        wq_src = W["mla_w_uq"][l].rearrange("(c p) n -> p c n", p=128)
        for b_ in range(0, 1536, 512):
            _load_bf(g, wuq[:, :, b_:b_ + 512], wq_src[:, :, b_:b_ + 512], (4, 512), "wuq")
        wk_src = W["mla_w_ukv"][l].rearrange("(c p) n -> p c n", p=128)
        for b_ in range(0, 2048, 1024):
            _load_bf(g, wukv[:, :, b_:b_ + 1024], wk_src[:, :, b_:b_ + 1024], (2, 1024), "wukv")
        qn0, kvn0 = VBI["q_n"], VBI["kv_n"]
        for tl in range(ntl):
            sl = slice(tl * TL, (tl + 1) * TL)
            gsl = slice(t0 + tl * TL, t0 + (tl + 1) * TL)
            DMA(st_cq, scr_p[2304:2816, gsl].rearrange("(c p) t -> p c t", p=128), ["scr_p"], ["st_cq"])
            ACTF(sq4, st_cq, AF.Square, ["st_cq"], ["sq4"])
            for c in range(4):
                MM(ps[4][:, 0:TL], ones_bf[:], sq4[:, c, :], c == 0, c == 3, ["sq4", "ones_bf"], [psk(4)])
            RSTD(rstd, ps[4][:, 0:TL], [psk(4)], ["rstd"], 1.0 / 512, 1e-6)
            for c in range(4):
                STT(cqn[:, c, sl], st_cq[:, c, :], VB[:, qn0 + c:qn0 + c + 1], rstd, ALU.mult, ALU.mult, ["st_cq", "VB", "rstd"], ["cqn"])
            DMA(st_kv, scr_p[2816:3072, gsl].rearrange("(c p) t -> p c t", p=128), ["scr_p"], ["st_kv"])
            ACTF(sq4[:, 0:2, :], st_kv, AF.Square, ["st_kv"], ["sq4"])
            for c in range(2):
                MM(ps[5][:, 0:TL], ones_bf[:], sq4[:, c, :], c == 0, c == 1, ["sq4", "ones_bf"], [psk(5)])
            RSTD(rstd, ps[5][:, 0:TL], [psk(5)], ["rstd"], 1.0 / 256, 1e-6)
            for c in range(2):
                STT(ckvn[:, c, :], st_kv[:, c, :], VB[:, kvn0 + c:kvn0 + c + 1], rstd, ALU.mult, ALU.mult, ["st_kv", "VB", "rstd"], ["ckvn"])
            CP("act", keysT[:, :, sl], ckvn, ["ckvn"], ["keysT"])
            DMA(st_kr, scr_p[3072:3136, gsl], ["scr_p"], ["st_kr"])
            if not smp:
                for nb in range(TL // 128):
                    for c in range(2):
                        TRN(ps[6][:, c * 128:(c + 1) * 128], ckvn[:, c, nb * 128:(nb + 1) * 128], ident[:], ["ckvn", "ident"], [psk(6)])
                    CP("dve", ot, ps[6][:, 0:256], [psk(6)], ["ot"])
                    tg = tl * TL + nb * 128
                    pi = tg // Tsub; tk = tg % Tsub
                    DMA(O["o_ckv"][pi, l, tk:tk + 128, :], ot, ["ot"], ["o_ckv"])
                    TRN(ps[6][:, 256:320], st_kr[:, nb * 128:(nb + 1) * 128], ident[0:64, 0:64], ["st_kr", "ident"], [psk(6)])
                    CP("dve", ot[:, 0:64], ps[6][:, 256:320], [psk(6)], ["ot"])
                    DMA(O["o_kr"][pi, l, tk:tk + 128, :], ot[:, 0:64], ["ot"], ["o_kr"])
                CP("act", krR[:, sl], st_kr, ["st_kr"], ["krR"])
            else:
                DMA(cs, C["cosT"][:, tl * TL:(tl + 1) * TL], (), ["cs"])
                DMA(sn, C["sinT"][:, tl * TL:(tl + 1) * TL], (), ["sn"])
                MM(ps[6][0:64, 0:TL], rm[:], st_kr, True, True, ["rm", "st_kr"], [psk(6)])
                TT_("dve", t64, ps[6][0:64, 0:TL], sn, ALU.mult, [psk(6), "sn"], ["t64"])
                TT_("pool", st_kr, st_kr, cs, ALU.mult, ["st_kr", "cs"], ["st_kr"])
                TT_("dve", krR[:, sl], st_kr, t64, ALU.add, ["st_kr", "t64"], ["krR"])
        if smp:
            DMA(ck, I["c_ckv"][l].rearrange("(n p) f -> p n f", p=128), (), ["ck"])
            DMA(kk_, I["c_kr"][l].rearrange("(n p) f -> p n f", p=128), (), ["kk_"])
            for n_ in range(2):
                for c in range(2):
                    TRN(ps[6][:, c * 128:(c + 1) * 128], ck[:, n_, c * 128:(c + 1) * 128], ident[:], ["ck", "ident"], [psk(6)])
                CP("dve", keysT[:, :, T + n_ * 128:T + (n_ + 1) * 128], ps[6][:, 0:256].rearrange("p (a b) -> p a b", a=2), [psk(6)], ["keysT"])
                TRN(ps[6][0:64, 256:384], kk_[:, n_, :], ident[:], ["kk_", "ident"], [psk(6)])
                CP("dve", krR[:, T + n_ * 128:T + (n_ + 1) * 128], ps[6][0:64, 256:384], [psk(6)], ["krR"])
        for h in range(8):
            for k0 in range(0, nk, 512):
                kw = min(512, nk - k0)
                for kc in range(2):
                    MM(ps[4][:, 0:kw], wukv[:, kc, 256 * h:256 * h + 128], keysT[:, kc, k0:k0 + kw], kc == 0, kc == 1, ["wukv", "keysT"], [psk(4)])
                CP("act", kn[:, k0:k0 + kw], ps[4][:, 0:kw], [psk(4)], ["kn"])
            for q0 in range(0, nkc, 4):
                nq = min(4, nkc - q0)
                for qq in range(nq):
                    kcn = q0 + qq
                    for kc in range(2):
                        MM(ps[5][:, qq * 128:(qq + 1) * 128], keysT[:, kc, kcn * 128:(kcn + 1) * 128], wukv[:, kc, 256 * h + 128:256 * h + 256],
                           kc == 0, kc == 1, ["wukv", "keysT"], [psk(5)])
                CP("dve", Vh[:, q0:q0 + nq, :], ps[5][:, 0:nq * 128].rearrange("p (a b) -> p a b", a=nq), [psk(5)], ["Vh"])
            for tl in range(ntl):
                sl = slice(tl * TL, (tl + 1) * TL)
                for kc in range(4):
                    MM(ps[4][:, 0:TL], wuq[:, kc, 192 * h:192 * h + 128], cqn[:, kc, sl], kc == 0, kc == 3, ["wuq", "cqn"], [psk(4)])
                CP("act", qn[:, sl], ps[4][:, 0:TL], [psk(4)], ["qn"])
                for kc in range(4):
                    MM(ps[6][0:64, 0:TL], wuq[:, kc, 192 * h + 128:192 * h + 192], cqn[:, kc, sl], kc == 0, kc == 3, ["wuq", "cqn"], [psk(6)])
                if not smp:
                    CP("dve", qrb[:, sl], ps[6][0:64, 0:TL], [psk(6)], ["qrb"])
                else:
                    CP("dve", qrf, ps[6][0:64, 0:TL], [psk(6)], ["qrf"])
                    DMA(cs, C["cosT"][:, tl * TL:(tl + 1) * TL], (), ["cs"])
                    DMA(sn, C["sinT"][:, tl * TL:(tl + 1) * TL], (), ["sn"])
                    MM(ps[6][0:64, 0:TL], rm[:], qrf, True, True, ["rm", "qrf"], [psk(6)])
                    TT_("dve", t64, ps[6][0:64, 0:TL], sn, ALU.mult, [psk(6), "sn"], ["t64"])
                    TT_("pool", qrf, qrf, cs, ALU.mult, ["qrf", "cs"], ["qrf"])
                    TT_("dve", qrb[:, sl], qrf, t64, ALU.add, ["qrf", "t64"], ["qrb"])
            sbank = (0, 1, 5, 6)
            if smp:
                qjobs = [(tl * TL, TL, list(range(nkc))) for tl in range(ntl)]
            else:
                qjobs = [(s_ * Tsub, Tsub, list(range(s_ * Tsub // 128, (s_ + 1) * Tsub // 128))) for s_ in range(nsub)]
            for tl, (q0_, QW, kcs) in enumerate(qjobs):
                sl = slice(q0_, q0_ + QW)
                nkq = len(kcs)
                LAG = 2
                for it in range(nkq + LAG):
                    if it < nkq:
                        kc = kcs[it]; a = it % 4; bnk = sbank[a]
                        ksl = slice(kc * 128, (kc + 1) * 128)
                        MM(ps[bnk][:, 0:QW], kn[:, ksl], qn[:, sl], True, False, ["kn", "qn"], [psk(bnk)])
                        MM(ps[bnk][:, 0:QW], krR[:, ksl], qrb[:, sl], False, True, ["krR", "qrb"], [psk(bnk)])
                        ACTF(pT[a][:, 0:QW], ps[bnk][:, 0:QW], AF.Exp, [psk(bnk)], [("pT", a)], scale=SCALE)
                    j_ = it - LAG
                    if j_ >= 0:
                        kc = kcs[j_]; a = j_ % 4
                        MM(ps[2][:, 0:QW], Vh[:, kc, :], pT[a][:, 0:QW], j_ == 0, j_ == nkq - 1, ["Vh", ("pT", a)], [psk(2)])
                        if j_ == 0:
                            CP("dve", dacc[:, 0:QW], pT[a][:, 0:QW], [("pT", a)], ["dacc"])
                        else:
                            TT_("dve", dacc[:, 0:QW], dacc[:, 0:QW], pT[a][:, 0:QW], ALU.add, ["dacc", ("pT", a)], ["dacc"])
                CP("dve", daccb[:, 0:QW], dacc[:, 0:QW], ["dacc"], ["daccb"])
                MM(ps[3][:, 0:QW], ones_bf[:], daccb[:, 0:QW], True, True, ["ones_bf", "daccb"], [psk(3)])
                ob = tl % 2
                P.op("dve", lambda e, o_=rden[:, 0:QW], i_=ps[3][:, 0:QW]: e.reciprocal(o_, i_), [psk(3)], ["rden"])
                TT_("dve", ost[ob][:, 0:QW], ps[2][:, 0:QW], rden[:, 0:QW], ALU.mult, [psk(2), "rden"], [("ost", ob)])
                DMA(scr_m[1024 + 128 * h:1024 + 128 * (h + 1), t0 + q0_:t0 + q0_ + QW], ost[ob][:, 0:QW], [("ost", ob)], ["scr_m"])
                if h == 0:
                    ACTF(ssacc[:, sl], ost[ob][:, 0:QW], AF.Square, [("ost", ob)], ["ssacc"])
                else:
                    ACTF(sqt[:, 0:QW], ost[ob][:, 0:QW], AF.Square, [("ost", ob)], ["sqt"])
                    TT_("pool", ssacc[:, sl], ssacc[:, sl], sqt[:, 0:QW], ALU.add, ["ssacc", "sqt"], ["ssacc"])
        for tl in range(ntl):
            sl = slice(tl * TL, (tl + 1) * TL)
            CP("act", ssb, ssacc[:, sl], ["ssacc"], ["ssb"])
            MM(ps[4][:, 0:TL], ones_bf[:], ssb, True, True, ["ssb", "ones_bf"], [psk(4)])
            RSTD(rstd, ps[4][:, 0:TL], [psk(4)], ["rstd"], 1.0 / 1024, 1e-6)
            DMA(scr_r[1, :, t0 + tl * TL:t0 + (tl + 1) * TL], rstd, ["rstd"], ["scr_r"])
        P.barrier()


def kmix_s5(g, l):
    P, AR, ps, psk, W, I, O, C, VB, VBI = g["P"], g["AR"], g["ps"], g["psk"], g["W"], g["I"], g["O"], g["C"], g["VB"], g["VBI"]
    MM, TRN, ACTF, TT_, TS_, STT, CP, DMA, MSET, RSTD = (g[k] for k in ("MM", "TRN", "ACTF", "TT_", "TS_", "STT", "CP", "DMA", "MSET", "RSTD"))
    ident, ones_bf, sel2, mask4, mask8, twopi = g["ident"], g["ones_bf"], g["sel2"], g["mask4"], g["mask8"], g["twopi"]
    scr_p, scr_m, scr_r = g["scr_p"], g["scr_m"], g["scr_r"]
    AR.reset()
    NLV = 1
    Bre = AR.get([128, 32, 128], BF16); Bim = AR.get([128, 32, 128], BF16)
    Cre = AR.get([128, 32, 128], BF16); Cimn = AR.get([128, 32, 128], BF16)
    pw = AR.get([128, NLV, 3, 32])
    wglu = AR.get([128, 4, 1024], BF16)
    th1 = AR.get([128, 32]); th64 = AR.get([128, 32]); rho = AR.get([128, 32])
    KI = AR.get([128, 32]); JI = AR.get([128, 64])
    DMA(KI, C["ki_tab"], (), ["KI"]); DMA(JI, C["ji_tab"], (), ["JI"])
    rst256 = AR.get([128, 1024])
    DMA(rst256, C["rst256"], (), ["rst256"])
    mark0 = AR.off
    are = AR.get([128, 32]); aim = AR.get([128, 32]); x2 = AR.get([2, 32]); dtt = AR.get([128, 32])
    ar = AR.get([128, 32]); ai = AR.get([128, 32]); mag = AR.get([128, 32]); sn = AR.get([128, 32]); csn = AR.get([128, 32])
    t1 = AR.get([128, 32]); t2 = AR.get([128, 32]); t3 = AR.get([128, 32]); cr = AR.get([128, 32]); ci = AR.get([128, 32])
    braw = AR.get([128, 32, 16]); biraw = AR.get([128, 32, 16]); bbr = AR.get([128, 32, 16]); bbi = AR.get([128, 32, 16]); tb = AR.get([128, 32, 16])
    msr = AR.get([128, 32, 2, 16]); msi = AR.get([128, 32, 2, 16])
    craw = AR.get([128, 8, 64]); ciraw = AR.get([128, 8, 64]); tc = [AR.get([128, 2, 64]) for _ in range(2)]
    DMA(are, W["s5_a_re"][l].rearrange("d (gp g2) p -> (g2 p) (d gp)", g2=2), (), ["are"], slow=True)
    DMA(aim, W["s5_a_im"][l].rearrange("d (gp g2) p -> (g2 p) (d gp)", g2=2), (), ["aim"], slow=True)
    DMA(x2, W["s5_log_dt"][l].rearrange("d (gp g2) -> g2 (d gp)", g2=2), (), ["x2"], slow=True)
    MM(ps[0][:, 0:32], sel2[:], x2, True, True, ["sel2", "x2"], [psk(0)])
    ACTF(dtt, ps[0][:, 0:32], AF.Exp, [psk(0)], ["dtt"])
    TT_("dve", ar, are, dtt, ALU.mult, ["are", "dtt"], ["ar"])
    TT_("dve", ai, aim, dtt, ALU.mult, ["aim", "dtt"], ["ai"])
    ACTF(mag, ar, AF.Exp, ["ar"], ["mag"])
    import concourse.mybir as mybir
    ki = AR.get([128, 32]).bitcast(mybir.dt.int32)
    s2 = AR.get([128, 32]); s4 = AR.get([128, 32])
    TS_("dve", t1, ai, 1.0 / (2 * math.pi), None, ALU.mult, None, ["ai"], ["t1"])
    CP("dve", ki, t1, ["t1"], ["ki"])
    CP("dve", t1, ki, ["ki"], ["t1"])
    STT(t1, t1, -2 * math.pi, ai, ALU.mult, ALU.add, ["t1", "ai"], ["t1"])
    ACTF(s2, t1, AF.Sin, ["t1"], ["s2"], scale=0.5)
    ACTF(s4, t1, AF.Sin, ["t1"], ["s4"], scale=0.25)
    TT_("dve", t2, s2, s2, ALU.mult, ["s2"], ["t2"])
    TS_("dve", csn, t2, -2.0, 1.0, ALU.mult, ALU.add, ["t2"], ["csn"])
    TT_("dve", t3, s4, s4, ALU.mult, ["s4"], ["t3"])
    TS_("dve", t3, t3, -2.0, 1.0, ALU.mult, ALU.add, ["t3"], ["t3"])
    TT_("dve", sn, s2, t3, ALU.mult, ["s2", "t3"], ["sn"])
    TS_("dve", sn, sn, 2.0, None, ALU.mult, None, ["sn"], ["sn"])
    I32 = mybir.dt.int32
    TS_("dve", t1, ai, 1.0 / (2 * math.pi), None, ALU.mult, None, ["ai"], ["t1"])
    CP("dve", ki, t1, ["t1"], ["ki"])
    CP("dve", t1, ki, ["ki"], ["t1"])
    STT(th1, t1, -2 * math.pi, ai, ALU.mult, ALU.add, ["t1", "ai"], ["th1"])
    TS_("dve", t1, th1, 64.0 / (2 * math.pi), None, ALU.mult, None, ["th1"], ["t1"])
    CP("dve", ki, t1, ["t1"], ["ki"])
    CP("dve", t1, ki, ["ki"], ["t1"])
    TS_("dve", th64, th1, 64.0, None, ALU.mult, None, ["th1"], ["th64"])
    STT(th64, t1, -2 * math.pi, th64, ALU.mult, ALU.add, ["t1", "th64"], ["th64"])
    CP("dve", rho, mag, ["mag"], ["rho"])
    lr = pw[:, 0, 0, :]; li = pw[:, 0, 1, :]
    TT_("dve", lr, mag, csn, ALU.mult, ["mag", "csn"], ["pw"])
    TT_("dve", li, mag, sn, ALU.mult, ["mag", "sn"], ["pw"])
    TS_("dve", pw[:, 0, 2, :], li, -1.0, None, ALU.mult, None, ["pw"], ["pw"])
    for k in range(1, 1):
        r_, i_ = pw[:, k - 1, 0, :], pw[:, k - 1, 1, :]
        TT_("dve", t1, r_, r_, ALU.mult, ["pw"], ["t1"])
        TT_("dve", t2, i_, i_, ALU.mult, ["pw"], ["t2"])
        TT_("dve", pw[:, k, 0, :], t1, t2, ALU.subtract, ["t1", "t2"], ["pw"])
        TT_("dve", t3, r_, i_, ALU.mult, ["pw"], ["t3"])
        TS_("dve", pw[:, k, 1, :], t3, 2.0, None, ALU.mult, None, ["t3"], ["pw"])
        TS_("dve", pw[:, k, 2, :], t3, -2.0, None, ALU.mult, None, ["t3"], ["pw"])
    TS_("dve", t1, lr, -1.0, None, ALU.add, None, ["pw"], ["t1"])
    TT_("dve", t2, are, are, ALU.mult, ["are"], ["t2"])
    TT_("dve", t3, aim, aim, ALU.mult, ["aim"], ["t3"])
    TT_("dve", t2, t2, t3, ALU.add, ["t2", "t3"], ["t2"])
    P.op("dve", lambda e: e.reciprocal(t2, t2), ["t2"], ["t2"])
    TT_("dve", cr, t1, are, ALU.mult, ["t1", "are"], ["cr"])
    TT_("dve", t3, li, aim, ALU.mult, ["pw", "aim"], ["t3"])
    TT_("dve", cr, cr, t3, ALU.add, ["cr", "t3"], ["cr"])
    TT_("dve", cr, cr, t2, ALU.mult, ["cr", "t2"], ["cr"])
    TT_("dve", ci, li, are, ALU.mult, ["pw", "are"], ["ci"])
    TT_("dve", t3, t1, aim, ALU.mult, ["t1", "aim"], ["t3"])
    TT_("dve", ci, ci, t3, ALU.subtract, ["ci", "t3"], ["ci"])
    TT_("dve", ci, ci, t2, ALU.mult, ["ci", "t2"], ["ci"])
    DMA(braw, W["s5_b_re"][l].rearrange("d (gp g2) p c -> (g2 p) (d gp) c", g2=2), (), ["braw"])
    DMA(biraw, W["s5_b_im"][l].rearrange("d (gp g2) p c -> (g2 p) (d gp) c", g2=2), (), ["biraw"])
    crb = cr.unsqueeze(2).broadcast_to([128, 32, 16]); cib = ci.unsqueeze(2).broadcast_to([128, 32, 16])
    TT_("dve", bbr, braw, crb, ALU.mult, ["braw", "cr"], ["bbr"])
    TT_("dve", tb, biraw, cib, ALU.mult, ["biraw", "ci"], ["tb"])
    TT_("dve", bbr, bbr, tb, ALU.subtract, ["bbr", "tb"], ["bbr"])
    TT_("dve", bbi, biraw, crb, ALU.mult, ["biraw", "cr"], ["bbi"])
    TT_("dve", tb, braw, cib, ALU.mult, ["braw", "ci"], ["tb"])
    TT_("dve", bbi, bbi, tb, ALU.add, ["bbi", "tb"], ["bbi"])
    for (src, ms, big, nm) in ((bbr, msr, Bre, "Bre"), (bbi, msi, Bim, "Bim")):
        MSET("pool", ms, 0.0, ["ms" + nm])
        CP("dve", ms[0:64, :, 0, :], src[0:64], ["bbr", "bbi", "ms" + nm], ["ms" + nm])
        CP("dve", ms[64:128, :, 1, :], src[64:128], ["bbr", "bbi", "ms" + nm], ["ms" + nm])
        for d in range(2):
            for c in range(4):
                u0 = d * 16 + 4 * c
                TRN(ps[1][:, 0:128], ms[:, u0:u0 + 4, :, :].rearrange("p a b c -> p (a b c)"), ident[:], ["ms" + nm, "ident"], [psk(1)])
                for j in range(4):
                    TS_("dve", big[:, u0 + j, :], ps[1][:, 0:128], mask4[:, j:j + 1], None, ALU.mult, None, [psk(1), "mask4"], [nm])
    DMA(craw, W["s5_c_re"][l].rearrange("d (c gi) ch p -> (gi ch) (d c) p", gi=8), (), ["craw"])
    DMA(ciraw, W["s5_c_im"][l].rearrange("d (c gi) ch p -> (gi ch) (d c) p", gi=8), (), ["ciraw"])
    n = 0
    for (src, big, nm, sgn) in ((craw, Cre, "Cre", 1.0), (ciraw, Cimn, "Cimn", -1.0)):
        for d in range(2):
            for c in range(4):
                for j in range(4):
                    b = n % 2
                    n += 1
                    for g2 in range(2):
                        TS_("dve", tc[b][:, g2, :], src[:, d * 4 + c, :], mask8[:, 2 * j + g2:2 * j + g2 + 1], None, ALU.mult, None,
                            ["craw", "ciraw", "mask8"], [("tc", b)])
                    TRN(ps[2 + b][:, 0:128], tc[b].rearrange("p a b -> p (a b)"), ident[:], [("tc", b), "ident"], [psk(2 + b)])
                    TS_("dve", big[:, d * 16 + 4 * c + j, :], ps[2 + b][:, 0:128], sgn, None, ALU.mult, None, [psk(2 + b)], [nm])
    wg_src = W["s5_w_glu"][l].rearrange("(c p) n -> p c n", p=128)
    _load_bf(g, wglu, wg_src, (4, 1024), "wglu")
    P.barrier()
    dsk0 = VBI["s5_d"]
    cfg_ = g["cfg"]
    for (t0, T, smp, nsub, Tsub) in ((0, cfg_.TS, True, 1, cfg_.TS), (cfg_.TS, cfg_.NP * cfg_.TP, False, cfg_.NP, cfg_.TP)):
        AR.off = mark0
        nlev = int(round(math.log2(T)))
        TL = min(512, T); ntl = T // TL
        uc = AR.get([128, T]); ub = AR.get([128, T], BF16)
        hbr = AR.get([128, T], BF16); hbi = AR.get([128, T], BF16)
        ygb = AR.get([128, 4, T], BF16); yt = AR.get([128, TL]); yt2 = AR.get([128, TL])
        fin = AR.get([128, nsub, 32, 2]); h0 = AR.get([128, 32, 2]); addr = AR.get([128, 32]); addi = AR.get([128, 32]); ta = AR.get([128, 32])
        AJ = AR.get([128, 64]); et = AR.get([128, 4])
        markB = AR.off
        B1 = AR.get([128, T]); B2 = AR.get([128, T]); B3 = AR.get([128, T]); B4 = AR.get([128, T]); B5 = AR.get([128, T]); B6 = AR.get([128, T])
        B4i = B4.bitcast(mybir.dt.int32)
        v3 = lambda ap_: ap_.rearrange("p (n t) -> p n t", t=64)
        dv = lambda ap_: ap_.rearrange("p (s t) -> p s t", s=nsub)
        tb = lambda ap_: ap_[:, 0:Tsub].unsqueeze(1).broadcast_to([128, nsub, Tsub])
        rt = AR.get([128, T]) if nsub > 1 else None
        if smp:
            DMA(h0, I["st_s5"][l].rearrange("d (gp g2) p r -> (g2 p) (d gp) r", g2=2), (), ["h0"])
            lr = pw[:, 0, 0, :]; li = pw[:, 0, 1, :]
            TT_("dve", addr, lr, h0[:, :, 0], ALU.mult, ["pw", "h0"], ["addr"])
            TT_("dve", ta, li, h0[:, :, 1], ALU.mult, ["pw", "h0"], ["ta"])
            TT_("dve", addr, addr, ta, ALU.subtract, ["addr", "ta"], ["addr"])
            TT_("dve", addi, lr, h0[:, :, 1], ALU.mult, ["pw", "h0"], ["addi"])
            TT_("dve", ta, li, h0[:, :, 0], ALU.mult, ["pw", "h0"], ["ta"])
            TT_("dve", addi, addi, ta, ALU.add, ["addi", "ta"], ["addi"])
        for c in range(4):
            DMA(uc, scr_p[c * 128:(c + 1) * 128, t0:t0 + T], ["scr_p"], ["uc"])
            CP("act", ub, uc, ["uc"], ["ub"])
            first = True
            for d in range(2):
                for j in range(4):
                    u = d * 16 + 4 * c + j
                    HV = [(0, T // 2), (T // 2, T)] if nsub == 1 else [(0, T)]
                    hof = lambda col: 0 if len(HV) == 1 else (0 if col < T // 2 else 1)
                    D_ = lambda x, hf: dv(x) if nsub > 1 else x[:, hf[0]:hf[1]]
                    TB_ = lambda x, hf: tb(x) if nsub > 1 else x[:, hf[0]:hf[1]]
                    F_ = lambda x, hf: x if nsub > 1 else x[:, hf[0]:hf[1]]
                    for tl in range(ntl):
                        sl = slice(tl * TL, (tl + 1) * TL)
                        hxs = sorted({hof(tl * TL), hof((tl + 1) * TL - 1)})
                        MM(ps[4][:, 0:TL], Bre[:, u, :], ub[:, sl], True, True, ["Bre", "ub"], [psk(4)])
                        CP("act", B1[:, sl], ps[4][:, 0:TL], [psk(4)], [("B1", hx) for hx in hxs])
                        MM(ps[5][:, 0:TL], Bim[:, u, :], ub[:, sl], True, True, ["Bim", "ub"], [psk(5)])
                        CP("act", B2[:, sl], ps[5][:, 0:TL], [psk(5)], [("B2", hx) for hx in hxs])
                    e_ = 0 if d == 0 else T - 1
                    if smp:
                        hx = hof(e_)
                        TT_("pool", B1[:, e_:e_ + 1], B1[:, e_:e_ + 1], addr[:, u:u + 1], ALU.add, [("B1", hx), "addr"], [("B1", hx)])
                        TT_("pool", B2[:, e_:e_ + 1], B2[:, e_:e_ + 1], addi[:, u:u + 1], ALU.add, [("B2", hx), "addi"], [("B2", hx)])
                    TS_("pool", AJ, JI, th1[:, u:u + 1], None, ALU.mult, None, ["JI", "th1"], ["AJ"])
                    THV = HV if nsub == 1 else [(0, Tsub)]
                    for hi, (a_, b_) in enumerate(THV):
                        ts_ = slice(a_, b_)
                        nk_ = (b_ - a_) // 64
                        k3, k4_, k5 = ("B3", hi), ("B4", hi), ("B5", hi)
                        STT(v3(B3[:, ts_]), KI[:, a_ // 64:b_ // 64].unsqueeze(2).broadcast_to([128, nk_, 64]), th64[:, u:u + 1],
                            AJ.unsqueeze(1).broadcast_to([128, nk_, 64]), ALU.mult, ALU.add, ["KI", "th64", "AJ"], [k3])
                        ACTF(B4[:, ts_], B3[:, ts_], AF.Copy, [k3], [k4_], scale=1.0 / (2 * math.pi))
                        CP("dve", B4i[:, ts_], B4[:, ts_], [k4_], [k4_])
                        ACTF(B4[:, ts_], B4i[:, ts_], AF.Copy, [k4_], [k4_])
                        STT(B3[:, ts_], B4[:, ts_], -2 * math.pi, B3[:, ts_], ALU.mult, ALU.add, [k4_, k3], [k3])
                        ACTF(B4[:, ts_], B3[:, ts_], AF.Sin, [k3], [k4_])
                        ACTF(B5[:, ts_], B3[:, ts_], AF.Sin, [k3], [k5], scale=0.5)
                        ACTF(B5[:, ts_], B5[:, ts_], AF.Square, [k5], [k5])
                        ACTF(B5[:, ts_], B5[:, ts_], AF.Identity, [k5], [k5], scale=-2.0, bias=1.0)
                    op_a = ALU.add if d == 0 else ALU.subtract
                    op_b = ALU.subtract if d == 0 else ALU.add
                    for hi, hf in enumerate(HV):
                        k1, k2, k3, k4_, k5, k6 = (("B%d" % n_, hi) for n_ in range(1, 7))
                        TT_("dve", D_(B3, hf), TB_(B5, hf), D_(B1, hf), ALU.mult, [k5, k1], [k3])
                        TT_("pool", D_(B6, hf), TB_(B4, hf), D_(B2, hf), ALU.mult, [k4_, k2], [k6])
                        TT_("dve", F_(B3, hf), F_(B3, hf), F_(B6, hf), op_a, [k3, k6], [k3])
                        TT_("dve", D_(B6, hf), TB_(B5, hf), D_(B2, hf), ALU.mult, [k5, k2, k3], [k6])
                        TT_("pool", D_(B1, hf), TB_(B4, hf), D_(B1, hf), ALU.mult, [k4_, k1], [k1])
                        TT_("dve", F_(B6, hf), F_(B6, hf), F_(B1, hf), op_b, [k6, k1], [k6])
                    if nsub > 1:
                        TS_("dve", rt, rst256[:, 0:T], rho[:, u:u + 1], None, ALU.mult, None, ["rst256", "rho"], ["rho_t"])
                    order = list(range(len(HV))) if d == 0 else list(range(len(HV) - 1, -1, -1))
                    for oi, hi in enumerate(order):
                        a_, b_ = HV[hi]
                        rb = rt if nsub > 1 else rho[:, u:u + 1].broadcast_to([128, b_ - a_])
                        for (Bo, Bc, no_, nc_) in ((B1, B3, 1, 3), (B2, B6, 2, 6)):
                            ko = ("B%d" % no_, hi); kc_ = ("B%d" % nc_, hi)
                            if oi == 0:
                                ini = 0.0; extra = []
                            else:
                                ph = order[oi - 1]
                                pcol = HV[ph][1] - 1 if d == 0 else HV[ph][0]
                                ini = Bo[:, pcol:pcol + 1]; extra = [("B%d" % no_, ph)]
                            if d == 0:
                                o_ap, c_ap = Bo[:, a_:b_], Bc[:, a_:b_]
                            else:
                                o_ap, c_ap = Bo[:, a_:b_][:, ::-1], Bc[:, a_:b_][:, ::-1]
                            P.op("dve", lambda e, o_=o_ap, r_=rb, c_=c_ap, i_=ini: e.tensor_tensor_scan(o_, r_, c_, i_, ALU.mult, ALU.add),
                                 [kc_, "rho", "rho_t", ko] + extra, [ko])
                    for hi, hf in enumerate(HV):
                        k1, k2, k3, k4_, k5, k6 = (("B%d" % n_, hi) for n_ in range(1, 7))
                        kr_, ki2 = ("hbr", hi), ("hbi", hi)
                        TT_("dve", D_(B3, hf), TB_(B5, hf), D_(B1, hf), ALU.mult, [k5, k1], [k3])
                        TT_("pool", D_(B6, hf), TB_(B4, hf), D_(B2, hf), ALU.mult, [k4_, k2], [k6])
                        TT_("dve", F_(hbr, hf), F_(B3, hf), F_(B6, hf), op_b, [k3, k6], [kr_])
                        if not smp:
                            f_ = Tsub - 1 if d == 0 else 0
                            TT_("dve", fin[:, :, u, 0], dv(B3)[:, :, f_], dv(B6)[:, :, f_], op_b, [k3, k6], ["fin"])
                        TT_("pool", D_(B6, hf), TB_(B4, hf), D_(B1, hf), ALU.mult, [k4_, k1, kr_, "fin"], [k6])
                        TT_("dve", D_(B3, hf), TB_(B5, hf), D_(B2, hf), ALU.mult, [k5, k2, kr_, "fin"], [k3])
                        TT_("dve", F_(hbi, hf), F_(B3, hf), F_(B6, hf), op_a, [k3, k6], [ki2])
                        if not smp:
                            TT_("dve", fin[:, :, u, 1], dv(B3)[:, :, f_], dv(B6)[:, :, f_], op_a, [k3, k6], ["fin"])
                    last = (d == 1 and j == 3)
                    for tl in range(ntl):
                        sl = slice(tl * TL, (tl + 1) * TL)
                        hxs = sorted({hof(tl * TL), hof((tl + 1) * TL - 1)})
                        MM(ps[tl][:, 0:TL], Cre[:, u, :], hbr[:, sl], first, False, ["Cre"] + [("hbr", hx) for hx in hxs], [psk(tl)])
                        MM(ps[tl][:, 0:TL], Cimn[:, u, :], hbi[:, sl], False, last, ["Cimn"] + [("hbi", hx) for hx in hxs], [psk(tl)])
                    first = False
            for tl in range(ntl):
                sl = slice(tl * TL, (tl + 1) * TL)
                STT(yt, uc[:, sl], VB[:, dsk0 + c:dsk0 + c + 1], ps[tl][:, 0:TL], ALU.mult, ALU.add, ["uc", "VB", psk(tl)], ["yt"])
                TT_("pool", yt2, yt, yt, ALU.mult, ["yt"], ["yt2"])
                TS_("dve", yt2, yt2, 0.044715, 1.0, ALU.mult, ALU.add, ["yt2"], ["yt2"])
                TT_("dve", yt2, yt2, yt, ALU.mult, ["yt2", "yt"], ["yt2"])
                ACTF(yt2, yt2, AF.Sigmoid, ["yt2"], ["yt2"], scale=1.5957691216)
                TT_("dve", ygb[:, c, sl], yt, yt2, ALU.mult, ["yt", "yt2"], ["ygb"])
        P.barrier()
        AR.off = markB
        ssacc = AR.get([128, T]); sgt = AR.get([128, TL]); yo = [AR.get([128, TL]) for _ in range(2)]
        ssb = AR.get([128, TL], BF16); rstd = AR.get([128, TL])
        n = 0
        for m in range(4):
            for tl in range(ntl):
                sl = slice(tl * TL, (tl + 1) * TL)
                b = n % 2
                n += 1
                for kc in range(4):
                    MM(ps[4 + b][:, 0:TL], wglu[:, kc, m * 128:(m + 1) * 128], ygb[:, kc, sl], kc == 0, kc == 3, ["wglu", "ygb"], [psk(4 + b)])
                for kc in range(4):
                    MM(ps[6 + b][:, 0:TL], wglu[:, kc, 512 + m * 128:512 + (m + 1) * 128], ygb[:, kc, sl], kc == 0, kc == 3, ["wglu", "ygb"], [psk(6 + b)])
                ACTF(sgt, ps[6 + b][:, 0:TL], AF.Sigmoid, [psk(6 + b)], ["sgt"])
                TT_("dve", yo[b], ps[4 + b][:, 0:TL], sgt, ALU.mult, [psk(4 + b), "sgt"], [("yo", b)])
                DMA(scr_m[m * 128:(m + 1) * 128, t0 + tl * TL:t0 + (tl + 1) * TL], yo[b], [("yo", b)], ["scr_m"])
                if m == 0:
                    ACTF(ssacc[:, sl], yo[b], AF.Square, [("yo", b)], ["ssacc"])
                else:
                    ACTF(sgt, yo[b], AF.Square, [("yo", b)], ["sgt"])
                    TT_("pool", ssacc[:, sl], ssacc[:, sl], sgt, ALU.add, ["ssacc", "sgt"], ["ssacc"])
        for tl in range(ntl):
            sl = slice(tl * TL, (tl + 1) * TL)
            CP("act", ssb, ssacc[:, sl], ["ssacc"], ["ssb"])
            MM(ps[4][:, 0:TL], ones_bf[:], ssb, True, True, ["ssb", "ones_bf"], [psk(4)])
            RSTD(rstd, ps[4][:, 0:TL], [psk(4)], ["rstd"], 1.0 / 512, 1e-6)
            DMA(scr_r[0, :, t0 + tl * TL:t0 + (tl + 1) * TL], rstd, ["rstd"], ["scr_r"])
        if not smp:
            for s_ in range(nsub):
                DMA(O["o_s5"][s_, l].rearrange("d (gp g2) p r -> (g2 p) (d gp) r", g2=2), fin[:, s_], ["fin"], ["o_s5"])
        P.barrier()


import math
import numpy as np


def krwkv_rwkv(g, l):
    P, AR, ps, psk, W, I, O, C, VB, VBI = g["P"], g["AR"], g["ps"], g["psk"], g["W"], g["I"], g["O"], g["C"], g["VB"], g["VBI"]
    MM, TRN, ACTF, TT_, TS_, STT, CP, DMA, MSET, RSTD = (g[k] for k in ("MM", "TRN", "ACTF", "TT_", "TS_", "STT", "CP", "DMA", "MSET", "RSTD"))
    ident, identb, blkf, blk_bf, msk, reset, omka = g["ident"], g["identb"], g["blkf"], g["blk_bf"], g["msk"], g["reset"], g["omka"]
    scr_p, scr_m = g["scr_p"], g["scr_m"]
    psb2 = ps[2][:].bitcast(BF16); psb3 = ps[3][:].bitcast(BF16); psb6 = ps[6][:].bitcast(BF16)
    AR.reset()
    w2b = AR.get([64, 2, 512], BF16); a2b = AR.get([64, 2, 512], BF16); g2b = AR.get([128, 512], BF16)
    stw = AR.get([64, 2, 512])
    DMA(stw, W["rwkv_w2"][l].rearrange("d m n -> m d n"), (), ["stw"])
    CP("dve", w2b, stw, ["stw"], ["w2b"])
    DMA(stw, W["rwkv_a2"][l].rearrange("d m n -> m d n"), ["stw"], ["stw"])
    CP("dve", a2b, stw, ["stw"], ["a2b"])
    stg = AR.get([128, 512])
    DMA(stg, W["rwkv_g2"][l], (), ["stg"])
    CP("dve", g2b, stg, ["stg"], ["g2b"])
    m_ib = [AR.get([128, 128], BF16) for _ in range(2)]
    CP("dve", m_ib[0], msk["m_iu"][:], ["m_iu"], ["m_b"])
    CP("dve", m_ib[1], msk["m_il"][:], ["m_il"], ["m_b"])
    mark0 = AR.off
    kk0, ka0, rk0, lw0, lb0 = VBI["k_k"], VBI["k_a"], VBI["r_k"], VBI["ln_w"], VBI["ln_b"]
    v3_ = lambda ap_: ap_.rearrange("p (n t) -> p n t", t=64)
    pq = lambda bank, q: ps[bank][:, q * 128:(q + 1) * 128]

    def capture(fn):
        saved = P.ops
        P.ops = []
        fn()
        out = P.ops
        P.ops = saved
        return out

    cfg_ = g["cfg"]
    assert cfg_.TP == 256
    for (t0, T, smp) in ((0, cfg_.TS, True), (cfg_.TS, cfg_.NP * cfg_.TP, False)):
        AR.off = mark0
        SEG = min(256, T); nseg = T // SEG; NCH = SEG // 64; NCHT = T // 64
        TL = SEG; ntl = nseg
        twb = AR.get([64, T], BF16); alb = AR.get([64, T], BF16); sgb = AR.get([128, T], BF16)
        rp = AR.get([128, T]); kp = AR.get([128, T]); vp = AR.get([128, T]); kk = AR.get([128, T])
        vbd = AR.get([128, NCHT, 2, 64], BF16); Vtok = AR.get([128, NCHT, 128], BF16)
        y0 = AR.get([128, T]); kdsum = AR.get([128, T]); lst = y0
        sgm = AR.get([128, SEG]); a_ = AR.get([128, SEG]); kd = AR.get([128, SEG]); b_ = AR.get([128, SEG])
        Lc = AR.get([128, SEG]); Lx = AR.get([128, SEG]); E1 = AR.get([128, SEG]); E2 = AR.get([128, SEG]); E3 = AR.get([128, SEG])
        tmp = E3; sqb = AR.get([128, SEG], BF16)
        AR4p = [AR.get([128, NCH, 6, 128], BF16) for _ in range(2)]
        btokp = [AR.get([128, NCH, 128], BF16) for _ in range(2)]; ktokp = [AR.get([128, NCH, 128], BF16) for _ in range(2)]
        atokp = [AR.get([128, NCH, 128], BF16) for _ in range(2)]; dcolp = [AR.get([128, NCH]) for _ in range(2)]
        LUt = [AR.get([128, 128], BF16) for _ in range(4)]; Wvb = [AR.get([128, 128], BF16) for _ in range(4)]
        Z32 = AR.get([128, 128]); Zb = AR.get([128, 128], BF16); ztmp = AR.get([128, 128]); zsrc = AR.get([128, 2, 64])
        AabT = [AR.get([128, 128], BF16) for _ in range(4)]; ArbT = [AR.get([128, 128], BF16) for _ in range(4)]
        AakT = [AR.get([128, 128], BF16) for _ in range(4)]; ArkT = [AR.get([128, 128], BF16) for _ in range(4)]
        Xa = [[AR.get([128, 128], BF16) for _ in range(2)] for _ in range(4)]; XTa = [[AR.get([128, 128], BF16) for _ in range(2)] for _ in range(4)]
        PT = [[AR.get([128, 128], BF16) for _ in range(2)] for _ in range(4)]
        Ub = AR.get([128, 128], BF16)
        yc = sgm; yn = a_; rstd = kd; bi = b_; yo = [Lc, Lx]
        DMA(lst[0:64, :], scr_p[2048:2112, t0:t0 + T], ["scr_p"], ["y0"])
        ACTF(twb, lst[0:64, :], AF.Tanh, ["y0"], ["twb"])
        DMA(lst[0:64, :], scr_p[2112:2176, t0:t0 + T], ["scr_p", "y0"], ["y0"])
        CP("dve", alb, lst[0:64, :], ["y0"], ["alb"])
        DMA(lst, scr_p[2176:2304, t0:t0 + T], ["scr_p", "y0"], ["y0"])
        ACTF(sgb, lst, AF.Sigmoid, ["y0"], ["sgb"])
        for par in range(2):
            MSET("pool", AR4p[par], 0.0, [("AR4", par)])
        MSET("pool", vbd, 0.0, ["vbd"])
        MSET("pool", zsrc, 0.0, ["zsrc"])
        for hp in range(4):
            DMA(rp, scr_p[512 + hp * 128:512 + (hp + 1) * 128, t0:t0 + T], ["scr_p"], ["rp"])
            DMA(kp, scr_p[1024 + hp * 128:1024 + (hp + 1) * 128, t0:t0 + T], ["scr_p"], ["kp"])
            DMA(vp, scr_p[1536 + hp * 128:1536 + (hp + 1) * 128, t0:t0 + T], ["scr_p"], ["vp"])
            TS_("dve", kk, kp, VB[:, kk0 + hp:kk0 + hp + 1], None, ALU.mult, None, ["kp", "VB"], ["kk"])
            for tl in range(ntl):
                sl = slice(tl * TL, (tl + 1) * TL)
                ACTF(sqb, kk[:, sl], AF.Square, ["kk"], ["sqb"])
                MM(ps[4][:, 0:TL], blk_bf[:], sqb, True, True, ["sqb", "blk_bf"], [psk(4)])
                RSTD(rstd, ps[4][:, 0:TL], [psk(4)], ["rstd"], 1.0, 1e-12)
                TT_("dve", kk[:, sl], kk[:, sl], rstd, ALU.mult, ["kk", "rstd"], ["kk"])
            v3 = vp.rearrange("p (n t) -> p n t", t=64)
            CP("dve", vbd[0:64, :, 0, :], v3[0:64], ["vp", "vbd"], ["vbd"])
            CP("pool", vbd[64:128, :, 1, :], v3[64:128], ["vp", "vbd"], ["vbd"])
            for q0 in range(0, NCHT, 4):
                for qq in range(4):
                    TRN(psb6[:, qq * 128:(qq + 1) * 128], vbd[:, q0 + qq].rearrange("p a b -> p (a b)"), identb[:], ["vbd", "identb"], [psk(6)])
                CP("act", Vtok[:, q0:q0 + 4, :], psb6[:, 0:512].rearrange("p (a b) -> p a b", a=4), [psk(6)], ["Vtok"])
            for d in range(2):
                w0c = VBI["w0_%d" % d] + hp; a0c = VBI["a0_%d" % d] + hp
                m_s = (msk["m_su"] if d == 0 else msk["m_sl"])[:]
                m_ts = (msk["m_sl"] if d == 0 else msk["m_su"])[:]
                mib = m_ib[d]
                if smp:
                    DMA(zsrc[0:64, 0, :], I["st_rw"][l, d, 2 * hp], ["zsrc"], ["zsrc"])
                    DMA(zsrc[64:128, 1, :], I["st_rw"][l, d, 2 * hp + 1], ["zsrc"], ["zsrc"])
                    TRN(ps[1][:, 0:128], zsrc.rearrange("p a b -> p (a b)"), ident[:], ["zsrc", "ident"], [psk(1)])
                    CP("dve", Z32, ps[1][:, 0:128], [psk(1)], ["Z32"])
                    CP("dve", Zb, Z32, ["Z32"], ["Zb"])


                def prep(seg, par):
                    AR4 = AR4p[par]; btok = btokp[par]; ktok = ktokp[par]; atok = atokp[par]; dcol = dcolp[par]
                    k4, kb, kk_, ka_, kd_ = ("AR4", par), ("btok", par), ("ktok", par), ("atok", par), ("dcol", par)
                    ss = slice(seg * SEG, (seg + 1) * SEG)
                    MM(ps[0][:, 0:SEG], w2b[:, d, hp * 128:(hp + 1) * 128], twb[:, ss], True, True, ["w2b", "twb"], [psk(0)])
                    ACTF(sgm, ps[0][:, 0:SEG], AF.Sigmoid, [psk(0), "VB"], ["sgm"], bias=VB[:, w0c:w0c + 1])
                    MM(ps[1][:, 0:SEG], a2b[:, d, hp * 128:(hp + 1) * 128], alb[:, ss], True, True, ["a2b", "alb"], [psk(1)])
                    ACTF(a_, ps[1][:, 0:SEG], AF.Sigmoid, [psk(1), "VB"], ["a_"], bias=VB[:, a0c:a0c + 1])
                    TS_("dve", sgm, sgm, -0.6065306597, None, ALU.mult, None, ["sgm"], ["sgm"])
                    TS_("dve", kd, a_, VB[:, ka0 + hp:ka0 + hp + 1], omka[:, hp:hp + 1], ALU.mult, ALU.add, ["a_", "VB", "omka"], ["kd"])
                    TT_("dve", kd, kd, kp[:, ss], ALU.mult, ["kd", "kp"], ["kd"])
                    if d == 0:
                        CP("pool", kdsum[:, ss], kd, ["kd"], ["kdsum"])
                    else:
                        TT_("pool", kdsum[:, ss], kdsum[:, ss], kd, ALU.add, ["kd", "kdsum"], ["kdsum"])
                    TT_("pool", b_, kk[:, ss], a_, ALU.mult, ["kk", "a_"], ["b_"])
                    P.op("dve", lambda e, o_=Lc, d0=reset[:, 0:SEG], d1=sgm: e.tensor_tensor_scan(o_, d0, d1, 0.0, ALU.mult, ALU.add),
                         ["reset", "sgm"], ["Lc"])
                    if d == 1:
                        TT_("dve", tmp, sgm, Lc, ALU.subtract, ["sgm", "Lc"], ["E3"])
                        tot = v3_(Lc)[:, :, 63:64].broadcast_to([128, NCH, 64])
                        TT_("dve", v3_(Lx), v3_(tmp), tot, ALU.add, ["E3", "Lc"], ["Lx"])
                        CP("dve", Lc, Lx, ["Lx"], ["Lc"])
                    TT_("dve", Lx, Lc, sgm, ALU.subtract, ["Lc", "sgm"], ["Lx"])
                    ACTF(E1, Lc, AF.Exp, ["Lc"], ["E1"])
                    ACTF(E2, Lx, AF.Exp, ["Lx"], ["E2"])
                    ACTF(E3, Lc, AF.Exp, ["Lc"], ["E3"], scale=-1.0)
                    for half, eng in ((0, "dve"), (1, "pool")):
                        hs = slice(half * 64, (half + 1) * 64)
                        cs_ = slice(half * 64, (half + 1) * 64)
                        TT_(eng, AR4[hs, :, 0, cs_], v3_(kk[:, ss])[hs], v3_(E2)[hs], ALU.mult, ["kk", "E2", k4], [k4])
                        TT_(eng, AR4[hs, :, 1, cs_], v3_(rp[:, ss])[hs], v3_(E1)[hs], ALU.mult, ["rp", "E1", k4], [k4])
                        TT_(eng, AR4[hs, :, 2, cs_], v3_(b_)[hs], v3_(E3)[hs], ALU.mult, ["b_", "E3", k4], [k4])
                        TT_(eng, AR4[hs, :, 3, cs_], v3_(kd)[hs], v3_(E3)[hs], ALU.mult, ["kd", "E3", k4], [k4])
                    TS_("pool", AR4[:, :, 0, :], AR4[:, :, 0, :], -1.0, None, ALU.mult, None, [k4], [k4])
                    e_ = 63 if d == 0 else 0
                    CP("dve", dcol, v3_(E1)[:, :, e_], ["E1"], [kd_])
                    dcb = dcol.unsqueeze(2).broadcast_to([128, NCH, 128])
                    TT_("dve", AR4[:, :, 4, :], AR4[:, :, 2, :], dcb, ALU.mult, [k4, kd_], [k4])
                    TT_("pool", AR4[:, :, 5, :], AR4[:, :, 3, :], dcb, ALU.mult, [k4, kd_], [k4])
                    for qq in range(NCH):
                        TRN(psb2[:, qq * 128:(qq + 1) * 128], AR4[:, qq, 4, :], identb[:], [k4, "identb"], [psk(2)])
                        TRN(psb3[:, qq * 128:(qq + 1) * 128], AR4[:, qq, 5, :], identb[:], [k4, "identb"], [psk(3)])
                    CP("act", btok, psb2[:, 0:NCH * 128].rearrange("p (a b) -> p a b", a=NCH), [psk(2)], [kb])
                    CP("dve", ktok, psb3[:, 0:NCH * 128].rearrange("p (a b) -> p a b", a=NCH), [psk(3)], [kk_])
                    for qq in range(NCH):
                        TRN(psb2[:, qq * 128:(qq + 1) * 128], AR4[:, qq, 0, :], identb[:], [k4, "identb"], [psk(2)])
                    CP("act", atok, psb2[:, 0:NCH * 128].rearrange("p (a b) -> p a b", a=NCH), [psk(2)], [ka_])

                segs = list(range(nseg) if d == 0 else range(nseg - 1, -1, -1))
                prep(segs[0], 0)
                for si, seg in enumerate(segs):
                    par = si % 2
                    AR4 = AR4p[par]; btok = btokp[par]; ktok = ktokp[par]; atok = atokp[par]; dcol = dcolp[par]
                    k4, kb, kk_, ka_, kd_ = ("AR4", par), ("btok", par), ("ktok", par), ("atok", par), ("dcol", par)
                    nxt = capture(lambda: prep(segs[si + 1], 1 - par)) if si + 1 < len(segs) else []
                    batch = list(range(NCH) if d == 0 else range(NCH - 1, -1, -1))
                    nbch = len(batch)
                    if not smp:
                        MSET("dve", Z32, 0.0, ["Z32"])
                        MSET("pool", Zb, 0.0, ["Zb"])

                    def slots(ch):
                        return AR4[:, ch, 0, :], AR4[:, ch, 2, :], AR4[:, ch, 3, :], AR4[:, ch, 0:2, :].rearrange("p a b -> p (a b)")
                    for bi_, ch in enumerate(batch):
                        abd, bbd, kbd, ar2 = slots(ch)
                        bk = psk(bi_)
                        MM(ps[bi_][:, 0:256], bbd, ar2, True, True, [k4], [bk])
                        MM(ps[bi_][:, 256:512], kbd, ar2, True, True, [k4], [bk])
                    for bi_, ch in enumerate(batch):
                        bk = psk(bi_)
                        TT_("dve", AabT[bi_], pq(bi_, 0), m_s, ALU.mult, [], [("AabT", bi_), bk])
                        TT_("dve", AakT[bi_], pq(bi_, 2), m_s, ALU.mult, [], [("AakT", bi_), bk])
                        CP("act", ArbT[bi_], pq(bi_, 1), [], [("ArbT", bi_), bk])
                        CP("act", ArkT[bi_], pq(bi_, 3), [], [("ArkT", bi_), bk])
                    for bi_, ch in enumerate(batch):
                        TT_("pool", ArbT[bi_], ArbT[bi_], mib, ALU.mult, [("ArbT", bi_), "m_b"], [("ArbT", bi_)])
                        TT_("pool", ArkT[bi_], ArkT[bi_], mib, ALU.mult, [("ArkT", bi_), "m_b"], [("ArkT", bi_)])
                    for bi_, ch in enumerate(batch):
                        abd, bbd, kbd, ar2 = slots(ch)
                        bk = psk(bi_)
                        MM(pq(bi_, 0), abd, bbd, True, True, [k4], [bk])
                        MM(pq(bi_, 1), AakT[bi_], Vtok[:, seg * NCH + ch, :], True, True, [("AakT", bi_), "Vtok"], [bk])
                    for bi_, ch in enumerate(batch):
                        bk = psk(bi_)
                        TT_("dve", Xa[bi_][0], pq(bi_, 0), m_ts, ALU.mult, [], [("X", bi_, 0), bk])
                        CP("act", Wvb[bi_], pq(bi_, 1), [], [("Wvb", bi_), bk])
                        TT_("pool", PT[bi_][0], AabT[bi_], identb[:], ALU.add, [("AabT", bi_), "identb"], [("PT", bi_, 0)])
                    cur = {}
                    for bi_ in range(nbch):
                        cur[bi_] = (Xa[bi_][0], AabT[bi_], ("X", bi_, 0), ("AabT", bi_), 0)
                    for k in range(1, 6):
                        nb = k % 2
                        for bi_ in range(nbch):
                            bk = psk(bi_)
                            Xc, XTc, Xk, XTk, pc = cur[bi_]
                            MM(pq(bi_, 0), XTc, Xc, True, True, [Xk, XTk], [bk])
                            if k < 5:
                                MM(pq(bi_, 1), Xc, XTc, True, True, [Xk, XTk], [bk])
                        for bi_ in range(nbch):
                            bk = psk(bi_)
                            CP("act", Xa[bi_][nb], pq(bi_, 0), [], [("X", bi_, nb), bk])
                            if k < 5:
                                CP("act", XTa[bi_][nb], pq(bi_, 1), [], [("XT", bi_, nb), bk])
                        for bi_ in range(nbch):
                            bk = psk(bi_)
                            pc = cur[bi_][4]
                            MM(pq(bi_, 2), Xa[bi_][nb], PT[bi_][pc], True, True, [("X", bi_, nb), ("PT", bi_, pc)], [bk])
                        for bi_ in range(nbch):
                            bk = psk(bi_)
                            pc = cur[bi_][4]
                            TT_("dve", PT[bi_][1 - pc], pq(bi_, 2), PT[bi_][pc], ALU.add, [("PT", bi_, pc)], [("PT", bi_, 1 - pc), bk])
                            cur[bi_] = (Xa[bi_][nb], XTa[bi_][nb], ("X", bi_, nb), ("XT", bi_, nb), 1 - pc)
                    for bi_, ch in enumerate(batch):
                        bk = psk(bi_)
                        pc = cur[bi_][4]
                        MM(pq(bi_, 0), atok[:, ch, :], PT[bi_][pc], True, True, [ka_, ("PT", bi_, pc)], [bk])
                        CP("act", LUt[bi_], pq(bi_, 0), [], [("LUt", bi_), bk])
                    qn_ = (len(nxt) + nbch - 1) // nbch if nxt else 0
                    for bi_, ch in enumerate(batch):
                        chg = seg * NCH + ch
                        rbd = AR4[:, ch, 1, :]
                        pc = cur[bi_][4]
                        Tinv = PT[bi_][pc]; tk = ("PT", bi_, pc)
                        MM(ps[6][:, 0:128], LUt[bi_], Zb, True, False, [("LUt", bi_), "Zb"], [psk(6)])
                        MM(ps[6][:, 0:128], Tinv, Wvb[bi_], False, True, [tk, ("Wvb", bi_)], [psk(6)])
                        CP("act", Ub, ps[6][:, 0:128], [psk(6)], ["Ub"])
                        MM(ps[4][:, 0:128], Zb, rbd, True, False, ["Zb", k4], [psk(4)])
                        MM(ps[4][:, 0:128], Ub, ArbT[bi_], False, False, ["Ub", ("ArbT", bi_)], [psk(4)])
                        MM(ps[4][:, 0:128], Vtok[:, chg, :], ArkT[bi_], False, True, ["Vtok", ("ArkT", bi_)], [psk(4)])
                        MM(ps[5][:, 0:128], btok[:, ch, :], Ub, True, False, [kb, "Ub"], [psk(5)])
                        MM(ps[5][:, 0:128], ktok[:, ch, :], Vtok[:, chg, :], False, True, [kk_, "Vtok"], [psk(5)])
                        STT(Zb, Z32, dcol[:, ch:ch + 1], ps[5][:, 0:128], ALU.mult, ALU.add, ["Z32", kd_, psk(5)], ["Zb"])
                        STT(Z32, Z32, dcol[:, ch:ch + 1], ps[5][:, 0:128], ALU.mult, ALU.add, ["Z32", kd_, psk(5)], ["Z32"])
                        tsl = slice(seg * SEG + ch * 64, seg * SEG + (ch + 1) * 64)
                        if d == 0:
                            CP("act", y0[0:64, tsl], ps[4][0:64, 0:64], [psk(4)], ["y0"])
                            CP("act", y0[64:128, tsl], ps[4][64:128, 64:128], [psk(4)], ["y0"])
                        else:
                            TT_("dve", y0[0:64, tsl], y0[0:64, tsl], ps[4][0:64, 0:64], ALU.add, [psk(4), "y0"], ["y0"])
                            TT_("dve", y0[64:128, tsl], y0[64:128, tsl], ps[4][64:128, 64:128], ALU.add, [psk(4), "y0"], ["y0"])
                        if nxt:
                            P.ops.extend(nxt[bi_ * qn_:(bi_ + 1) * qn_])
                    if nxt and nbch * qn_ < len(nxt):
                        P.ops.extend(nxt[nbch * qn_:])
                    if not smp:
                        TRN(ps[5][:, 0:128], Z32, ident[:], ["Z32", "ident"], [psk(5)])
                        CP("dve", ztmp, ps[5][:, 0:128], [psk(5)], ["ztmp"])
                        DMA(O["o_rw"][seg, l, d, 2 * hp], ztmp[0:64, 0:64], ["ztmp"], ["o_rw"])
                        DMA(O["o_rw"][seg, l, d, 2 * hp + 1], ztmp[64:128, 64:128], ["ztmp"], ["o_rw"])
            P.barrier()
            for tl in range(ntl):
                sl = slice(tl * TL, (tl + 1) * TL)
                ob = tl % 2
                MM(ps[0][:, 0:TL], blkf[:], y0[:, sl], True, True, ["blk", "y0"], [psk(0)])
                STT(yc, ps[0][:, 0:TL], -1.0 / 64, y0[:, sl], ALU.mult, ALU.add, [psk(0), "y0"], ["yc"])
                ACTF(yn, yc, AF.Square, ["yc"], ["yn"])
                MM(ps[1][:, 0:TL], blkf[:], yn, True, True, ["blk", "yn"], [psk(1)])
                RSTD(rstd, ps[1][:, 0:TL], [psk(1)], ["rstd"], 1.0 / 64, 64e-5)
                TT_("dve", yn, yc, rstd, ALU.mult, ["yc", "rstd"], ["yn"])
                TS_("dve", yn, yn, VB[:, lw0 + hp:lw0 + hp + 1], VB[:, lb0 + hp:lb0 + hp + 1], ALU.mult, ALU.add, ["yn", "VB"], ["yn"])
                STT(bi, rp[:, sl], VB[:, rk0 + hp:rk0 + hp + 1], kdsum[:, sl], ALU.mult, ALU.mult, ["rp", "VB", "kdsum"], ["bi"])
                MM(ps[2][:, 0:TL], blkf[:], bi, True, True, ["blk", "bi"], [psk(2)])
                TT_("dve", bi, ps[2][:, 0:TL], vp[:, sl], ALU.mult, [psk(2), "vp"], ["bi"])
                TT_("pool", yn, yn, bi, ALU.add, ["yn", "bi"], ["yn"])
                MM(ps[3][:, 0:TL], g2b[:, hp * 128:(hp + 1) * 128], sgb[:, sl], True, True, ["g2b", "sgb"], [psk(3)])
                TT_("dve", yo[ob], ps[3][:, 0:TL], yn, ALU.mult, [psk(3), "yn"], [("yo", ob)])
                DMA(scr_m[512 + hp * 128:512 + (hp + 1) * 128, t0 + tl * TL:t0 + (tl + 1) * TL], yo[ob], [("yo", ob)], ["scr_m"])
            P.barrier()
        P.barrier()


import numpy as np

D = 2048


def ktail_stageC(ctx, l):
    g = ctx
    P, AR, ps, psk, W, MOD, VB, VBI = g["P"], g["AR"], g["ps"], g["psk"], g["W"], g["MOD"], g["VB"], g["VBI"]
    MM, ACTF, TT_, TS_, STT, CP, DMA, MSET = g["MM"], g["ACTF"], g["TT_"], g["TS_"], g["STT"], g["CP"], g["DMA"], g["MSET"]
    cfg = g["cfg"]; TS = cfg.TS; NT = g["NT"]
    scr_x, scr_m, scr_r, scr_xv = g["scr_x"], g["scr_m"], g["scr_r"], g["scr_xv"]
    wo = W["w_out"][l].rearrange("(c p) n -> p c n", p=128)
    w1 = W["mlp_w1"][l].rearrange("(c p) n -> p c n", p=128)
    w2 = W["mlp_w2"][l]
    for g0 in range(0, NT, 2):
        tiles = list(range(g0, min(g0 + 2, NT)))
        ntl = len(tiles)
        tok0 = tiles[0] * 512; ntok = ntl * 512
        AR.reset()
        mT = AR.get([128, 16, 1024], BF16)
        mst = [AR.get([128, 1024]) for _ in range(2)]
        rs = [AR.get([128, 1024]) for _ in range(2)]
        xst = [AR.get([128, 512]) for _ in range(4)]
        DMA(rs[0][:, 0:ntok], scr_r[0, :, tok0:tok0 + ntok], ["scr_r"], [("rs", 0)])
        DMA(rs[1][:, 0:ntok], scr_r[1, :, tok0:tok0 + ntok], ["scr_r"], [("rs", 1)])
        for c in range(16):
            b = c % 2
            DMA(mst[b][:, 0:ntok], scr_m[c * 128:(c + 1) * 128, tok0:tok0 + ntok], ["scr_m"], [("mst", b)])
            if c < 4:
                col = VBI["s5_on"] + c
                STT(mT[:, c, 0:ntok], mst[b][:, 0:ntok], VB[:, col:col + 1], rs[0][:, 0:ntok], ALU.mult, ALU.mult,
                    [("mst", b), "VB", ("rs", 0)], [("mT", c)])
            elif c < 8:
                CP("act", mT[:, c, 0:ntok], mst[b][:, 0:ntok], [("mst", b)], [("mT", c)])
            else:
                col = VBI["mla_on"] + c - 8
                STT(mT[:, c, 0:ntok], mst[b][:, 0:ntok], VB[:, col:col + 1], rs[1][:, 0:ntok], ALU.mult, ALU.mult,
                    [("mst", b), "VB", ("rs", 1)], [("mT", c)])
        cnt = 0
        mkeys = [("mT", c) for c in range(16)]
        for blk in range(D // 256):
            wb, wk = g["load_w"](wo[:, :, blk * 256:(blk + 1) * 256], (16, 256))
            for m in range(2):
                dc = blk * 2 + m
                for ti, tl in enumerate(tiles):
                    j = 0 if tl * 512 < TS else 1
                    pb = cnt % 2
                    xb = cnt % 4
                    DMA(xst[xb], scr_x[dc * 128:(dc + 1) * 128, tl * 512:(tl + 1) * 512], [("scr_x", dc, tl)], [("xst", xb)], q="pool")
                    for kc in range(16):
                        MM(ps[pb][:], wb[:, kc, m * 128:(m + 1) * 128], mT[:, kc, ti * 512:(ti + 1) * 512], kc == 0, kc == 15,
                           [wk] + mkeys, [psk(pb)])
                    STT(xst[xb], ps[pb][:], MOD[:, 32 + dc, j:j + 1], xst[xb], ALU.mult, ALU.add, [psk(pb), "MOD", ("xst", xb)], [("xst", xb)])
                    DMA(scr_x[dc * 128:(dc + 1) * 128, tl * 512:(tl + 1) * 512], xst[xb], [("xst", xb)], [("scr_x", dc, tl)], q="pool")
                    cnt += 1
        P.barrier()
        AR.reset()
        hT = AR.get([128, 16, 1024], BF16)
        mark = AR.off
        xg = AR.get([128, 16, 512]); sqb = AR.get([128, 16, 512], BF16); rstd = AR.get([128, 512])
        tmp = [AR.get([128, 512]) for _ in range(2)]
        for ti, tl in enumerate(tiles):
            j = 0 if tl * 512 < TS else 1
            DMA(xg, scr_xv[:, :, tl * 512:(tl + 1) * 512], ["scr_x"], ["xg"])
            g["norm_mod"](xg, "xg", hT[:, :, ti * 512:(ti + 1) * 512], ("hT", ti), 1, j, sqb, rstd, tmp)
        P.barrier()
        AR.off = mark
        yacc = AR.get([128, 16, 1024])
        act = [AR.get([128, 2, 1024], BF16) for _ in range(2)]
        rl = [AR.get([128, 512], BF16) for _ in range(2)]
        xst = [AR.get([128, 512]) for _ in range(4)]
        P.op("pool", lambda e: e.memset(yacc, 0.0), (), [("yacc", dc_, ti_) for dc_ in range(16) for ti_ in range(2)])
        hk = [("hT", ti) for ti in range(ntl)]
        cnt = 0
        cw = [0, 0]
        def emit_w1(fb):
            ab = fb % 2
            wb, wk = g["load_w"](w1[:, :, fb * 256:(fb + 1) * 256], (16, 256))
            for m in range(2):
                for ti in range(ntl):
                    pb = cw[0] % 2
                    cw[0] += 1
                    for kc in range(16):
                        MM(ps[pb][:], wb[:, kc, m * 128:(m + 1) * 128], hT[:, kc, ti * 512:(ti + 1) * 512], kc == 0, kc == 15,
                           [wk] + hk, [psk(pb)])
                    ACTF(rl[pb], ps[pb][:], AF.Relu, [psk(pb)], [("rl", pb)])
                    TT_("pool", act[ab][:, m, ti * 512:(ti + 1) * 512], rl[pb], rl[pb], ALU.mult, [("rl", pb)], [("act", ab)])
        def emit_w2(fb):
            ab = fb % 2
            w2b_, w2k = g["load_w"](w2[fb * 256:(fb + 1) * 256, :].rearrange("(c p) n -> p c n", p=128), (2, D))
            for dc in range(16):
                for ti in range(ntl):
                    pb = 2 + cw[1] % 4
                    cw[1] += 1
                    for m in range(2):
                        MM(ps[pb][:], w2b_[:, m, dc * 128:(dc + 1) * 128], act[ab][:, m, ti * 512:(ti + 1) * 512], m == 0, m == 1,
                           [w2k, ("act", ab)], [psk(pb)])
                    TT_("dve", yacc[:, dc, ti * 512:(ti + 1) * 512], yacc[:, dc, ti * 512:(ti + 1) * 512], ps[pb][:], ALU.add,
                        [psk(pb), ("yacc", dc, ti)], [("yacc", dc, ti)])
        NFB = 4 * D // 256
        emit_w1(0)
        for fb in range(NFB):
            if fb + 1 < NFB:
                emit_w1(fb + 1)
            emit_w2(fb)
        cnt = 0
        for dc in range(16):
            for ti, tl in enumerate(tiles):
                j = 0 if tl * 512 < TS else 1
                pb = cnt % 4
                cnt += 1
                DMA(xst[pb], scr_x[dc * 128:(dc + 1) * 128, tl * 512:(tl + 1) * 512], [("scr_x", dc, tl)], [("xst", pb)], q="pool")
                STT(xst[pb], yacc[:, dc, ti * 512:(ti + 1) * 512], MOD[:, 80 + dc, j:j + 1], xst[pb], ALU.mult, ALU.add,
                    [("yacc", dc, ti), "MOD", ("xst", pb)], [("xst", pb)])
                DMA(scr_x[dc * 128:(dc + 1) * 128, tl * 512:(tl + 1) * 512], xst[pb], [("xst", pb)], [("scr_x", dc, tl)], q="pool")
        P.barrier()


def ktail_final(ctx):
    g = ctx
    P, AR, ps, psk, O = g["P"], g["AR"], g["ps"], g["psk"], g["O"]
    MM, TRN, ACTF, STT, CP, DMA, RSTD = g["MM"], g["TRN"], g["ACTF"], g["STT"], g["CP"], g["DMA"], g["RSTD"]
    cfg = g["cfg"]; TS = cfg.TS; NT = g["NT"]
    ident, ones_bf, gfin, scr_xv = g["ident"], g["ones_bf"], g["gfin"], g["scr_xv"]
    AR.reset()
    xg = AR.get([128, 16, 512]); sqb = AR.get([128, 16, 512], BF16); rstd = AR.get([128, 512])
    yT = AR.get([128, 16, 512])
    yt = [AR.get([128, D]) for _ in range(2)]
    n = 0
    for tl in range(NT):
        DMA(xg, scr_xv[:, :, tl * 512:(tl + 1) * 512], ["scr_x"], ["xg"])
        ACTF(sqb, xg, AF.Square, ["xg"], ["sqb"])
        for c in range(16):
            MM(ps[7][:], ones_bf[:], sqb[:, c, :], c == 0, c == 15, ["sqb", "ones_bf"], [psk(7)])
        RSTD(rstd, ps[7][:], [psk(7)], ["rstd"], 1.0 / D, 1e-6)
        for c in range(16):
            STT(yT[:, c, :], xg[:, c, :], gfin[:, c:c + 1], rstd, ALU.mult, ALU.mult, ["xg", "gfin", "rstd"], ["yT"])
        for nb in range(4):
            b = n % 2
            n += 1
            for q in range(4):
                pb = q % 2
                for j in range(4):
                    c = 4 * q + j
                    TRN(ps[pb][:, j * 128:(j + 1) * 128], yT[:, c, nb * 128:(nb + 1) * 128], ident[:], ["yT", "ident"], [psk(pb)])
                CP("act" if q % 2 else "dve", yt[b][:, q * 512:(q + 1) * 512], ps[pb][:], [psk(pb)], [("yt", b)])
            tok = tl * 512 + nb * 128
            dst = O["y_s"][tok:tok + 128, :] if tok < TS else O["y_p"][tok - TS:tok - TS + 128, :]
            DMA(dst, yt[b], [("yt", b)], ["yout"])
    P.barrier()


import math
import numpy as np
from contextlib import ExitStack
import concourse.bass as bass
import concourse.mybir as mybir

D = 2048
KC = 16
NCOLS_IN = 3136
PAST = 256
EPS = 1e-6


class Cfg:
    def __init__(self, TS=2048, NP=4, TP=256, depth=2, upto="all", dbg=False):
        self.TS, self.NP, self.TP, self.depth, self.upto, self.dbg = TS, NP, TP, depth, upto, dbg
        self.TT = TS + NP * TP


WSPEC = [
    ("norm_mix", [D]), ("norm_mlp", [D]), ("w_ada", [D, 6 * D]), ("b_ada", [6 * D]), ("w_in", [D, NCOLS_IN]),
    ("w_out", [D, D]), ("s5_a_re", [2, 32, 64]), ("s5_a_im", [2, 32, 64]), ("s5_log_dt", [2, 32]),
    ("s5_b_re", [2, 32, 64, 16]), ("s5_b_im", [2, 32, 64, 16]), ("s5_c_re", [2, 32, 16, 64]), ("s5_c_im", [2, 32, 16, 64]),
    ("s5_d", [512]), ("s5_w_glu", [512, 1024]), ("s5_out_norm", [512]), ("rwkv_mu", [1792]), ("rwkv_w0", [2, 512]),
    ("rwkv_w2", [2, 64, 512]), ("rwkv_a0", [2, 512]), ("rwkv_a2", [2, 64, 512]), ("rwkv_g2", [128, 512]),
    ("rwkv_k_k", [512]), ("rwkv_k_a", [512]), ("rwkv_r_k", [8, 64]), ("rwkv_ln_w", [512]), ("rwkv_ln_b", [512]),
    ("mla_q_norm", [512]), ("mla_w_uq", [512, 1536]), ("mla_kv_norm", [256]), ("mla_w_ukv", [256, 2048]),
    ("mla_out_norm", [1024]), ("mlp_w1", [D, 4 * D]), ("mlp_w2", [4 * D, D]),
]


def host_consts(cfg):
    c = {}
    c["ident"] = np.eye(128, dtype=np.float32)
    blk = np.zeros((128, 128), np.float32); blk[:64, :64] = 1; blk[64:, 64:] = 1
    c["blk"] = blk
    Rm = np.zeros((64, 64), np.float32)
    for half in range(2):
        o = half * 32
        for i in range(16):
            Rm[o + i + 16, o + i] = -1.0
            Rm[o + i, o + i + 16] = 1.0
    c["rm"] = Rm
    T = cfg.TS
    t = np.arange(T)
    row = (t // 64).astype(np.float32); col = (t % 64).astype(np.float32)
    inv = (1.0 / (10000.0 ** (np.arange(0, 32, 2, dtype=np.float32) / 32))).astype(np.float32)
    ang = np.concatenate([row[:, None] * inv, row[:, None] * inv, col[:, None] * inv, col[:, None] * inv], -1)
    c["cosT"] = np.ascontiguousarray(np.cos(ang).T.astype(np.float32))
    c["sinT"] = np.ascontiguousarray(np.sin(ang).T.astype(np.float32))
    sel2 = np.zeros((2, 128), np.float32); sel2[0, :64] = 1; sel2[1, 64:] = 1
    c["sel2"] = sel2
    m4 = np.zeros((128, 4), np.float32)
    for r in range(128):
        m4[r, r // 32] = 1
    c["mask4"] = m4
    m8 = np.zeros((128, 8), np.float32)
    for r in range(128):
        m8[r, r // 16] = 1
    c["mask8"] = m8
    s_ = np.arange(64)[:, None]; t_ = np.arange(64)[None, :]
    def bd(m):
        z = np.zeros((128, 128), np.float32); z[:64, :64] = m; z[64:, 64:] = m; return z
    c["m_su"] = bd((s_ < t_).astype(np.float32)); c["m_iu"] = bd((s_ <= t_).astype(np.float32))
    c["m_sl"] = bd((s_ > t_).astype(np.float32)); c["m_il"] = bd((s_ >= t_).astype(np.float32))
    rs = np.ones((128, 512), np.float32); rs[:, ::64] = 0
    c["reset"] = rs
    c["twopi"] = np.full((128, 64), 2 * math.pi, np.float32)
    c["ki_tab"] = np.ascontiguousarray(np.broadcast_to(np.arange(32).astype(np.float32), (128, 32)))
    r256 = np.ones((128, 1024), np.float32); r256[:, ::256] = 0
    c["rst256"] = r256
    c["ji_tab"] = np.ascontiguousarray(np.broadcast_to(np.arange(64).astype(np.float32), (128, 64)))
    return c


def build(cfg):
    nc = bass.Bass("TRN2", target_bir_lowering=False)
    TS, NP, TP, L, TT = cfg.TS, cfg.NP, cfg.TP, cfg.depth, cfg.TT
    assert TS % 512 == 0 and (NP * TP) % 512 == 0 and TP % 128 == 0
    NT = TT // 512
    dt_in = lambda n, s: nc.dram_tensor(n, list(s), F32, kind="ExternalInput").ap()
    dt_out = lambda n, s: nc.dram_tensor(n, list(s), F32, kind="ExternalOutput").ap()
    dt_scr = lambda n, s, d=F32: nc.dram_tensor(n, list(s), d, kind="Internal").ap()
    I = {}
    I["xs"] = dt_in("xs", [TS, D]); I["xp"] = dt_in("xp", [NP * TP, D])
    I["c_ckv"] = dt_in("c_ckv", [L, PAST, 256]); I["c_kr"] = dt_in("c_kr", [L, PAST, 64])
    I["st_s5"] = dt_in("st_s5", [L, 2, 32, 64, 2]); I["st_rw"] = dt_in("st_rw", [L, 2, 8, 64, 64])
    I["cvec"] = dt_in("cvec", [2, D]); I["norm_final"] = dt_in("norm_final", [D])
    W = {n: dt_in(n, [L] + s) for n, s in WSPEC}
    HC = host_consts(cfg)
    C = {n: dt_in("k_" + n, a.shape) for n, a in HC.items()}
    O = {}
    O["y_s"] = dt_out("y_s", [TS, D]); O["y_p"] = dt_out("y_p", [NP * TP, D])
    O["o_ckv"] = dt_out("o_ckv", [NP, L, TP, 256]); O["o_kr"] = dt_out("o_kr", [NP, L, TP, 64])
    O["o_s5"] = dt_out("o_s5", [NP, L, 2, 32, 64, 2]); O["o_rw"] = dt_out("o_rw", [NP, L, 2, 8, 64, 64])
    scr_x = dt_scr("scr_x", [D, TT]); scr_p = dt_scr("scr_p", [3200, TT]); scr_m = dt_scr("scr_m", [D, TT])
    scr_r = dt_scr("scr_r", [2, 128, TT])
    if cfg.dbg:
        O["d_x"] = dt_out("d_x", [D, TT]); O["d_p"] = dt_out("d_p", [3200, TT]); O["d_m"] = dt_out("d_m", [D, TT])
        O["d_r"] = dt_out("d_r", [2, 128, TT])

    seqs = [(0, TS, True, -1)] + [(TS + i * TP, TP, False, i) for i in range(NP)]

    st = ExitStack()
    P = Prog(nc)
    sb = lambda n, s, d=F32: st.enter_context(nc.sbuf_tensor(n, list(s), d))
    ident = sb("ident", [128, 128]); blkf = sb("blkf", [128, 128]); ones_bf = sb("ones_bf", [128, 128], BF16)
    blk_bf = sb("blk_bf", [128, 128], BF16); identb = sb("identb", [128, 128], BF16)
    rm = sb("rm", [64, 64]); sel2 = sb("sel2", [2, 128]); mask4 = sb("mask4", [128, 4]); mask8 = sb("mask8", [128, 8])
    msk = {k: sb(k, [128, 128]) for k in ("m_su", "m_iu", "m_sl", "m_il")}
    reset = sb("reset", [128, 512]); twopi = sb("twopi", [128, 64])
    VA = sb("VA", [128, 128]); VB = sb("VB", [128, 128]); MOD = sb("MOD", [128, 96, 2]); AMs = sb("AMs", [128, 2, 16, 2])
    scT = sb("scT", [128, 16, 2]); scTb = sb("scTb", [128, 16, 2], BF16); gfin = sb("gfin", [128, 16]); omm = sb("omm", [128, 14]); hmu = sb("hmu", [128, 14])
    omka = sb("omka", [128, 4])
    wst = [sb(f"wst{i}", [128, 4096]) for i in range(2)]
    wbf = [sb(f"wbf{i}", [128, 4096], BF16) for i in range(2)]
    ARENA = 38500
    arena = sb("arena", [128, ARENA])
    ps = [st.enter_context(nc.psum_tensor(f"ps{i}", [128, 512], F32)) for i in range(8)]
    psk = lambda i: ("ps", i)

    class Arena:
        def __init__(self):
            self.off = 0
        def reset(self):
            self.off = 0
        def get(self, shape, dt=F32):
            n = int(np.prod(shape[1:]))
            nf = n if dt == F32 else (n + 1) // 2
            a = arena[:, self.off:self.off + nf]
            self.off += nf
            assert self.off <= ARENA, (self.off, ARENA)
            if dt != F32:
                a = a.bitcast(BF16)[:, 0:n]
            if len(shape) == 3:
                a = a.rearrange("p (a b) -> p a b", a=shape[1])
            elif len(shape) == 4:
                a = a.rearrange("p (a b c) -> p a b c", a=shape[1], b=shape[2])
            if shape[0] < 128:
                a = a[0:shape[0]]
            return a
    AR = Arena()

    def MM(out, lhsT, rhs, start, stop, R, Wk):
        P.op("pe", lambda e: e.matmul(out, lhsT, rhs, start=start, stop=stop), R, Wk)
    def TRN(out, in_, idt, R, Wk):
        P.op("pe", lambda e: e.transpose(out, in_, idt), R, Wk)
    def ACTF(out, in_, func, R, Wk, bias=0.0, scale=1.0):
        P.op("act", lambda e: e.activation(out, in_, func, bias=bias, scale=scale), R, Wk)
    def TT_(eng, out, a, b, op, R, Wk):
        P.op(eng, lambda e: e.tensor_tensor(out, a, b, op), R, Wk)
    def TS_(eng, out, a, s1, s2, op0, op1, R, Wk):
        if s2 is None and eng == "pool" and op0 in (ALU.mult, ALU.add):
            s2, op1 = (0.0, ALU.add) if op0 == ALU.mult else (1.0, ALU.mult)
        if s2 is None:
            P.op(eng, lambda e: e.tensor_scalar(out, a, s1, None, op0), R, Wk)
        else:
            P.op(eng, lambda e: e.tensor_scalar(out, a, s1, s2, op0, op1), R, Wk)
    def STT(out, a, s, b, op0, op1, R, Wk):
        P.op("dve", lambda e: e.scalar_tensor_tensor(out, a, s, b, op0, op1), R, Wk)
    def CP(eng, out, in_, R, Wk):
        if eng == "act":
            P.op("act", lambda e: e.copy(out, in_), R, Wk)
        else:
            P.op(eng, lambda e: e.tensor_copy(out, in_), R, Wk)
    def MSET(eng, out, v, Wk):
        P.op(eng, lambda e: e.memset(out, v), (), Wk)
    def DMA(out, in_, R, Wk, slow=False, q="sp"):
        if slow:
            P.dma(q, lambda e: e.dma_start(out=out, in_=in_, allow_slow_non_contiguous=True), R, Wk)
        else:
            P.dma(q, lambda e: e.dma_start(out=out, in_=in_), R, Wk)
    def RSTD(out, in_, R, Wk, scale, eps):
        ACTF(out, in_, AF.Ln, R, Wk, bias=eps, scale=scale)
        ACTF(out, out, AF.Exp, Wk, Wk, scale=-0.5)

    wcount = [0]
    def load_w(src_ap, shape3, R=()):
        i = wcount[0] % 2
        wcount[0] += 1
        a, b = shape3
        sv = wst[i][:, 0:a * b].rearrange("p (a b) -> p a b", a=a)
        bv = wbf[i][:, 0:a * b].rearrange("p (a b) -> p a b", a=a)
        DMA(sv, src_ap, R, [("wst", i)])
        CP("act", bv, sv, [("wst", i)], [("wbf", i)])
        return bv, ("wbf", i)

    for n, t_ in (("ident", ident), ("blk", blkf), ("rm", rm), ("sel2", sel2), ("mask4", mask4), ("mask8", mask8),
                  ("reset", reset), ("twopi", twopi)):
        DMA(t_[:], C[n], (), [n])
    for k in msk:
        DMA(msk[k][:], C[k], (), [k])
    MSET("pool", ones_bf[:], 1.0, ["ones_bf"])
    CP("dve", blk_bf[:], blkf[:], ["blk"], ["blk_bf"])
    CP("dve", identb[:], ident[:], ["ident"], ["identb"])
    for j in range(2):
        DMA(scT[:, :, j], I["cvec"][j].rearrange("(c p) -> p c", p=128), (), ["scT"], slow=True)
    ACTF(scT[:], scT[:], AF.Silu, ["scT"], ["scT"])
    CP("dve", scTb[:], scT[:], ["scT"], ["scTb"])
    AR.reset()
    stg = AR.get([128, 128])
    DMA(stg[0:16, :], I["norm_final"].rearrange("(c p) -> c p", p=128), (), ["stg"])
    TRN(ps[0][:, 0:16], stg[0:16, :], ident[0:16, 0:16], ["stg", "ident"], [psk(0)])
    CP("dve", gfin[:], ps[0][:, 0:16], [psk(0)], ["gfin"])
    P.barrier()

    def stage0():
        AR.reset()
        xin = [AR.get([128, D]) for _ in range(2)]
        xTs = [AR.get([128, 16, 128]) for _ in range(2)]
        scr_xv = scr_x.rearrange("(c p) t -> p c t", p=128)
        for n in range(TT // 128):
            b = n % 2
            src = I["xs"][n * 128:(n + 1) * 128, :] if n * 128 < TS else I["xp"][n * 128 - TS:(n + 1) * 128 - TS, :]
            DMA(xin[b], src, (), [("xin", b)])
            for q in range(4):
                pb = q % 2
                for j in range(4):
                    c = 4 * q + j
                    TRN(ps[pb][:, j * 128:(j + 1) * 128], xin[b][:, c * 128:(c + 1) * 128], ident[:],
                        [("xin", b), "ident"], [psk(pb)])
                CP("act" if q % 2 else "dve", xTs[b][:, 4 * q:4 * q + 4, :],
                   ps[pb][:].rearrange("p (a b) -> p a b", a=4), [psk(pb)], [("xTs", b)])
            DMA(scr_xv[:, :, n * 128:(n + 1) * 128], xTs[b], [("xTs", b)], [("scr_x", n)], q="pool")
        P.barrier()

    VBI = {}
    def layer_vectors(l):
        AR.reset()
        sA = AR.get([128, 128]); sB = AR.get([128, 128])
        DMA(sA[0:96, :], W["b_ada"][l].rearrange("(c p) -> c p", p=128), (), ["sA"])
        DMA(sA[96:112, :], W["norm_mix"][l].rearrange("(c p) -> c p", p=128), (), ["sA"])
        DMA(sA[112:128, :], W["norm_mlp"][l].rearrange("(c p) -> c p", p=128), (), ["sA"])
        TRN(ps[0][:, 0:128], sA, ident[:], ["sA", "ident"], [psk(0)])
        CP("dve", VA[:], ps[0][:, 0:128], [psk(0)], ["VA"])
        MSET("pool", sB, 0.0, ["sB"])
        r = 0
        VBI.clear()
        for nm, ap2 in (("s5_d", W["s5_d"][l]), ("s5_on", W["s5_out_norm"][l]), ("mu", W["rwkv_mu"][l]),
                        ("w0_0", W["rwkv_w0"][l, 0]), ("w0_1", W["rwkv_w0"][l, 1]), ("a0_0", W["rwkv_a0"][l, 0]),
                        ("a0_1", W["rwkv_a0"][l, 1]), ("k_k", W["rwkv_k_k"][l]), ("k_a", W["rwkv_k_a"][l]),
                        ("r_k", W["rwkv_r_k"][l].rearrange("h n -> (h n)")), ("ln_w", W["rwkv_ln_w"][l]),
                        ("ln_b", W["rwkv_ln_b"][l]), ("q_n", W["mla_q_norm"][l]), ("kv_n", W["mla_kv_norm"][l]),
                        ("mla_on", W["mla_out_norm"][l])):
            nr = ap2.shape[0] // 128
            DMA(sB[r:r + nr, :], ap2.rearrange("(c p) -> c p", p=128), ["sB"], ["sB"])
            VBI[nm] = r
            r += nr
        assert r <= 128
        TRN(ps[1][:, 0:128], sB, ident[:], ["sB", "ident"], [psk(1)])
        CP("dve", VB[:], ps[1][:, 0:128], [psk(1)], ["VB"])
        m0 = VBI["mu"]
        TS_("dve", omm[:], VB[:, m0:m0 + 14], -1.0, 1.0, ALU.mult, ALU.add, ["VB"], ["omm"])
        TS_("dve", hmu[:], VB[:, m0:m0 + 14], 0.5, None, ALU.mult, None, ["VB"], ["hmu"])
        ka = VBI["k_a"]
        TS_("dve", omka[:], VB[:, ka:ka + 4], -1.0, 1.0, ALU.mult, ALU.add, ["VB"], ["omka"])
        wv = W["w_ada"][l].rearrange("(c p) n -> p c n", p=128)
        for blk in range(6 * D // 256):
            wb_, wk_ = load_w(wv[:, :, blk * 256:(blk + 1) * 256], (16, 256))
            for m in range(2):
                ch = blk * 2 + m
                pb = ch % 2
                for kc in range(16):
                    MM(ps[pb][:, 0:2], wb_[:, kc, m * 128:(m + 1) * 128], scTb[:, kc, :], kc == 0, kc == 15,
                       [wk_, "scTb"], [psk(pb)])
                TS_("dve", MOD[:, ch, :], ps[pb][:, 0:2], VA[:, ch:ch + 1], None, ALU.add, None, [psk(pb), "VA"], ["MOD"])
        for which, (sc0, g0) in enumerate(((16, 96), (64, 112))):
            for j in range(2):
                TS_("dve", AMs[:, which, :, j], MOD[:, sc0:sc0 + 16, j], 1.0, None, ALU.add, None, ["MOD"], ["AMs"])
                TT_("dve", AMs[:, which, :, j], AMs[:, which, :, j], VA[:, g0:g0 + 16], ALU.mult, ["AMs", "VA"], ["AMs"])
        P.barrier()

    def norm_mod(xg, xk, hT_out, hk, which, j, sqb, rstd, tmp):
        sh0 = 0 if which == 0 else 48
        ACTF(sqb, xg, AF.Square, [xk], ["sqb"])
        for c in range(16):
            MM(ps[7][:], ones_bf[:], sqb[:, c, :], c == 0, c == 15, ["sqb", "ones_bf"], [psk(7)])
        RSTD(rstd, ps[7][:], [psk(7)], ["rstd"], 1.0 / D, EPS)
        for c in range(16):
            t2 = tmp[c % 2]
            STT(t2, xg[:, c, :], AMs[:, which, c, j:j + 1], rstd, ALU.mult, ALU.mult, [xk, "AMs", "rstd"], [("tmp", c % 2)])
            ACTF(hT_out[:, c, :], t2, AF.Identity, [("tmp", c % 2), "MOD"], [hk], bias=MOD[:, sh0 + c, j:j + 1])

    scr_xv = scr_x.rearrange("(c p) t -> p c t", p=128)

    def stageA(l):
        wv = W["w_in"][l].rearrange("(c p) n -> p c n", p=128)
        for g0 in range(0, NT, 2):
            tiles = list(range(g0, min(g0 + 2, NT)))
            AR.reset()
            xg = AR.get([128, 16, 512]); sqb = AR.get([128, 16, 512], BF16)
            hT = AR.get([128, 16, 1024], BF16); rstd = AR.get([128, 512]); tmp = [AR.get([128, 512]) for _ in range(2)]
            ost = [AR.get([128, 512]) for _ in range(2)]
            for ti, tl in enumerate(tiles):
                j = 0 if tl * 512 < TS else 1
                DMA(xg, scr_xv[:, :, tl * 512:(tl + 1) * 512], ["scr_x"], ["xg"])
                norm_mod(xg, "xg", hT[:, :, ti * 512:(ti + 1) * 512], ("hT", ti), 0, j, sqb, rstd, tmp)
            cnt = 0
            for blk in range((NCOLS_IN + 255) // 256):
                c0 = blk * 256
                ncol = min(256, NCOLS_IN - c0)
                wb, wk = load_w(wv[:, :, c0:c0 + ncol], (16, ncol))
                for m0 in range(0, ncol, 128):
                    mw = min(128, ncol - m0)
                    for ti, tl in enumerate(tiles):
                        pb = cnt % 2
                        for kc in range(16):
                            MM(ps[pb][0:mw, :], wb[:, kc, m0:m0 + mw], hT[:, kc, ti * 512:(ti + 1) * 512], kc == 0, kc == 15,
                               [wk, ("hT", ti)], [psk(pb)])
                        CP("act" if cnt % 2 else "dve", ost[pb][0:mw, :], ps[pb][0:mw, :], [psk(pb)], [("ost", pb)])
                        DMA(scr_p[c0 + m0:c0 + m0 + mw, tl * 512:(tl + 1) * 512], ost[pb][0:mw, :], [("ost", pb)], [("scr_p", c0 + m0, tl)], q="pool")
                        cnt += 1
            P.barrier()

    def stageA2(l):
        AR.reset()
        pch = [AR.get([128, TS + 2]) for _ in range(2)]
        t1 = [AR.get([128, TS]) for _ in range(2)]
        t2 = [AR.get([128, TS]) for _ in range(2)]
        n = 0
        for (t0, T, smp, pi) in seqs:
            for ch in range(14):
                b = n % 2
                n += 1
                rows = slice(512 + ch * 128, 512 + (ch + 1) * 128)
                MSET("pool", pch[b][:, 0:1], 0.0, [("pch", b)])
                MSET("pool", pch[b][:, T + 1:T + 2], 0.0, [("pch", b)])
                DMA(pch[b][:, 1:T + 1], scr_p[rows, t0:t0 + T], [("scr_p", ch, t0)], [("pch", b)])
                TT_("dve", t1[b][:, 0:T], pch[b][:, 0:T], pch[b][:, 2:T + 2], ALU.add, [("pch", b)], [("t1", b)])
                ACTF(t2[b][:, 0:T], pch[b][:, 1:T + 1], AF.Copy, [("pch", b), "omm"], [("t2", b)], scale=omm[:, ch:ch + 1])
                STT(t1[b][:, 0:T], t1[b][:, 0:T], hmu[:, ch:ch + 1], t2[b][:, 0:T], ALU.mult, ALU.add,
                    [("t1", b), ("t2", b), "hmu"], [("t1", b)])
                DMA(scr_p[rows, t0:t0 + T], t1[b][:, 0:T], [("t1", b)], [("scr_p", ch, t0)], q="pool")
        P.barrier()

    ctx = dict(nc=nc, cfg=cfg, P=P, I=I, W=W, O=O, C=C, AR=AR, ps=ps, psk=psk, seqs=seqs, VB=VB, VBI=VBI, MOD=MOD, AMs=AMs,
               scr_x=scr_x, scr_p=scr_p, scr_m=scr_m, scr_r=scr_r, scr_xv=scr_xv, ident=ident, identb=identb,
               ones_bf=ones_bf, blk_bf=blk_bf, blkf=blkf, rm=rm, sel2=sel2, mask4=mask4, mask8=mask8, msk=msk, reset=reset,
               twopi=twopi, omka=omka, gfin=gfin, VA=VA, load_w=load_w, norm_mod=norm_mod,
               MM=MM, TRN=TRN, ACTF=ACTF, TT_=TT_, TS_=TS_, STT=STT, CP=CP, MSET=MSET, DMA=DMA, RSTD=RSTD, NT=NT)

    stage0()
    for l in range(L):
        layer_vectors(l)
        stageA(l)
        if cfg.upto == "A":
            break
        stageA2(l)
        kmix_s5(ctx, l)
        if cfg.upto == "s5":
            break
        krwkv_rwkv(ctx, l)
        if cfg.upto == "rwkv":
            break
        kmix_mla(ctx, l)
        if cfg.upto == "mla":
            break
        ktail_stageC(ctx, l)
    if cfg.upto == "all":
        ktail_final(ctx)
    if cfg.dbg:
        P.barrier()
        DMA(O["d_x"], scr_x, ["scr_x"], ["d_x"]); DMA(O["d_p"], scr_p, ["scr_p"], ["d_p"])
        DMA(O["d_m"], scr_m, ["scr_m"], ["d_m"]); DMA(O["d_r"], scr_r, ["scr_r"], ["d_r"])
    P.emit(st)
    st.close()
    return nc, HC, P.stats


from concourse.bass_utils import run_bass_kernel_spmd

_CACHE = {}


def kernel(**inp):
    inp = {k: np.asarray(v) for k, v in inp.items()}
    cfg = Cfg(TS=2048, NP=4, TP=256, depth=2, upto="all", dbg=False)
    if "nc" not in _CACHE:
        _CACHE["nc"] = build(cfg)
    nc, HC, stats = _CACHE["nc"]
    L = 2
    in_maps = []
    shared = {n: np.ascontiguousarray(inp[n], dtype=np.float32) for n, s in WSPEC}
    shared["norm_final"] = np.ascontiguousarray(inp["norm_final"], dtype=np.float32)
    for k, v in HC.items():
        shared["k_" + k] = v
    for b in range(8):
        m = dict(shared)
        m["xs"] = np.ascontiguousarray(inp["x_sample"][b])
        m["xp"] = np.ascontiguousarray(inp["x_prompt"][4 * b:4 * b + 4].reshape(1024, 2048))
        m["c_ckv"] = np.ascontiguousarray(inp["cache_mla_ckv"][b]); m["c_kr"] = np.ascontiguousarray(inp["cache_mla_krope"][b])
        m["st_s5"] = np.ascontiguousarray(inp["state_s5"][b]); m["st_rw"] = np.ascontiguousarray(inp["state_rwkv"][b])
        m["cvec"] = np.ascontiguousarray(np.stack([inp["c"][b], inp["c_ctx"]]).astype(np.float32))
        in_maps.append(m)
    res = run_bass_kernel_spmd(nc, in_maps, core_ids=list(range(8)))
    R = res.results
    y_p = np.concatenate([r["y_p"].reshape(4, 256, 2048) for r in R], 0)
    y_s = np.stack([r["y_s"] for r in R], 0)
    o_ckv = np.concatenate([r["o_ckv"] for r in R], 0)
    o_kr = np.concatenate([r["o_kr"] for r in R], 0)
    o_s5 = np.concatenate([r["o_s5"] for r in R], 0)
    o_rw = np.concatenate([r["o_rw"] for r in R], 0)
    return (y_p.astype(np.float32), y_s.astype(np.float32), o_ckv.astype(np.float32), o_kr.astype(np.float32),
            o_s5.astype(np.float32), o_rw.astype(np.float32))
```

```python
import numpy as np
import concourse.bass as bass
import concourse.mybir as mybir

F32 = mybir.dt.float32
BF16 = mybir.dt.bfloat16
AF = mybir.ActivationFunctionType
ALU = mybir.AluOpType
AX = mybir.AxisListType

COMPUTE = ("pe", "act", "dve", "pool", "sp")
EPOCH = 30000
EMBED_WAIT = True
N_DMA_SLOTS = {"sp": 20, "pool": 12, "act": 8}


class Op:
    __slots__ = ("eng", "fn", "reads", "writes", "dma", "deps", "sig", "slot", "slot_cnt", "idx", "kind")

    def __init__(self, eng, fn, reads, writes, dma):
        self.eng = eng
        self.fn = fn
        self.reads = tuple(reads)
        self.writes = tuple(writes)
        self.dma = dma
        self.deps = ()
        self.sig = None
        self.slot = None
        self.slot_cnt = None
        self.kind = None


class Prog:
    def __init__(self, nc, same_engine_sync=True):
        self.nc = nc
        self.ops = []
        self.same_engine_sync = same_engine_sync

    def op(self, eng, fn, reads=(), writes=()):
        self.ops.append(Op(eng, fn, reads, writes, False))

    def dma(self, q, fn, reads=(), writes=()):
        self.ops.append(Op(q, fn, reads, writes, True))

    def barrier(self):
        n = getattr(self, "_nbar", 0)
        self._nbar = n + 1
        engs = ("pe", "act", "dve", "pool", "sp")
        for e in engs:
            o = Op(e, (lambda en: en.nop()) if e == "sp" else (lambda en: en.drain()), (), [("bar", n, e)], False)
            o.kind = "arrive"
            self.ops.append(o)
        for e in engs:
            o = Op(e, lambda en: en.nop(), [("bar", n, x) for x in engs], (), False)
            o.kind = "depart"
            self.ops.append(o)

    def emit(self, stack):
        nc = self.nc
        ops = self.ops
        last_writer = {}
        readers = {}
        for i, o in enumerate(ops):
            o.idx = i
            deps = set()
            for k in o.reads:
                w = last_writer.get(k)
                if w is not None:
                    deps.add(w)
            for k in o.writes:
                w = last_writer.get(k)
                if w is not None:
                    deps.add(w)
                for r in readers.get(k, ()):
                    deps.add(r)
            deps.discard(i)
            o.deps = deps
            for k in o.reads:
                readers.setdefault(k, []).append(i)
            for k in o.writes:
                last_writer[k] = i
                readers[k] = []
            if o.kind == "depart" and o.eng == "sp":
                last_writer = {}
                readers = {}
        need_sig = set()
        for o in ops:
            for d in o.deps:
                p = ops[d]
                if p.dma:
                    continue
                if p.eng != o.eng or o.dma:
                    need_sig.add(d)
                elif self.same_engine_sync and p.eng != "pe":
                    need_sig.add(d)
        cnt = {e: 0 for e in COMPUTE}
        for o in ops:
            if o.dma:
                continue
            if o.idx in need_sig:
                cnt[o.eng] += 1
                o.sig = cnt[o.eng]
        dcnt = {q: 0 for q in N_DMA_SLOTS}
        slot_uses = {}
        for o in ops:
            if o.dma:
                n = N_DMA_SLOTS[o.eng]
                s = dcnt[o.eng] % n
                dcnt[o.eng] += 1
                o.slot = (o.eng, s)
                slot_uses[o.slot] = slot_uses.get(o.slot, 0) + 1
                o.slot_cnt = slot_uses[o.slot]
        sems = {}
        for e in COMPUTE:
            for ep in range(max(0, cnt[e] - 1) // EPOCH + 1):
                sems[(e, ep)] = stack.enter_context(nc.semaphore(f"s_{e}_{ep}"))
        dsems = {}
        for q, n in N_DMA_SLOTS.items():
            for s in range(min(n, dcnt[q])):
                dsems[(q, s)] = stack.enter_context(nc.semaphore(f"d_{q}_{s}"))
        self.stats = dict(cnt=cnt, dcnt=dcnt, nops=len(ops))

        per_eng = {e: [] for e in ("pe", "act", "dve", "pool", "sp")}
        for o in ops:
            per_eng[o.eng].append(o)

        def run_engine(ename, eng):
            waited = {}
            dwaited = {}
            issued = {}
            for o in per_eng[ename]:
                if o.dma:
                    issued[o.slot] = o.slot_cnt
                if o.kind == "arrive":
                    for sl, v in issued.items():
                        if dwaited.get(sl, 0) < v:
                            eng.wait_ge(dsems[sl], 16 * v)
                            dwaited[sl] = v
                cw = {}
                dw = {}
                for d in o.deps:
                    p = ops[d]
                    if p.dma:
                        if dwaited.get(p.slot, 0) < p.slot_cnt:
                            dw[p.slot] = max(dw.get(p.slot, 0), p.slot_cnt)
                    else:
                        if p.sig is None:
                            continue
                        if p.eng == ename and not o.dma and not (self.same_engine_sync and ename != "pe"):
                            continue
                        if waited.get(p.eng, 0) < p.sig:
                            cw[p.eng] = max(cw.get(p.eng, 0), p.sig)
                if o.dma and o.slot_cnt > 1:
                    if dwaited.get(o.slot, 0) < o.slot_cnt - 1:
                        dw[o.slot] = max(dw.get(o.slot, 0), o.slot_cnt - 1)
                wl = []
                for pe_, v in cw.items():
                    ep, r = divmod(v - 1, EPOCH)
                    wl.append((sems[(pe_, ep)], r + 1))
                    waited[pe_] = v
                for sl, v in dw.items():
                    wl.append((dsems[sl], 16 * v))
                    dwaited[sl] = v
                embed = None
                if EMBED_WAIT and wl and not o.dma and o.kind is None:
                    embed = wl.pop()
                for (sm_, vv_) in wl:
                    eng.wait_ge(sm_, vv_)
                ins = o.fn(eng)
                if embed is not None:
                    ins._wait_ge(embed[0], embed[1])
                if o.dma:
                    ins.then_inc(dsems[o.slot], 16)
                elif o.sig is not None:
                    ins.then_inc(sems[(o.eng, (o.sig - 1) // EPOCH)], 1)
            if ename in N_DMA_SLOTS:
                for (q, s), v in slot_uses.items():
                    if q == ename and dwaited.get((q, s), 0) < v:
                        eng.wait_ge(dsems[(q, s)], 16 * v)

        with nc.Block() as block:
            @block.tensor
            def _(e):
                run_engine("pe", e)

            @block.scalar
            def _(e):
                run_engine("act", e)

            @block.vector
            def _(e):
                run_engine("dve", e)

            @block.gpsimd
            def _(e):
                run_engine("pool", e)

            @block.sync
            def _(e):
                run_engine("sp", e)


import math
import numpy as np

PAST = 256


def _load_bf(g, dst, src, shape2, key):
    DMA, CP = g["DMA"], g["CP"]
    bv, bk = g["load_w"](src, shape2)
    np_ = dst.shape[0]
    CP("dve", dst, bv[0:np_], [bk], [key])


def kmix_mla(g, l):
    P, AR, ps, psk, W, I, O, C, VB, VBI = g["P"], g["AR"], g["ps"], g["psk"], g["W"], g["I"], g["O"], g["C"], g["VB"], g["VBI"]
    MM, TRN, ACTF, TT_, TS_, STT, CP, DMA, MSET, RSTD = (g[k] for k in ("MM", "TRN", "ACTF", "TT_", "TS_", "STT", "CP", "DMA", "MSET", "RSTD"))
    ident, ones_bf, rm = g["ident"], g["ones_bf"], g["rm"]
    scr_p, scr_m, scr_r = g["scr_p"], g["scr_m"], g["scr_r"]
    SCALE = 192.0 ** -0.5
    cfg_ = g["cfg"]
    for (t0, T, smp, nsub, Tsub) in ((0, cfg_.TS, True, 1, cfg_.TS), (cfg_.TS, cfg_.NP * cfg_.TP, False, cfg_.NP, cfg_.TP)):
        AR.reset()
        nk = T + (PAST if smp else 0); nkc = nk // 128
        TL = min(512, T); ntl = T // TL
        cqn = AR.get([128, 4, T], BF16); keysT = AR.get([128, 2, nk], BF16); krR = AR.get([64, nk], BF16)
        wukv = AR.get([128, 2, 2048], BF16); wuq = AR.get([128, 4, 1536], BF16)
        kn = AR.get([128, nk], BF16); Vh = AR.get([128, nkc, 128], BF16); qn = AR.get([128, T], BF16)
        qrb = AR.get([64, T], BF16); qrf = AR.get([64, TL]); pT = [AR.get([128, TL], BF16) for _ in range(4)]
        dacc = AR.get([128, TL]); daccb = AR.get([128, TL], BF16)
        rden = AR.get([128, TL]); ost = [AR.get([128, TL]) for _ in range(2)]; sqt = AR.get([128, TL])
        ssacc = AR.get([128, T]); st_cq = AR.get([128, 4, TL]); sq4 = AR.get([128, 4, TL], BF16)
        st_kv = AR.get([128, 2, TL]); ckvn = AR.get([128, 2, TL]); st_kr = AR.get([64, TL])
        cs = AR.get([64, TL]); sn = AR.get([64, TL]); rstd = AR.get([128, TL]); ot = AR.get([128, 256])
        ck = AR.get([128, 2, 256]); kk_ = AR.get([128, 2, 64]); t64 = AR.get([64, TL]); ssb = AR.get([128, TL], BF16)
        wq_src = W["mla_w_uq"][l].rearrange("(c p) n -> p c n", p=128)
        for b_ in range(0, 1536, 512):
            _load_bf(g, wuq[:, :, b_:b_ + 512], wq_src[:, :, b_:b_ + 512], (4, 512), "wuq")
        wk_src = W["mla_w_ukv"][l].rearrange("(c p) n -> p c n", p=128)
        for b_ in range(0, 2048, 1024):
            _load_bf(g, wukv[:, :, b_:b_ + 1024], wk_src[:, :, b_:b_ + 1024], (2, 1024), "wukv")
        qn0, kvn0 = VBI["q_n"], VBI["kv_n"]
        for tl in range(ntl):
            sl = slice(tl * TL, (tl + 1) * TL)
            gsl = slice(t0 + tl * TL, t0 + (tl + 1) * TL)
            DMA(st_cq, scr_p[2304:2816, gsl].rearrange("(c p) t -> p c t", p=128), ["scr_p"], ["st_cq"])
            ACTF(sq4, st_cq, AF.Square, ["st_cq"], ["sq4"])
            for c in range(4):
                MM(ps[4][:, 0:TL], ones_bf[:], sq4[:, c, :], c == 0, c == 3, ["sq4", "ones_bf"], [psk(4)])
            RSTD(rstd, ps[4][:, 0:TL], [psk(4)], ["rstd"], 1.0 / 512, 1e-6)
            for c in range(4):
                STT(cqn[:, c, sl], st_cq[:, c, :], VB[:, qn0 + c:qn0 + c + 1], rstd, ALU.mult, ALU.mult, ["st_cq", "VB", "rstd"], ["cqn"])
            DMA(st_kv, scr_p[2816:3072, gsl].rearrange("(c p) t -> p c t", p=128), ["scr_p"], ["st_kv"])
            ACTF(sq4[:, 0:2, :], st_kv, AF.Square, ["st_kv"], ["sq4"])
            for c in range(2):
                MM(ps[5][:, 0:TL], ones_bf[:], sq4[:, c, :], c == 0, c == 1, ["sq4", "ones_bf"], [psk(5)])
            RSTD(rstd, ps[5][:, 0:TL], [psk(5)], ["rstd"], 1.0 / 256, 1e-6)
            for c in range(2):
                STT(ckvn[:, c, :], st_kv[:, c, :], VB[:, kvn0 + c:kvn0 + c + 1], rstd, ALU.mult, ALU.mult, ["st_kv", "VB", "rstd"], ["ckvn"])
            CP("act", keysT[:, :, sl], ckvn, ["ckvn"], ["keysT"])
            DMA(st_kr, scr_p[3072:3136, gsl], ["scr_p"], ["st_kr"])
            if not smp:
                for nb in range(TL // 128):
                    for c in range(2):
                        TRN(ps[6][:, c * 128:(c + 1) * 128], ckvn[:, c, nb * 128:(nb + 1) * 128], ident[:], ["ckvn", "ident"], [psk(6)])
                    CP("dve", ot, ps[6][:, 0:256], [psk(6)], ["ot"])
                    tg = tl * TL + nb * 128
                    pi = tg // Tsub; tk = tg % Tsub
                    DMA(O["o_ckv"][pi, l, tk:tk + 128, :], ot, ["ot"], ["o_ckv"])
                    TRN(ps[6][:, 256:320], st_kr[:, nb * 128:(nb + 1) * 128], ident[0:64, 0:64], ["st_kr", "ident"], [psk(6)])
                    CP("dve", ot[:, 0:64], ps[6][:, 256:320], [psk(6)], ["ot"])
                    DMA(O["o_kr"][pi, l, tk:tk + 128, :], ot[:, 0:64], ["ot"], ["o_kr"])
                CP("act", krR[:, sl], st_kr, ["st_kr"], ["krR"])
            else:
                DMA(cs, C["cosT"][:, tl * TL:(tl + 1) * TL], (), ["cs"])
                DMA(sn, C["sinT"][:, tl * TL:(tl + 1) * TL], (), ["sn"])
                MM(ps[6][0:64, 0:TL], rm[:], st_kr, True, True, ["rm", "st_kr"], [psk(6)])
                TT_("dve", t64, ps[6][0:64, 0:TL], sn, ALU.mult, [psk(6), "sn"], ["t64"])
                TT_("pool", st_kr, st_kr, cs, ALU.mult, ["st_kr", "cs"], ["st_kr"])
                TT_("dve", krR[:, sl], st_kr, t64, ALU.add, ["st_kr", "t64"], ["krR"])
        if smp:
            DMA(ck, I["c_ckv"][l].rearrange("(n p) f -> p n f", p=128), (), ["ck"])
            DMA(kk_, I["c_kr"][l].rearrange("(n p) f -> p n f", p=128), (), ["kk_"])
            for n_ in range(2):
                for c in range(2):
                    TRN(ps[6][:, c * 128:(c + 1) * 128], ck[:, n_, c * 128:(c + 1) * 128], ident[:], ["ck", "ident"], [psk(6)])
                CP("dve", keysT[:, :, T + n_ * 128:T + (n_ + 1) * 128], ps[6][:, 0:256].rearrange("p (a b) -> p a b", a=2), [psk(6)], ["keysT"])
                TRN(ps[6][0:64, 256:384], kk_[:, n_, :], ident[:], ["kk_", "ident"], [psk(6)])
                CP("dve", krR[:, T + n_ * 128:T + (n_ + 1) * 128], ps[6][0:64, 256:384], [psk(6)], ["krR"])
        for h in range(8):
            for k0 in range(0, nk, 512):
                kw = min(512, nk - k0)
                for kc in range(2):
                    MM(ps[4][:, 0:kw], wukv[:, kc, 256 * h:256 * h + 128], keysT[:, kc, k0:k0 + kw], kc == 0, kc == 1, ["wukv", "keysT"], [psk(4)])
                CP("act", kn[:, k0:k0 + kw], ps[4][:, 0:kw], [psk(4)], ["kn"])
            for q0 in range(0, nkc, 4):
                nq = min(4, nkc - q0)
                for qq in range(nq):
                    kcn = q0 + qq
                    for kc in range(2):
                        MM(ps[5][:, qq * 128:(qq + 1) * 128], keysT[:, kc, kcn * 128:(kcn + 1) * 128], wukv[:, kc, 256 * h + 128:256 * h + 256],
                           kc == 0, kc == 1, ["wukv", "keysT"], [psk(5)])
                CP("dve", Vh[:, q0:q0 + nq, :], ps[5][:, 0:nq * 128].rearrange("p (a b) -> p a b", a=nq), [psk(5)], ["Vh"])
            for tl in range(ntl):
                sl = slice(tl * TL, (tl + 1) * TL)
                for kc in range(4):
                    MM(ps[4][:, 0:TL], wuq[:, kc, 192 * h:192 * h + 128], cqn[:, kc, sl], kc == 0, kc == 3, ["wuq", "cqn"], [psk(4)])
                CP("act", qn[:, sl], ps[4][:, 0:TL], [psk(4)], ["qn"])
                for kc in range(4):
                    MM(ps[6][0:64, 0:TL], wuq[:, kc, 192 * h + 128:192 * h + 192], cqn[:, kc, sl], kc == 0, kc == 3, ["wuq", "cqn"], [psk(6)])
                if not smp:
                    CP("dve", qrb[:, sl], ps[6][0:64, 0:TL], [psk(6)], ["qrb"])
                else:
                    CP("dve", qrf, ps[6][0:64, 0:TL], [psk(6)], ["qrf"])
                    DMA(cs, C["cosT"][:, tl * TL:(tl + 1) * TL], (), ["cs"])
                    DMA(sn, C["sinT"][:, tl * TL:(tl + 1) * TL], (), ["sn"])
                    MM(ps[6][0:64, 0:TL], rm[:], qrf, True, True, ["rm", "qrf"], [psk(6)])
                    TT_("dve", t64, ps[6][0:64, 0:TL], sn, ALU.mult, [psk(6), "sn"], ["t64"])
                    TT_("pool", qrf, qrf, cs, ALU.mult, ["qrf", "cs"], ["qrf"])
                    TT_("dve", qrb[:, sl], qrf, t64, ALU.add, ["qrf", "t64"], ["qrb"])
            sbank = (0, 1, 5, 6)
            if smp:
                qjobs = [(tl * TL, TL, list(range(nkc))) for tl in range(ntl)]
            else:
                qjobs = [(s_ * Tsub, Tsub, list(range(s_ * Tsub // 128, (s_ + 1) * Tsub // 128))) for s_ in range(nsub)]
            for tl, (q0_, QW, kcs) in enumerate(qjobs):
                sl = slice(q0_, q0_ + QW)
                nkq = len(kcs)
                LAG = 2
                for it in range(nkq + LAG):
                    if it < nkq:
                        kc = kcs[it]; a = it % 4; bnk = sbank[a]
                        ksl = slice(kc * 128, (kc + 1) * 128)
                        MM(ps[bnk][:, 0:QW], kn[:, ksl], qn[:, sl], True, False, ["kn", "qn"], [psk(bnk)])
                        MM(ps[bnk][:, 0:QW], krR[:, ksl], qrb[:, sl], False, True, ["krR", "qrb"], [psk(bnk)])
                        ACTF(pT[a][:, 0:QW], ps[bnk][:, 0:QW], AF.Exp, [psk(bnk)], [("pT", a)], scale=SCALE)
                    j_ = it - LAG
                    if j_ >= 0:
                        kc = kcs[j_]; a = j_ % 4
                        MM(ps[2][:, 0:QW], Vh[:, kc, :], pT[a][:, 0:QW], j_ == 0, j_ == nkq - 1, ["Vh", ("pT", a)], [psk(2)])
                        if j_ == 0:
                            CP("dve", dacc[:, 0:QW], pT[a][:, 0:QW], [("pT", a)], ["dacc"])
                        else:
                            TT_("dve", dacc[:, 0:QW], dacc[:, 0:QW], pT[a][:, 0:QW], ALU.add, ["dacc", ("pT", a)], ["dacc"])
                CP("dve", daccb[:, 0:QW], dacc[:, 0:QW], ["dacc"], ["daccb"])
                MM(ps[3][:, 0:QW], ones_bf[:], daccb[:, 0:QW], True, True, ["ones_bf", "daccb"], [psk(3)])
                ob = tl % 2
                P.op("dve", lambda e, o_=rden[:, 0:QW], i_=ps[3][:, 0:QW]: e.reciprocal(o_, i_), [psk(3)], ["rden"])
                TT_("dve", ost[ob][:, 0:QW], ps[2][:, 0:QW], rden[:, 0:QW], ALU.mult, [psk(2), "rden"], [("ost", ob)])
                DMA(scr_m[1024 + 128 * h:1024 + 128 * (h + 1), t0 + q0_:t0 + q0_ + QW], ost[ob][:, 0:QW], [("ost", ob)], ["scr_m"])
                if h == 0:
                    ACTF(ssacc[:, sl], ost[ob][:, 0:QW], AF.Square, [("ost", ob)], ["ssacc"])
                else:
                    ACTF(sqt[:, 0:QW], ost[ob][:, 0:QW], AF.Square, [("ost", ob)], ["sqt"])
                    TT_("pool", ssacc[:, sl], ssacc[:, sl], sqt[:, 0:QW], ALU.add, ["ssacc", "sqt"], ["ssacc"])
        for tl in range(ntl):
            sl = slice(tl * TL, (tl + 1) * TL)
            CP("act", ssb, ssacc[:, sl], ["ssacc"], ["ssb"])
            MM(ps[4][:, 0:TL], ones_bf[:], ssb, True, True, ["ssb", "ones_bf"], [psk(4)])
            RSTD(rstd, ps[4][:, 0:TL], [psk(4)], ["rstd"], 1.0 / 1024, 1e-6)
            DMA(scr_r[1, :, t0 + tl * TL:t0 + (tl + 1) * TL], rstd, ["rstd"], ["scr_r"])
        P.barrier()


def kmix_s5(g, l):
    P, AR, ps, psk, W, I, O, C, VB, VBI = g["P"], g["AR"], g["ps"], g["psk"], g["W"], g["I"], g["O"], g["C"], g["VB"], g["VBI"]
    MM, TRN, ACTF, TT_, TS_, STT, CP, DMA, MSET, RSTD = (g[k] for k in ("MM", "TRN", "ACTF", "TT_", "TS_", "STT", "CP", "DMA", "MSET", "RSTD"))
    ident, ones_bf, sel2, mask4, mask8, twopi = g["ident"], g["ones_bf"], g["sel2"], g["mask4"], g["mask8"], g["twopi"]
    scr_p, scr_m, scr_r = g["scr_p"], g["scr_m"], g["scr_r"]
    AR.reset()
    NLV = 1
    Bre = AR.get([128, 32, 128], BF16); Bim = AR.get([128, 32, 128], BF16)
    Cre = AR.get([128, 32, 128], BF16); Cimn = AR.get([128, 32, 128], BF16)
    pw = AR.get([128, NLV, 3, 32])
    wglu = AR.get([128, 4, 1024], BF16)
    th1 = AR.get([128, 32]); th64 = AR.get([128, 32]); rho = AR.get([128, 32])
    KI = AR.get([128, 32]); JI = AR.get([128, 64])
    DMA(KI, C["ki_tab"], (), ["KI"]); DMA(JI, C["ji_tab"], (), ["JI"])
    rst256 = AR.get([128, 1024])
    DMA(rst256, C["rst256"], (), ["rst256"])
    mark0 = AR.off
    are = AR.get([128, 32]); aim = AR.get([128, 32]); x2 = AR.get([2, 32]); dtt = AR.get([128, 32])
    ar = AR.get([128, 32]); ai = AR.get([128, 32]); mag = AR.get([128, 32]); sn = AR.get([128, 32]); csn = AR.get([128, 32])
    t1 = AR.get([128, 32]); t2 = AR.get([128, 32]); t3 = AR.get([128, 32]); cr = AR.get([128, 32]); ci = AR.get([128, 32])
    braw = AR.get([128, 32, 16]); biraw = AR.get([128, 32, 16]); bbr = AR.get([128, 32, 16]); bbi = AR.get([128, 32, 16]); tb = AR.get([128, 32, 16])
    msr = AR.get([128, 32, 2, 16]); msi = AR.get([128, 32, 2, 16])
    craw = AR.get([128, 8, 64]); ciraw = AR.get([128, 8, 64]); tc = [AR.get([128, 2, 64]) for _ in range(2)]
    DMA(are, W["s5_a_re"][l].rearrange("d (gp g2) p -> (g2 p) (d gp)", g2=2), (), ["are"], slow=True)
    DMA(aim, W["s5_a_im"][l].rearrange("d (gp g2) p -> (g2 p) (d gp)", g2=2), (), ["aim"], slow=True)
    DMA(x2, W["s5_log_dt"][l].rearrange("d (gp g2) -> g2 (d gp)", g2=2), (), ["x2"], slow=True)
    MM(ps[0][:, 0:32], sel2[:], x2, True, True, ["sel2", "x2"], [psk(0)])
    ACTF(dtt, ps[0][:, 0:32], AF.Exp, [psk(0)], ["dtt"])
    TT_("dve", ar, are, dtt, ALU.mult, ["are", "dtt"], ["ar"])
    TT_("dve", ai, aim, dtt, ALU.mult, ["aim", "dtt"], ["ai"])
    ACTF(mag, ar, AF.Exp, ["ar"], ["mag"])
    import concourse.mybir as mybir
    ki = AR.get([128, 32]).bitcast(mybir.dt.int32)
    s2 = AR.get([128, 32]); s4 = AR.get([128, 32])
    TS_("dve", t1, ai, 1.0 / (2 * math.pi), None, ALU.mult, None, ["ai"], ["t1"])
    CP("dve", ki, t1, ["t1"], ["ki"])
    CP("dve", t1, ki, ["ki"], ["t1"])
    STT(t1, t1, -2 * math.pi, ai, ALU.mult, ALU.add, ["t1", "ai"], ["t1"])
    ACTF(s2, t1, AF.Sin, ["t1"], ["s2"], scale=0.5)
    ACTF(s4, t1, AF.Sin, ["t1"], ["s4"], scale=0.25)
    TT_("dve", t2, s2, s2, ALU.mult, ["s2"], ["t2"])
    TS_("dve", csn, t2, -2.0, 1.0, ALU.mult, ALU.add, ["t2"], ["csn"])
    TT_("dve", t3, s4, s4, ALU.mult, ["s4"], ["t3"])
    TS_("dve", t3, t3, -2.0, 1.0, ALU.mult, ALU.add, ["t3"], ["t3"])
    TT_("dve", sn, s2, t3, ALU.mult, ["s2", "t3"], ["sn"])
    TS_("dve", sn, sn, 2.0, None, ALU.mult, None, ["sn"], ["sn"])
    I32 = mybir.dt.int32
    TS_("dve", t1, ai, 1.0 / (2 * math.pi), None, ALU.mult, None, ["ai"], ["t1"])
    CP("dve", ki, t1, ["t1"], ["ki"])
    CP("dve", t1, ki, ["ki"], ["t1"])
    STT(th1, t1, -2 * math.pi, ai, ALU.mult, ALU.add, ["t1", "ai"], ["th1"])
    TS_("dve", t1, th1, 64.0 / (2 * math.pi), None, ALU.mult, None, ["th1"], ["t1"])
    CP("dve", ki, t1, ["t1"], ["ki"])
    CP("dve", t1, ki, ["ki"], ["t1"])
    TS_("dve", th64, th1, 64.0, None, ALU.mult, None, ["th1"], ["th64"])
    STT(th64, t1, -2 * math.pi, th64, ALU.mult, ALU.add, ["t1", "th64"], ["th64"])
    CP("dve", rho, mag, ["mag"], ["rho"])
    lr = pw[:, 0, 0, :]; li = pw[:, 0, 1, :]
    TT_("dve", lr, mag, csn, ALU.mult, ["mag", "csn"], ["pw"])
    TT_("dve", li, mag, sn, ALU.mult, ["mag", "sn"], ["pw"])
    TS_("dve", pw[:, 0, 2, :], li, -1.0, None, ALU.mult, None, ["pw"], ["pw"])
    for k in range(1, 1):
        r_, i_ = pw[:, k - 1, 0, :], pw[:, k - 1, 1, :]
        TT_("dve", t1, r_, r_, ALU.mult, ["pw"], ["t1"])
        TT_("dve", t2, i_, i_, ALU.mult, ["pw"], ["t2"])
        TT_("dve", pw[:, k, 0, :], t1, t2, ALU.subtract, ["t1", "t2"], ["pw"])
        TT_("dve", t3, r_, i_, ALU.mult, ["pw"], ["t3"])
        TS_("dve", pw[:, k, 1, :], t3, 2.0, None, ALU.mult, None, ["t3"], ["pw"])
        TS_("dve", pw[:, k, 2, :], t3, -2.0, None, ALU.mult, None, ["t3"], ["pw"])
    TS_("dve", t1, lr, -1.0, None, ALU.add, None, ["pw"], ["t1"])
    TT_("dve", t2, are, are, ALU.mult, ["are"], ["t2"])
    TT_("dve", t3, aim, aim, ALU.mult, ["aim"], ["t3"])
    TT_("dve", t2, t2, t3, ALU.add, ["t2", "t3"], ["t2"])
    P.op("dve", lambda e: e.reciprocal(t2, t2), ["t2"], ["t2"])
    TT_("dve", cr, t1, are, ALU.mult, ["t1", "are"], ["cr"])
    TT_("dve", t3, li, aim, ALU.mult, ["pw", "aim"], ["t3"])
    TT_("dve", cr, cr, t3, ALU.add, ["cr", "t3"], ["cr"])
    TT_("dve", cr, cr, t2, ALU.mult, ["cr", "t2"], ["cr"])
    TT_("dve", ci, li, are, ALU.mult, ["pw", "are"], ["ci"])
    TT_("dve", t3, t1, aim, ALU.mult, ["t1", "aim"], ["t3"])
    TT_("dve", ci, ci, t3, ALU.subtract, ["ci", "t3"], ["ci"])
    TT_("dve", ci, ci, t2, ALU.mult, ["ci", "t2"], ["ci"])
    DMA(braw, W["s5_b_re"][l].rearrange("d (gp g2) p c -> (g2 p) (d gp) c", g2=2), (), ["braw"])
    DMA(biraw, W["s5_b_im"][l].rearrange("d (gp g2) p c -> (g2 p) (d gp) c", g2=2), (), ["biraw"])
    crb = cr.unsqueeze(2).broadcast_to([128, 32, 16]); cib = ci.unsqueeze(2).broadcast_to([128, 32, 16])
    TT_("dve", bbr, braw, crb, ALU.mult, ["braw", "cr"], ["bbr"])
    TT_("dve", tb, biraw, cib, ALU.mult, ["biraw", "ci"], ["tb"])
    TT_("dve", bbr, bbr, tb, ALU.subtract, ["bbr", "tb"], ["bbr"])
    TT_("dve", bbi, biraw, crb, ALU.mult, ["biraw", "cr"], ["bbi"])
    TT_("dve", tb, braw, cib, ALU.mult, ["braw", "ci"], ["tb"])
    TT_("dve", bbi, bbi, tb, ALU.add, ["bbi", "tb"], ["bbi"])
    for (src, ms, big, nm) in ((bbr, msr, Bre, "Bre"), (bbi, msi, Bim, "Bim")):
        MSET("pool", ms, 0.0, ["ms" + nm])
        CP("dve", ms[0:64, :, 0, :], src[0:64], ["bbr", "bbi", "ms" + nm], ["ms" + nm])
        CP("dve", ms[64:128, :, 1, :], src[64:128], ["bbr", "bbi", "ms" + nm], ["ms" + nm])
        for d in range(2):
            for c in range(4):
                u0 = d * 16 + 4 * c
                TRN(ps[1][:, 0:128], ms[:, u0:u0 + 4, :, :].rearrange("p a b c -> p (a b c)"), ident[:], ["ms" + nm, "ident"], [psk(1)])
                for j in range(4):
                    TS_("dve", big[:, u0 + j, :], ps[1][:, 0:128], mask4[:, j:j + 1], None, ALU.mult, None, [psk(1), "mask4"], [nm])
    DMA(craw, W["s5_c_re"][l].rearrange("d (c gi) ch p -> (gi ch) (d c) p", gi=8), (), ["craw"])
    DMA(ciraw, W["s5_c_im"][l].rearrange("d (c gi) ch p -> (gi ch) (d c) p", gi=8), (), ["ciraw"])
    n = 0
    for (src, big, nm, sgn) in ((craw, Cre, "Cre", 1.0), (ciraw, Cimn, "Cimn", -1.0)):
        for d in range(2):
            for c in range(4):
                for j in range(4):
                    b = n % 2
                    n += 1
                    for g2 in range(2):
                        TS_("dve", tc[b][:, g2, :], src[:, d * 4 + c, :], mask8[:, 2 * j + g2:2 * j + g2 + 1], None, ALU.mult, None,
                            ["craw", "ciraw", "mask8"], [("tc", b)])
                    TRN(ps[2 + b][:, 0:128], tc[b].rearrange("p a b -> p (a b)"), ident[:], [("tc", b), "ident"], [psk(2 + b)])
                    TS_("dve", big[:, d * 16 + 4 * c + j, :], ps[2 + b][:, 0:128], sgn, None, ALU.mult, None, [psk(2 + b)], [nm])
    wg_src = W["s5_w_glu"][l].rearrange("(c p) n -> p c n", p=128)
    _load_bf(g, wglu, wg_src, (4, 1024), "wglu")
    P.barrier()
    dsk0 = VBI["s5_d"]
    cfg_ = g["cfg"]
    for (t0, T, smp, nsub, Tsub) in ((0, cfg_.TS, True, 1, cfg_.TS), (cfg_.TS, cfg_.NP * cfg_.TP, False, cfg_.NP, cfg_.TP)):
        AR.off = mark0
        nlev = int(round(math.log2(T)))
        TL = min(512, T); ntl = T // TL
        uc = AR.get([128, T]); ub = AR.get([128, T], BF16)
        hbr = AR.get([128, T], BF16); hbi = AR.get([128, T], BF16)
        ygb = AR.get([128, 4, T], BF16); yt = AR.get([128, TL]); yt2 = AR.get([128, TL])
        fin = AR.get([128, nsub, 32, 2]); h0 = AR.get([128, 32, 2]); addr = AR.get([128, 32]); addi = AR.get([128, 32]); ta = AR.get([128, 32])
        AJ = AR.get([128, 64]); et = AR.get([128, 4])
        markB = AR.off
        B1 = AR.get([128, T]); B2 = AR.get([128, T]); B3 = AR.get([128, T]); B4 = AR.get([128, T]); B5 = AR.get([128, T]); B6 = AR.get([128, T])
        B4i = B4.bitcast(mybir.dt.int32)
        v3 = lambda ap_: ap_.rearrange("p (n t) -> p n t", t=64)
        dv = lambda ap_: ap_.rearrange("p (s t) -> p s t", s=nsub)
        tb = lambda ap_: ap_[:, 0:Tsub].unsqueeze(1).broadcast_to([128, nsub, Tsub])
        rt = AR.get([128, T]) if nsub > 1 else None
        if smp:
            DMA(h0, I["st_s5"][l].rearrange("d (gp g2) p r -> (g2 p) (d gp) r", g2=2), (), ["h0"])
            lr = pw[:, 0, 0, :]; li = pw[:, 0, 1, :]
            TT_("dve", addr, lr, h0[:, :, 0], ALU.mult, ["pw", "h0"], ["addr"])
            TT_("dve", ta, li, h0[:, :, 1], ALU.mult, ["pw", "h0"], ["ta"])
            TT_("dve", addr, addr, ta, ALU.subtract, ["addr", "ta"], ["addr"])
            TT_("dve", addi, lr, h0[:, :, 1], ALU.mult, ["pw", "h0"], ["addi"])
            TT_("dve", ta, li, h0[:, :, 0], ALU.mult, ["pw", "h0"], ["ta"])
            TT_("dve", addi, addi, ta, ALU.add, ["addi", "ta"], ["addi"])
        for c in range(4):
            DMA(uc, scr_p[c * 128:(c + 1) * 128, t0:t0 + T], ["scr_p"], ["uc"])
            CP("act", ub, uc, ["uc"], ["ub"])
            first = True
            for d in range(2):
                for j in range(4):
                    u = d * 16 + 4 * c + j
                    HV = [(0, T // 2), (T // 2, T)] if nsub == 1 else [(0, T)]
                    hof = lambda col: 0 if len(HV) == 1 else (0 if col < T // 2 else 1)
                    D_ = lambda x, hf: dv(x) if nsub > 1 else x[:, hf[0]:hf[1]]
                    TB_ = lambda x, hf: tb(x) if nsub > 1 else x[:, hf[0]:hf[1]]
                    F_ = lambda x, hf: x if nsub > 1 else x[:, hf[0]:hf[1]]
                    for tl in range(ntl):
                        sl = slice(tl * TL, (tl + 1) * TL)
                        hxs = sorted({hof(tl * TL), hof((tl + 1) * TL - 1)})
                        MM(ps[4][:, 0:TL], Bre[:, u, :], ub[:, sl], True, True, ["Bre", "ub"], [psk(4)])
                        CP("act", B1[:, sl], ps[4][:, 0:TL], [psk(4)], [("B1", hx) for hx in hxs])
                        MM(ps[5][:, 0:TL], Bim[:, u, :], ub[:, sl], True, True, ["Bim", "ub"], [psk(5)])
                        CP("act", B2[:, sl], ps[5][:, 0:TL], [psk(5)], [("B2", hx) for hx in hxs])
                    e_ = 0 if d == 0 else T - 1
                    if smp:
                        hx = hof(e_)
                        TT_("pool", B1[:, e_:e_ + 1], B1[:, e_:e_ + 1], addr[:, u:u + 1], ALU.add, [("B1", hx), "addr"], [("B1", hx)])
                        TT_("pool", B2[:, e_:e_ + 1], B2[:, e_:e_ + 1], addi[:, u:u + 1], ALU.add, [("B2", hx), "addi"], [("B2", hx)])
                    TS_("pool", AJ, JI, th1[:, u:u + 1], None, ALU.mult, None, ["JI", "th1"], ["AJ"])
                    THV = HV if nsub == 1 else [(0, Tsub)]
                    for hi, (a_, b_) in enumerate(THV):
                        ts_ = slice(a_, b_)
                        nk_ = (b_ - a_) // 64
                        k3, k4_, k5 = ("B3", hi), ("B4", hi), ("B5", hi)
                        STT(v3(B3[:, ts_]), KI[:, a_ // 64:b_ // 64].unsqueeze(2).broadcast_to([128, nk_, 64]), th64[:, u:u + 1],
                            AJ.unsqueeze(1).broadcast_to([128, nk_, 64]), ALU.mult, ALU.add, ["KI", "th64", "AJ"], [k3])
                        ACTF(B4[:, ts_], B3[:, ts_], AF.Copy, [k3], [k4_], scale=1.0 / (2 * math.pi))
                        CP("dve", B4i[:, ts_], B4[:, ts_], [k4_], [k4_])
                        ACTF(B4[:, ts_], B4i[:, ts_], AF.Copy, [k4_], [k4_])
                        STT(B3[:, ts_], B4[:, ts_], -2 * math.pi, B3[:, ts_], ALU.mult, ALU.add, [k4_, k3], [k3])
                        ACTF(B4[:, ts_], B3[:, ts_], AF.Sin, [k3], [k4_])
                        ACTF(B5[:, ts_], B3[:, ts_], AF.Sin, [k3], [k5], scale=0.5)
                        ACTF(B5[:, ts_], B5[:, ts_], AF.Square, [k5], [k5])
                        ACTF(B5[:, ts_], B5[:, ts_], AF.Identity, [k5], [k5], scale=-2.0, bias=1.0)
                    op_a = ALU.add if d == 0 else ALU.subtract
                    op_b = ALU.subtract if d == 0 else ALU.add
                    for hi, hf in enumerate(HV):
                        k1, k2, k3, k4_, k5, k6 = (("B%d" % n_, hi) for n_ in range(1, 7))
                        TT_("dve", D_(B3, hf), TB_(B5, hf), D_(B1, hf), ALU.mult, [k5, k1], [k3])
                        TT_("pool", D_(B6, hf), TB_(B4, hf), D_(B2, hf), ALU.mult, [k4_, k2], [k6])
                        TT_("dve", F_(B3, hf), F_(B3, hf), F_(B6, hf), op_a, [k3, k6], [k3])
                        TT_("dve", D_(B6, hf), TB_(B5, hf), D_(B2, hf), ALU.mult, [k5, k2, k3], [k6])
                        TT_("pool", D_(B1, hf), TB_(B4, hf), D_(B1, hf), ALU.mult, [k4_, k1], [k1])
                        TT_("dve", F_(B6, hf), F_(B6, hf), F_(B1, hf), op_b, [k6, k1], [k6])
                    if nsub > 1:
                        TS_("dve", rt, rst256[:, 0:T], rho[:, u:u + 1], None, ALU.mult, None, ["rst256", "rho"], ["rho_t"])
                    order = list(range(len(HV))) if d == 0 else list(range(len(HV) - 1, -1, -1))
                    for oi, hi in enumerate(order):
                        a_, b_ = HV[hi]
                        rb = rt if nsub > 1 else rho[:, u:u + 1].broadcast_to([128, b_ - a_])
                        for (Bo, Bc, no_, nc_) in ((B1, B3, 1, 3), (B2, B6, 2, 6)):
                            ko = ("B%d" % no_, hi); kc_ = ("B%d" % nc_, hi)
                            if oi == 0:
                                ini = 0.0; extra = []
                            else:
                                ph = order[oi - 1]
                                pcol = HV[ph][1] - 1 if d == 0 else HV[ph][0]
                                ini = Bo[:, pcol:pcol + 1]; extra = [("B%d" % no_, ph)]
                            if d == 0:
                                o_ap, c_ap = Bo[:, a_:b_], Bc[:, a_:b_]
                            else:
                                o_ap, c_ap = Bo[:, a_:b_][:, ::-1], Bc[:, a_:b_][:, ::-1]
                            P.op("dve", lambda e, o_=o_ap, r_=rb, c_=c_ap, i_=ini: e.tensor_tensor_scan(o_, r_, c_, i_, ALU.mult, ALU.add),
                                 [kc_, "rho", "rho_t", ko] + extra, [ko])
                    for hi, hf in enumerate(HV):
                        k1, k2, k3, k4_, k5, k6 = (("B%d" % n_, hi) for n_ in range(1, 7))
                        kr_, ki2 = ("hbr", hi), ("hbi", hi)
                        TT_("dve", D_(B3, hf), TB_(B5, hf), D_(B1, hf), ALU.mult, [k5, k1], [k3])
                        TT_("pool", D_(B6, hf), TB_(B4, hf), D_(B2, hf), ALU.mult, [k4_, k2], [k6])
                        TT_("dve", F_(hbr, hf), F_(B3, hf), F_(B6, hf), op_b, [k3, k6], [kr_])
                        if not smp:
                            f_ = Tsub - 1 if d == 0 else 0
                            TT_("dve", fin[:, :, u, 0], dv(B3)[:, :, f_], dv(B6)[:, :, f_], op_b, [k3, k6], ["fin"])
                        TT_("pool", D_(B6, hf), TB_(B4, hf), D_(B1, hf), ALU.mult, [k4_, k1, kr_, "fin"], [k6])
                        TT_("dve", D_(B3, hf), TB_(B5, hf), D_(B2, hf), ALU.mult, [k5, k2, kr_, "fin"], [k3])
                        TT_("dve", F_(hbi, hf), F_(B3, hf), F_(B6, hf), op_a, [k3, k6], [ki2])
                        if not smp:
                            TT_("dve", fin[:, :, u, 1], dv(B3)[:, :, f_], dv(B6)[:, :, f_], op_a, [k3, k6], ["fin"])
                    last = (d == 1 and j == 3)
                    for tl in range(ntl):
                        sl = slice(tl * TL, (tl + 1) * TL)
                        hxs = sorted({hof(tl * TL), hof((tl + 1) * TL - 1)})
                        MM(ps[tl][:, 0:TL], Cre[:, u, :], hbr[:, sl], first, False, ["Cre"] + [("hbr", hx) for hx in hxs], [psk(tl)])
                        MM(ps[tl][:, 0:TL], Cimn[:, u, :], hbi[:, sl], False, last, ["Cimn"] + [("hbi", hx) for hx in hxs], [psk(tl)])
                    first = False
            for tl in range(ntl):
                sl = slice(tl * TL, (tl + 1) * TL)
                STT(yt, uc[:, sl], VB[:, dsk0 + c:dsk0 + c + 1], ps[tl][:, 0:TL], ALU.mult, ALU.add, ["uc", "VB", psk(tl)], ["yt"])
                TT_("pool", yt2, yt, yt, ALU.mult, ["yt"], ["yt2"])
                TS_("dve", yt2, yt2, 0.044715, 1.0, ALU.mult, ALU.add, ["yt2"], ["yt2"])
                TT_("dve", yt2, yt2, yt, ALU.mult, ["yt2", "yt"], ["yt2"])
                ACTF(yt2, yt2, AF.Sigmoid, ["yt2"], ["yt2"], scale=1.5957691216)
                TT_("dve", ygb[:, c, sl], yt, yt2, ALU.mult, ["yt", "yt2"], ["ygb"])
        P.barrier()
        AR.off = markB
        ssacc = AR.get([128, T]); sgt = AR.get([128, TL]); yo = [AR.get([128, TL]) for _ in range(2)]
        ssb = AR.get([128, TL], BF16); rstd = AR.get([128, TL])
        n = 0
        for m in range(4):
            for tl in range(ntl):
                sl = slice(tl * TL, (tl + 1) * TL)
                b = n % 2
                n += 1
                for kc in range(4):
                    MM(ps[4 + b][:, 0:TL], wglu[:, kc, m * 128:(m + 1) * 128], ygb[:, kc, sl], kc == 0, kc == 3, ["wglu", "ygb"], [psk(4 + b)])
                for kc in range(4):
                    MM(ps[6 + b][:, 0:TL], wglu[:, kc, 512 + m * 128:512 + (m + 1) * 128], ygb[:, kc, sl], kc == 0, kc == 3, ["wglu", "ygb"], [psk(6 + b)])
                ACTF(sgt, ps[6 + b][:, 0:TL], AF.Sigmoid, [psk(6 + b)], ["sgt"])
                TT_("dve", yo[b], ps[4 + b][:, 0:TL], sgt, ALU.mult, [psk(4 + b), "sgt"], [("yo", b)])
                DMA(scr_m[m * 128:(m + 1) * 128, t0 + tl * TL:t0 + (tl + 1) * TL], yo[b], [("yo", b)], ["scr_m"])
                if m == 0:
                    ACTF(ssacc[:, sl], yo[b], AF.Square, [("yo", b)], ["ssacc"])
                else:
                    ACTF(sgt, yo[b], AF.Square, [("yo", b)], ["sgt"])
                    TT_("pool", ssacc[:, sl], ssacc[:, sl], sgt, ALU.add, ["ssacc", "sgt"], ["ssacc"])
        for tl in range(ntl):
            sl = slice(tl * TL, (tl + 1) * TL)
            CP("act", ssb, ssacc[:, sl], ["ssacc"], ["ssb"])
            MM(ps[4][:, 0:TL], ones_bf[:], ssb, True, True, ["ssb", "ones_bf"], [psk(4)])
            RSTD(rstd, ps[4][:, 0:TL], [psk(4)], ["rstd"], 1.0 / 512, 1e-6)
            DMA(scr_r[0, :, t0 + tl * TL:t0 + (tl + 1) * TL], rstd, ["rstd"], ["scr_r"])
        if not smp:
            for s_ in range(nsub):
                DMA(O["o_s5"][s_, l].rearrange("d (gp g2) p r -> (g2 p) (d gp) r", g2=2), fin[:, s_], ["fin"], ["o_s5"])
        P.barrier()


import math
import numpy as np


def krwkv_rwkv(g, l):
    P, AR, ps, psk, W, I, O, C, VB, VBI = g["P"], g["AR"], g["ps"], g["psk"], g["W"], g["I"], g["O"], g["C"], g["VB"], g["VBI"]
    MM, TRN, ACTF, TT_, TS_, STT, CP, DMA, MSET, RSTD = (g[k] for k in ("MM", "TRN", "ACTF", "TT_", "TS_", "STT", "CP", "DMA", "MSET", "RSTD"))
    ident, identb, blkf, blk_bf, msk, reset, omka = g["ident"], g["identb"], g["blkf"], g["blk_bf"], g["msk"], g["reset"], g["omka"]
    scr_p, scr_m = g["scr_p"], g["scr_m"]
    psb2 = ps[2][:].bitcast(BF16); psb3 = ps[3][:].bitcast(BF16); psb6 = ps[6][:].bitcast(BF16)
    AR.reset()
    w2b = AR.get([64, 2, 512], BF16); a2b = AR.get([64, 2, 512], BF16); g2b = AR.get([128, 512], BF16)
    stw = AR.get([64, 2, 512])
    DMA(stw, W["rwkv_w2"][l].rearrange("d m n -> m d n"), (), ["stw"])
    CP("dve", w2b, stw, ["stw"], ["w2b"])
    DMA(stw, W["rwkv_a2"][l].rearrange("d m n -> m d n"), ["stw"], ["stw"])
    CP("dve", a2b, stw, ["stw"], ["a2b"])
    stg = AR.get([128, 512])
    DMA(stg, W["rwkv_g2"][l], (), ["stg"])
    CP("dve", g2b, stg, ["stg"], ["g2b"])
    m_ib = [AR.get([128, 128], BF16) for _ in range(2)]
    CP("dve", m_ib[0], msk["m_iu"][:], ["m_iu"], ["m_b"])
    CP("dve", m_ib[1], msk["m_il"][:], ["m_il"], ["m_b"])
    mark0 = AR.off
    kk0, ka0, rk0, lw0, lb0 = VBI["k_k"], VBI["k_a"], VBI["r_k"], VBI["ln_w"], VBI["ln_b"]
    v3_ = lambda ap_: ap_.rearrange("p (n t) -> p n t", t=64)
    pq = lambda bank, q: ps[bank][:, q * 128:(q + 1) * 128]

    def capture(fn):
        saved = P.ops
        P.ops = []
        fn()
        out = P.ops
        P.ops = saved
        return out

    cfg_ = g["cfg"]
    assert cfg_.TP == 256
    for (t0, T, smp) in ((0, cfg_.TS, True), (cfg_.TS, cfg_.NP * cfg_.TP, False)):
        AR.off = mark0
        SEG = min(256, T); nseg = T // SEG; NCH = SEG // 64; NCHT = T // 64
        TL = SEG; ntl = nseg
        twb = AR.get([64, T], BF16); alb = AR.get([64, T], BF16); sgb = AR.get([128, T], BF16)
        rp = AR.get([128, T]); kp = AR.get([128, T]); vp = AR.get([128, T]); kk = AR.get([128, T])
        vbd = AR.get([128, NCHT, 2, 64], BF16); Vtok = AR.get([128, NCHT, 128], BF16)
        y0 = AR.get([128, T]); kdsum = AR.get([128, T]); lst = y0
        sgm = AR.get([128, SEG]); a_ = AR.get([128, SEG]); kd = AR.get([128, SEG]); b_ = AR.get([128, SEG])
        Lc = AR.get([128, SEG]); Lx = AR.get([128, SEG]); E1 = AR.get([128, SEG]); E2 = AR.get([128, SEG]); E3 = AR.get([128, SEG])
        tmp = E3; sqb = AR.get([128, SEG], BF16)
        AR4p = [AR.get([128, NCH, 6, 128], BF16) for _ in range(2)]
        btokp = [AR.get([128, NCH, 128], BF16) for _ in range(2)]; ktokp = [AR.get([128, NCH, 128], BF16) for _ in range(2)]
        atokp = [AR.get([128, NCH, 128], BF16) for _ in range(2)]; dcolp = [AR.get([128, NCH]) for _ in range(2)]
        LUt = [AR.get([128, 128], BF16) for _ in range(4)]; Wvb = [AR.get([128, 128], BF16) for _ in range(4)]
        Z32 = AR.get([128, 128]); Zb = AR.get([128, 128], BF16); ztmp = AR.get([128, 128]); zsrc = AR.get([128, 2, 64])
        AabT = [AR.get([128, 128], BF16) for _ in range(4)]; ArbT = [AR.get([128, 128], BF16) for _ in range(4)]
        AakT = [AR.get([128, 128], BF16) for _ in range(4)]; ArkT = [AR.get([128, 128], BF16) for _ in range(4)]
        Xa = [[AR.get([128, 128], BF16) for _ in range(2)] for _ in range(4)]; XTa = [[AR.get([128, 128], BF16) for _ in range(2)] for _ in range(4)]
        PT = [[AR.get([128, 128], BF16) for _ in range(2)] for _ in range(4)]
        Ub = AR.get([128, 128], BF16)
        yc = sgm; yn = a_; rstd = kd; bi = b_; yo = [Lc, Lx]
        DMA(lst[0:64, :], scr_p[2048:2112, t0:t0 + T], ["scr_p"], ["y0"])
        ACTF(twb, lst[0:64, :], AF.Tanh, ["y0"], ["twb"])
        DMA(lst[0:64, :], scr_p[2112:2176, t0:t0 + T], ["scr_p", "y0"], ["y0"])
        CP("dve", alb, lst[0:64, :], ["y0"], ["alb"])
        DMA(lst, scr_p[2176:2304, t0:t0 + T], ["scr_p", "y0"], ["y0"])
        ACTF(sgb, lst, AF.Sigmoid, ["y0"], ["sgb"])
        for par in range(2):
            MSET("pool", AR4p[par], 0.0, [("AR4", par)])
        MSET("pool", vbd, 0.0, ["vbd"])
        MSET("pool", zsrc, 0.0, ["zsrc"])
        for hp in range(4):
            DMA(rp, scr_p[512 + hp * 128:512 + (hp + 1) * 128, t0:t0 + T], ["scr_p"], ["rp"])
            DMA(kp, scr_p[1024 + hp * 128:1024 + (hp + 1) * 128, t0:t0 + T], ["scr_p"], ["kp"])
            DMA(vp, scr_p[1536 + hp * 128:1536 + (hp + 1) * 128, t0:t0 + T], ["scr_p"], ["vp"])
            TS_("dve", kk, kp, VB[:, kk0 + hp:kk0 + hp + 1], None, ALU.mult, None, ["kp", "VB"], ["kk"])
            for tl in range(ntl):
                sl = slice(tl * TL, (tl + 1) * TL)
                ACTF(sqb, kk[:, sl], AF.Square, ["kk"], ["sqb"])
                MM(ps[4][:, 0:TL], blk_bf[:], sqb, True, True, ["sqb", "blk_bf"], [psk(4)])
                RSTD(rstd, ps[4][:, 0:TL], [psk(4)], ["rstd"], 1.0, 1e-12)
                TT_("dve", kk[:, sl], kk[:, sl], rstd, ALU.mult, ["kk", "rstd"], ["kk"])
            v3 = vp.rearrange("p (n t) -> p n t", t=64)
            CP("dve", vbd[0:64, :, 0, :], v3[0:64], ["vp", "vbd"], ["vbd"])
            CP("pool", vbd[64:128, :, 1, :], v3[64:128], ["vp", "vbd"], ["vbd"])
            for q0 in range(0, NCHT, 4):
                for qq in range(4):
                    TRN(psb6[:, qq * 128:(qq + 1) * 128], vbd[:, q0 + qq].rearrange("p a b -> p (a b)"), identb[:], ["vbd", "identb"], [psk(6)])
                CP("act", Vtok[:, q0:q0 + 4, :], psb6[:, 0:512].rearrange("p (a b) -> p a b", a=4), [psk(6)], ["Vtok"])
            for d in range(2):
                w0c = VBI["w0_%d" % d] + hp; a0c = VBI["a0_%d" % d] + hp
                m_s = (msk["m_su"] if d == 0 else msk["m_sl"])[:]
                m_ts = (msk["m_sl"] if d == 0 else msk["m_su"])[:]
                mib = m_ib[d]
                if smp:
                    DMA(zsrc[0:64, 0, :], I["st_rw"][l, d, 2 * hp], ["zsrc"], ["zsrc"])
                    DMA(zsrc[64:128, 1, :], I["st_rw"][l, d, 2 * hp + 1], ["zsrc"], ["zsrc"])
                    TRN(ps[1][:, 0:128], zsrc.rearrange("p a b -> p (a b)"), ident[:], ["zsrc", "ident"], [psk(1)])
                    CP("dve", Z32, ps[1][:, 0:128], [psk(1)], ["Z32"])
                    CP("dve", Zb, Z32, ["Z32"], ["Zb"])


                def prep(seg, par):
                    AR4 = AR4p[par]; btok = btokp[par]; ktok = ktokp[par]; atok = atokp[par]; dcol = dcolp[par]
                    k4, kb, kk_, ka_, kd_ = ("AR4", par), ("btok", par), ("ktok", par), ("atok", par), ("dcol", par)
                    ss = slice(seg * SEG, (seg + 1) * SEG)
                    MM(ps[0][:, 0:SEG], w2b[:, d, hp * 128:(hp + 1) * 128], twb[:, ss], True, True, ["w2b", "twb"], [psk(0)])
                    ACTF(sgm, ps[0][:, 0:SEG], AF.Sigmoid, [psk(0), "VB"], ["sgm"], bias=VB[:, w0c:w0c + 1])
                    MM(ps[1][:, 0:SEG], a2b[:, d, hp * 128:(hp + 1) * 128], alb[:, ss], True, True, ["a2b", "alb"], [psk(1)])
                    ACTF(a_, ps[1][:, 0:SEG], AF.Sigmoid, [psk(1), "VB"], ["a_"], bias=VB[:, a0c:a0c + 1])
                    TS_("dve", sgm, sgm, -0.6065306597, None, ALU.mult, None, ["sgm"], ["sgm"])
                    TS_("dve", kd, a_, VB[:, ka0 + hp:ka0 + hp + 1], omka[:, hp:hp + 1], ALU.mult, ALU.add, ["a_", "VB", "omka"], ["kd"])
                    TT_("dve", kd, kd, kp[:, ss], ALU.mult, ["kd", "kp"], ["kd"])
                    if d == 0:
                        CP("pool", kdsum[:, ss], kd, ["kd"], ["kdsum"])
                    else:
                        TT_("pool", kdsum[:, ss], kdsum[:, ss], kd, ALU.add, ["kd", "kdsum"], ["kdsum"])
                    TT_("pool", b_, kk[:, ss], a_, ALU.mult, ["kk", "a_"], ["b_"])
                    P.op("dve", lambda e, o_=Lc, d0=reset[:, 0:SEG], d1=sgm: e.tensor_tensor_scan(o_, d0, d1, 0.0, ALU.mult, ALU.add),
                         ["reset", "sgm"], ["Lc"])
                    if d == 1:
                        TT_("dve", tmp, sgm, Lc, ALU.subtract, ["sgm", "Lc"], ["E3"])
                        tot = v3_(Lc)[:, :, 63:64].broadcast_to([128, NCH, 64])
                        TT_("dve", v3_(Lx), v3_(tmp), tot, ALU.add, ["E3", "Lc"], ["Lx"])
                        CP("dve", Lc, Lx, ["Lx"], ["Lc"])
                    TT_("dve", Lx, Lc, sgm, ALU.subtract, ["Lc", "sgm"], ["Lx"])
                    ACTF(E1, Lc, AF.Exp, ["Lc"], ["E1"])
                    ACTF(E2, Lx, AF.Exp, ["Lx"], ["E2"])
                    ACTF(E3, Lc, AF.Exp, ["Lc"], ["E3"], scale=-1.0)
                    for half, eng in ((0, "dve"), (1, "pool")):
                        hs = slice(half * 64, (half + 1) * 64)
                        cs_ = slice(half * 64, (half + 1) * 64)
                        TT_(eng, AR4[hs, :, 0, cs_], v3_(kk[:, ss])[hs], v3_(E2)[hs], ALU.mult, ["kk", "E2", k4], [k4])
                        TT_(eng, AR4[hs, :, 1, cs_], v3_(rp[:, ss])[hs], v3_(E1)[hs], ALU.mult, ["rp", "E1", k4], [k4])
                        TT_(eng, AR4[hs, :, 2, cs_], v3_(b_)[hs], v3_(E3)[hs], ALU.mult, ["b_", "E3", k4], [k4])
                        TT_(eng, AR4[hs, :, 3, cs_], v3_(kd)[hs], v3_(E3)[hs], ALU.mult, ["kd", "E3", k4], [k4])
                    TS_("pool", AR4[:, :, 0, :], AR4[:, :, 0, :], -1.0, None, ALU.mult, None, [k4], [k4])
                    e_ = 63 if d == 0 else 0
                    CP("dve", dcol, v3_(E1)[:, :, e_], ["E1"], [kd_])
                    dcb = dcol.unsqueeze(2).broadcast_to([128, NCH, 128])
                    TT_("dve", AR4[:, :, 4, :], AR4[:, :, 2, :], dcb, ALU.mult, [k4, kd_], [k4])
                    TT_("pool", AR4[:, :, 5, :], AR4[:, :, 3, :], dcb, ALU.mult, [k4, kd_], [k4])
                    for qq in range(NCH):
                        TRN(psb2[:, qq * 128:(qq + 1) * 128], AR4[:, qq, 4, :], identb[:], [k4, "identb"], [psk(2)])
                        TRN(psb3[:, qq * 128:(qq + 1) * 128], AR4[:, qq, 5, :], identb[:], [k4, "identb"], [psk(3)])
                    CP("act", btok, psb2[:, 0:NCH * 128].rearrange("p (a b) -> p a b", a=NCH), [psk(2)], [kb])
                    CP("dve", ktok, psb3[:, 0:NCH * 128].rearrange("p (a b) -> p a b", a=NCH), [psk(3)], [kk_])
                    for qq in range(NCH):
                        TRN(psb2[:, qq * 128:(qq + 1) * 128], AR4[:, qq, 0, :], identb[:], [k4, "identb"], [psk(2)])
                    CP("act", atok, psb2[:, 0:NCH * 128].rearrange("p (a b) -> p a b", a=NCH), [psk(2)], [ka_])

                segs = list(range(nseg) if d == 0 else range(nseg - 1, -1, -1))
                prep(segs[0], 0)
                for si, seg in enumerate(segs):
                    par = si % 2
                    AR4 = AR4p[par]; btok = btokp[par]; ktok = ktokp[par]; atok = atokp[par]; dcol = dcolp[par]
                    k4, kb, kk_, ka_, kd_ = ("AR4", par), ("btok", par), ("ktok", par), ("atok", par), ("dcol", par)
                    nxt = capture(lambda: prep(segs[si + 1], 1 - par)) if si + 1 < len(segs) else []
                    batch = list(range(NCH) if d == 0 else range(NCH - 1, -1, -1))
                    nbch = len(batch)
                    if not smp:
                        MSET("dve", Z32, 0.0, ["Z32"])
                        MSET("pool", Zb, 0.0, ["Zb"])

                    def slots(ch):
                        return AR4[:, ch, 0, :], AR4[:, ch, 2, :], AR4[:, ch, 3, :], AR4[:, ch, 0:2, :].rearrange("p a b -> p (a b)")
                    for bi_, ch in enumerate(batch):
                        abd, bbd, kbd, ar2 = slots(ch)
                        bk = psk(bi_)
                        MM(ps[bi_][:, 0:256], bbd, ar2, True, True, [k4], [bk])
                        MM(ps[bi_][:, 256:512], kbd, ar2, True, True, [k4], [bk])
                    for bi_, ch in enumerate(batch):
                        bk = psk(bi_)
                        TT_("dve", AabT[bi_], pq(bi_, 0), m_s, ALU.mult, [], [("AabT", bi_), bk])
                        TT_("dve", AakT[bi_], pq(bi_, 2), m_s, ALU.mult, [], [("AakT", bi_), bk])
                        CP("act", ArbT[bi_], pq(bi_, 1), [], [("ArbT", bi_), bk])
                        CP("act", ArkT[bi_], pq(bi_, 3), [], [("ArkT", bi_), bk])
                    for bi_, ch in enumerate(batch):
                        TT_("pool", ArbT[bi_], ArbT[bi_], mib, ALU.mult, [("ArbT", bi_), "m_b"], [("ArbT", bi_)])
                        TT_("pool", ArkT[bi_], ArkT[bi_], mib, ALU.mult, [("ArkT", bi_), "m_b"], [("ArkT", bi_)])
                    for bi_, ch in enumerate(batch):
                        abd, bbd, kbd, ar2 = slots(ch)
                        bk = psk(bi_)
                        MM(pq(bi_, 0), abd, bbd, True, True, [k4], [bk])
                        MM(pq(bi_, 1), AakT[bi_], Vtok[:, seg * NCH + ch, :], True, True, [("AakT", bi_), "Vtok"], [bk])
                    for bi_, ch in enumerate(batch):
                        bk = psk(bi_)
                        TT_("dve", Xa[bi_][0], pq(bi_, 0), m_ts, ALU.mult, [], [("X", bi_, 0), bk])
                        CP("act", Wvb[bi_], pq(bi_, 1), [], [("Wvb", bi_), bk])
                        TT_("pool", PT[bi_][0], AabT[bi_], identb[:], ALU.add, [("AabT", bi_), "identb"], [("PT", bi_, 0)])
                    cur = {}
                    for bi_ in range(nbch):
                        cur[bi_] = (Xa[bi_][0], AabT[bi_], ("X", bi_, 0), ("AabT", bi_), 0)
                    for k in range(1, 6):
                        nb = k % 2
                        for bi_ in range(nbch):
                            bk = psk(bi_)
                            Xc, XTc, Xk, XTk, pc = cur[bi_]
                            MM(pq(bi_, 0), XTc, Xc, True, True, [Xk, XTk], [bk])
                            if k < 5:
                                MM(pq(bi_, 1), Xc, XTc, True, True, [Xk, XTk], [bk])
                        for bi_ in range(nbch):
                            bk = psk(bi_)
                            CP("act", Xa[bi_][nb], pq(bi_, 0), [], [("X", bi_, nb), bk])
                            if k < 5:
                                CP("act", XTa[bi_][nb], pq(bi_, 1), [], [("XT", bi_, nb), bk])
                        for bi_ in range(nbch):
                            bk = psk(bi_)
                            pc = cur[bi_][4]
                            MM(pq(bi_, 2), Xa[bi_][nb], PT[bi_][pc], True, True, [("X", bi_, nb), ("PT", bi_, pc)], [bk])
                        for bi_ in range(nbch):
                            bk = psk(bi_)
                            pc = cur[bi_][4]
                            TT_("dve", PT[bi_][1 - pc], pq(bi_, 2), PT[bi_][pc], ALU.add, [("PT", bi_, pc)], [("PT", bi_, 1 - pc), bk])
                            cur[bi_] = (Xa[bi_][nb], XTa[bi_][nb], ("X", bi_, nb), ("XT", bi_, nb), 1 - pc)
                    for bi_, ch in enumerate(batch):
                        bk = psk(bi_)
                        pc = cur[bi_][4]
                        MM(pq(bi_, 0), atok[:, ch, :], PT[bi_][pc], True, True, [ka_, ("PT", bi_, pc)], [bk])
                        CP("act", LUt[bi_], pq(bi_, 0), [], [("LUt", bi_), bk])
                    qn_ = (len(nxt) + nbch - 1) // nbch if nxt else 0
                    for bi_, ch in enumerate(batch):
                        chg = seg * NCH + ch
                        rbd = AR4[:, ch, 1, :]
                        pc = cur[bi_][4]
                        Tinv = PT[bi_][pc]; tk = ("PT", bi_, pc)
                        MM(ps[6][:, 0:128], LUt[bi_], Zb, True, False, [("LUt", bi_), "Zb"], [psk(6)])
                        MM(ps[6][:, 0:128], Tinv, Wvb[bi_], False, True, [tk, ("Wvb", bi_)], [psk(6)])
                        CP("act", Ub, ps[6][:, 0:128], [psk(6)], ["Ub"])
                        MM(ps[4][:, 0:128], Zb, rbd, True, False, ["Zb", k4], [psk(4)])
                        MM(ps[4][:, 0:128], Ub, ArbT[bi_], False, False, ["Ub", ("ArbT", bi_)], [psk(4)])
                        MM(ps[4][:, 0:128], Vtok[:, chg, :], ArkT[bi_], False, True, ["Vtok", ("ArkT", bi_)], [psk(4)])
                        MM(ps[5][:, 0:128], btok[:, ch, :], Ub, True, False, [kb, "Ub"], [psk(5)])
                        MM(ps[5][:, 0:128], ktok[:, ch, :], Vtok[:, chg, :], False, True, [kk_, "Vtok"], [psk(5)])
                        STT(Zb, Z32, dcol[:, ch:ch + 1], ps[5][:, 0:128], ALU.mult, ALU.add, ["Z32", kd_, psk(5)], ["Zb"])
                        STT(Z32, Z32, dcol[:, ch:ch + 1], ps[5][:, 0:128], ALU.mult, ALU.add, ["Z32", kd_, psk(5)], ["Z32"])
                        tsl = slice(seg * SEG + ch * 64, seg * SEG + (ch + 1) * 64)
                        if d == 0:
                            CP("act", y0[0:64, tsl], ps[4][0:64, 0:64], [psk(4)], ["y0"])
                            CP("act", y0[64:128, tsl], ps[4][64:128, 64:128], [psk(4)], ["y0"])
                        else:
                            TT_("dve", y0[0:64, tsl], y0[0:64, tsl], ps[4][0:64, 0:64], ALU.add, [psk(4), "y0"], ["y0"])
                            TT_("dve", y0[64:128, tsl], y0[64:128, tsl], ps[4][64:128, 64:128], ALU.add, [psk(4), "y0"], ["y0"])
                        if nxt:
                            P.ops.extend(nxt[bi_ * qn_:(bi_ + 1) * qn_])
                    if nxt and nbch * qn_ < len(nxt):
                        P.ops.extend(nxt[nbch * qn_:])
                    if not smp:
                        TRN(ps[5][:, 0:128], Z32, ident[:], ["Z32", "ident"], [psk(5)])
                        CP("dve", ztmp, ps[5][:, 0:128], [psk(5)], ["ztmp"])
                        DMA(O["o_rw"][seg, l, d, 2 * hp], ztmp[0:64, 0:64], ["ztmp"], ["o_rw"])
                        DMA(O["o_rw"][seg, l, d, 2 * hp + 1], ztmp[64:128, 64:128], ["ztmp"], ["o_rw"])
            P.barrier()
            for tl in range(ntl):
                sl = slice(tl * TL, (tl + 1) * TL)
                ob = tl % 2
                MM(ps[0][:, 0:TL], blkf[:], y0[:, sl], True, True, ["blk", "y0"], [psk(0)])
                STT(yc, ps[0][:, 0:TL], -1.0 / 64, y0[:, sl], ALU.mult, ALU.add, [psk(0), "y0"], ["yc"])
                ACTF(yn, yc, AF.Square, ["yc"], ["yn"])
                MM(ps[1][:, 0:TL], blkf[:], yn, True, True, ["blk", "yn"], [psk(1)])
                RSTD(rstd, ps[1][:, 0:TL], [psk(1)], ["rstd"], 1.0 / 64, 64e-5)
                TT_("dve", yn, yc, rstd, ALU.mult, ["yc", "rstd"], ["yn"])
                TS_("dve", yn, yn, VB[:, lw0 + hp:lw0 + hp + 1], VB[:, lb0 + hp:lb0 + hp + 1], ALU.mult, ALU.add, ["yn", "VB"], ["yn"])
                STT(bi, rp[:, sl], VB[:, rk0 + hp:rk0 + hp + 1], kdsum[:, sl], ALU.mult, ALU.mult, ["rp", "VB", "kdsum"], ["bi"])
                MM(ps[2][:, 0:TL], blkf[:], bi, True, True, ["blk", "bi"], [psk(2)])
                TT_("dve", bi, ps[2][:, 0:TL], vp[:, sl], ALU.mult, [psk(2), "vp"], ["bi"])
                TT_("pool", yn, yn, bi, ALU.add, ["yn", "bi"], ["yn"])
                MM(ps[3][:, 0:TL], g2b[:, hp * 128:(hp + 1) * 128], sgb[:, sl], True, True, ["g2b", "sgb"], [psk(3)])
                TT_("dve", yo[ob], ps[3][:, 0:TL], yn, ALU.mult, [psk(3), "yn"], [("yo", ob)])
                DMA(scr_m[512 + hp * 128:512 + (hp + 1) * 128, t0 + tl * TL:t0 + (tl + 1) * TL], yo[ob], [("yo", ob)], ["scr_m"])
            P.barrier()
        P.barrier()


import numpy as np

D = 2048


def ktail_stageC(ctx, l):
    g = ctx
    P, AR, ps, psk, W, MOD, VB, VBI = g["P"], g["AR"], g["ps"], g["psk"], g["W"], g["MOD"], g["VB"], g["VBI"]
    MM, ACTF, TT_, TS_, STT, CP, DMA, MSET = g["MM"], g["ACTF"], g["TT_"], g["TS_"], g["STT"], g["CP"], g["DMA"], g["MSET"]
    cfg = g["cfg"]; TS = cfg.TS; NT = g["NT"]
    scr_x, scr_m, scr_r, scr_xv = g["scr_x"], g["scr_m"], g["scr_r"], g["scr_xv"]
    wo = W["w_out"][l].rearrange("(c p) n -> p c n", p=128)
    w1 = W["mlp_w1"][l].rearrange("(c p) n -> p c n", p=128)
    w2 = W["mlp_w2"][l]
    for g0 in range(0, NT, 2):
        tiles = list(range(g0, min(g0 + 2, NT)))
        ntl = len(tiles)
        tok0 = tiles[0] * 512; ntok = ntl * 512
        AR.reset()
        mT = AR.get([128, 16, 1024], BF16)
        mst = [AR.get([128, 1024]) for _ in range(2)]
        rs = [AR.get([128, 1024]) for _ in range(2)]
        xst = [AR.get([128, 512]) for _ in range(4)]
        DMA(rs[0][:, 0:ntok], scr_r[0, :, tok0:tok0 + ntok], ["scr_r"], [("rs", 0)])
        DMA(rs[1][:, 0:ntok], scr_r[1, :, tok0:tok0 + ntok], ["scr_r"], [("rs", 1)])
        for c in range(16):
            b = c % 2
            DMA(mst[b][:, 0:ntok], scr_m[c * 128:(c + 1) * 128, tok0:tok0 + ntok], ["scr_m"], [("mst", b)])
            if c < 4:
                col = VBI["s5_on"] + c
                STT(mT[:, c, 0:ntok], mst[b][:, 0:ntok], VB[:, col:col + 1], rs[0][:, 0:ntok], ALU.mult, ALU.mult,
                    [("mst", b), "VB", ("rs", 0)], [("mT", c)])
            elif c < 8:
                CP("act", mT[:, c, 0:ntok], mst[b][:, 0:ntok], [("mst", b)], [("mT", c)])
            else:
                col = VBI["mla_on"] + c - 8
                STT(mT[:, c, 0:ntok], mst[b][:, 0:ntok], VB[:, col:col + 1], rs[1][:, 0:ntok], ALU.mult, ALU.mult,
                    [("mst", b), "VB", ("rs", 1)], [("mT", c)])
        cnt = 0
        mkeys = [("mT", c) for c in range(16)]
        for blk in range(D // 256):
            wb, wk = g["load_w"](wo[:, :, blk * 256:(blk + 1) * 256], (16, 256))
            for m in range(2):
                dc = blk * 2 + m
                for ti, tl in enumerate(tiles):
                    j = 0 if tl * 512 < TS else 1
                    pb = cnt % 2
                    xb = cnt % 4
                    DMA(xst[xb], scr_x[dc * 128:(dc + 1) * 128, tl * 512:(tl + 1) * 512], [("scr_x", dc, tl)], [("xst", xb)], q="pool")
                    for kc in range(16):
                        MM(ps[pb][:], wb[:, kc, m * 128:(m + 1) * 128], mT[:, kc, ti * 512:(ti + 1) * 512], kc == 0, kc == 15,
                           [wk] + mkeys, [psk(pb)])
                    STT(xst[xb], ps[pb][:], MOD[:, 32 + dc, j:j + 1], xst[xb], ALU.mult, ALU.add, [psk(pb), "MOD", ("xst", xb)], [("xst", xb)])
                    DMA(scr_x[dc * 128:(dc + 1) * 128, tl * 512:(tl + 1) * 512], xst[xb], [("xst", xb)], [("scr_x", dc, tl)], q="pool")
                    cnt += 1
        P.barrier()
        AR.reset()
        hT = AR.get([128, 16, 1024], BF16)
        mark = AR.off
        xg = AR.get([128, 16, 512]); sqb = AR.get([128, 16, 512], BF16); rstd = AR.get([128, 512])
        tmp = [AR.get([128, 512]) for _ in range(2)]
        for ti, tl in enumerate(tiles):
            j = 0 if tl * 512 < TS else 1
            DMA(xg, scr_xv[:, :, tl * 512:(tl + 1) * 512], ["scr_x"], ["xg"])
            g["norm_mod"](xg, "xg", hT[:, :, ti * 512:(ti + 1) * 512], ("hT", ti), 1, j, sqb, rstd, tmp)
        P.barrier()
        AR.off = mark
        yacc = AR.get([128, 16, 1024])
        act = [AR.get([128, 2, 1024], BF16) for _ in range(2)]
        rl = [AR.get([128, 512], BF16) for _ in range(2)]
        xst = [AR.get([128, 512]) for _ in range(4)]
        P.op("pool", lambda e: e.memset(yacc, 0.0), (), [("yacc", dc_, ti_) for dc_ in range(16) for ti_ in range(2)])
        hk = [("hT", ti) for ti in range(ntl)]
        cnt = 0
        cw = [0, 0]
        def emit_w1(fb):
            ab = fb % 2
            wb, wk = g["load_w"](w1[:, :, fb * 256:(fb + 1) * 256], (16, 256))
            for m in range(2):
                for ti in range(ntl):
                    pb = cw[0] % 2
                    cw[0] += 1
                    for kc in range(16):
                        MM(ps[pb][:], wb[:, kc, m * 128:(m + 1) * 128], hT[:, kc, ti * 512:(ti + 1) * 512], kc == 0, kc == 15,
                           [wk] + hk, [psk(pb)])
                    ACTF(rl[pb], ps[pb][:], AF.Relu, [psk(pb)], [("rl", pb)])
                    TT_("pool", act[ab][:, m, ti * 512:(ti + 1) * 512], rl[pb], rl[pb], ALU.mult, [("rl", pb)], [("act", ab)])
        def emit_w2(fb):
            ab = fb % 2
            w2b_, w2k = g["load_w"](w2[fb * 256:(fb + 1) * 256, :].rearrange("(c p) n -> p c n", p=128), (2, D))
            for dc in range(16):
                for ti in range(ntl):
                    pb = 2 + cw[1] % 4
                    cw[1] += 1
                    for m in range(2):
                        MM(ps[pb][:], w2b_[:, m, dc * 128:(dc + 1) * 128], act[ab][:, m, ti * 512:(ti + 1) * 512], m == 0, m == 1,
                           [w2k, ("act", ab)], [psk(pb)])
                    TT_("dve", yacc[:, dc, ti * 512:(ti + 1) * 512], yacc[:, dc, ti * 512:(ti + 1) * 512], ps[pb][:], ALU.add,
                        [psk(pb), ("yacc", dc, ti)], [("yacc", dc, ti)])
        NFB = 4 * D // 256
        emit_w1(0)
        for fb in range(NFB):
            if fb + 1 < NFB:
                emit_w1(fb + 1)
            emit_w2(fb)
        cnt = 0
        for dc in range(16):
            for ti, tl in enumerate(tiles):
                j = 0 if tl * 512 < TS else 1
                pb = cnt % 4
                cnt += 1
                DMA(xst[pb], scr_x[dc * 128:(dc + 1) * 128, tl * 512:(tl + 1) * 512], [("scr_x", dc, tl)], [("xst", pb)], q="pool")
                STT(xst[pb], yacc[:, dc, ti * 512:(ti + 1) * 512], MOD[:, 80 + dc, j:j + 1], xst[pb], ALU.mult, ALU.add,
                    [("yacc", dc, ti), "MOD", ("xst", pb)], [("xst", pb)])
                DMA(scr_x[dc * 128:(dc + 1) * 128, tl * 512:(tl + 1) * 512], xst[pb], [("xst", pb)], [("scr_x", dc, tl)], q="pool")
        P.barrier()


def ktail_final(ctx):
    g = ctx
    P, AR, ps, psk, O = g["P"], g["AR"], g["ps"], g["psk"], g["O"]
    MM, TRN, ACTF, STT, CP, DMA, RSTD = g["MM"], g["TRN"], g["ACTF"], g["STT"], g["CP"], g["DMA"], g["RSTD"]
    cfg = g["cfg"]; TS = cfg.TS; NT = g["NT"]
    ident, ones_bf, gfin, scr_xv = g["ident"], g["ones_bf"], g["gfin"], g["scr_xv"]
    AR.reset()
    xg = AR.get([128, 16, 512]); sqb = AR.get([128, 16, 512], BF16); rstd = AR.get([128, 512])
    yT = AR.get([128, 16, 512])
    yt = [AR.get([128, D]) for _ in range(2)]
    n = 0
    for tl in range(NT):
        DMA(xg, scr_xv[:, :, tl * 512:(tl + 1) * 512], ["scr_x"], ["xg"])
        ACTF(sqb, xg, AF.Square, ["xg"], ["sqb"])
        for c in range(16):
            MM(ps[7][:], ones_bf[:], sqb[:, c, :], c == 0, c == 15, ["sqb", "ones_bf"], [psk(7)])
        RSTD(rstd, ps[7][:], [psk(7)], ["rstd"], 1.0 / D, 1e-6)
        for c in range(16):
            STT(yT[:, c, :], xg[:, c, :], gfin[:, c:c + 1], rstd, ALU.mult, ALU.mult, ["xg", "gfin", "rstd"], ["yT"])
        for nb in range(4):
            b = n % 2
            n += 1
            for q in range(4):
                pb = q % 2
                for j in range(4):
                    c = 4 * q + j
                    TRN(ps[pb][:, j * 128:(j + 1) * 128], yT[:, c, nb * 128:(nb + 1) * 128], ident[:], ["yT", "ident"], [psk(pb)])
                CP("act" if q % 2 else "dve", yt[b][:, q * 512:(q + 1) * 512], ps[pb][:], [psk(pb)], [("yt", b)])
            tok = tl * 512 + nb * 128
            dst = O["y_s"][tok:tok + 128, :] if tok < TS else O["y_p"][tok - TS:tok - TS + 128, :]
            DMA(dst, yt[b], [("yt", b)], ["yout"])
    P.barrier()


import math
import numpy as np
from contextlib import ExitStack
import concourse.bass as bass
import concourse.mybir as mybir

D = 2048
KC = 16
NCOLS_IN = 3136
PAST = 256
EPS = 1e-6


class Cfg:
    def __init__(self, TS=2048, NP=4, TP=256, depth=2, upto="all", dbg=False):
        self.TS, self.NP, self.TP, self.depth, self.upto, self.dbg = TS, NP, TP, depth, upto, dbg
        self.TT = TS + NP * TP


WSPEC = [
    ("norm_mix", [D]), ("norm_mlp", [D]), ("w_ada", [D, 6 * D]), ("b_ada", [6 * D]), ("w_in", [D, NCOLS_IN]),
    ("w_out", [D, D]), ("s5_a_re", [2, 32, 64]), ("s5_a_im", [2, 32, 64]), ("s5_log_dt", [2, 32]),
    ("s5_b_re", [2, 32, 64, 16]), ("s5_b_im", [2, 32, 64, 16]), ("s5_c_re", [2, 32, 16, 64]), ("s5_c_im", [2, 32, 16, 64]),
    ("s5_d", [512]), ("s5_w_glu", [512, 1024]), ("s5_out_norm", [512]), ("rwkv_mu", [1792]), ("rwkv_w0", [2, 512]),
    ("rwkv_w2", [2, 64, 512]), ("rwkv_a0", [2, 512]), ("rwkv_a2", [2, 64, 512]), ("rwkv_g2", [128, 512]),
    ("rwkv_k_k", [512]), ("rwkv_k_a", [512]), ("rwkv_r_k", [8, 64]), ("rwkv_ln_w", [512]), ("rwkv_ln_b", [512]),
    ("mla_q_norm", [512]), ("mla_w_uq", [512, 1536]), ("mla_kv_norm", [256]), ("mla_w_ukv", [256, 2048]),
    ("mla_out_norm", [1024]), ("mlp_w1", [D, 4 * D]), ("mlp_w2", [4 * D, D]),
]


def host_consts(cfg):
    c = {}
    c["ident"] = np.eye(128, dtype=np.float32)
    blk = np.zeros((128, 128), np.float32); blk[:64, :64] = 1; blk[64:, 64:] = 1
    c["blk"] = blk
    Rm = np.zeros((64, 64), np.float32)
    for half in range(2):
        o = half * 32
        for i in range(16):
            Rm[o + i + 16, o + i] = -1.0
            Rm[o + i, o + i + 16] = 1.0
    c["rm"] = Rm
    T = cfg.TS
    t = np.arange(T)
    row = (t // 64).astype(np.float32); col = (t % 64).astype(np.float32)
    inv = (1.0 / (10000.0 ** (np.arange(0, 32, 2, dtype=np.float32) / 32))).astype(np.float32)
    ang = np.concatenate([row[:, None] * inv, row[:, None] * inv, col[:, None] * inv, col[:, None] * inv], -1)
    c["cosT"] = np.ascontiguousarray(np.cos(ang).T.astype(np.float32))
    c["sinT"] = np.ascontiguousarray(np.sin(ang).T.astype(np.float32))
    sel2 = np.zeros((2, 128), np.float32); sel2[0, :64] = 1; sel2[1, 64:] = 1
    c["sel2"] = sel2
    m4 = np.zeros((128, 4), np.float32)
    for r in range(128):
        m4[r, r // 32] = 1
    c["mask4"] = m4
    m8 = np.zeros((128, 8), np.float32)
    for r in range(128):
        m8[r, r // 16] = 1
    c["mask8"] = m8
    s_ = np.arange(64)[:, None]; t_ = np.arange(64)[None, :]
    def bd(m):
        z = np.zeros((128, 128), np.float32); z[:64, :64] = m; z[64:, 64:] = m; return z
    c["m_su"] = bd((s_ < t_).astype(np.float32)); c["m_iu"] = bd((s_ <= t_).astype(np.float32))
    c["m_sl"] = bd((s_ > t_).astype(np.float32)); c["m_il"] = bd((s_ >= t_).astype(np.float32))
    rs = np.ones((128, 512), np.float32); rs[:, ::64] = 0
    c["reset"] = rs
    c["twopi"] = np.full((128, 64), 2 * math.pi, np.float32)
    c["ki_tab"] = np.ascontiguousarray(np.broadcast_to(np.arange(32).astype(np.float32), (128, 32)))
    r256 = np.ones((128, 1024), np.float32); r256[:, ::256] = 0
    c["rst256"] = r256
    c["ji_tab"] = np.ascontiguousarray(np.broadcast_to(np.arange(64).astype(np.float32), (128, 64)))
    return c


def build(cfg):
    nc = bass.Bass("TRN2", target_bir_lowering=False)
    TS, NP, TP, L, TT = cfg.TS, cfg.NP, cfg.TP, cfg.depth, cfg.TT
    assert TS % 512 == 0 and (NP * TP) % 512 == 0 and TP % 128 == 0
    NT = TT // 512
    dt_in = lambda n, s: nc.dram_tensor(n, list(s), F32, kind="ExternalInput").ap()
    dt_out = lambda n, s: nc.dram_tensor(n, list(s), F32, kind="ExternalOutput").ap()
    dt_scr = lambda n, s, d=F32: nc.dram_tensor(n, list(s), d, kind="Internal").ap()
    I = {}
    I["xs"] = dt_in("xs", [TS, D]); I["xp"] = dt_in("xp", [NP * TP, D])
    I["c_ckv"] = dt_in("c_ckv", [L, PAST, 256]); I["c_kr"] = dt_in("c_kr", [L, PAST, 64])
    I["st_s5"] = dt_in("st_s5", [L, 2, 32, 64, 2]); I["st_rw"] = dt_in("st_rw", [L, 2, 8, 64, 64])
    I["cvec"] = dt_in("cvec", [2, D]); I["norm_final"] = dt_in("norm_final", [D])
    W = {n: dt_in(n, [L] + s) for n, s in WSPEC}
    HC = host_consts(cfg)
    C = {n: dt_in("k_" + n, a.shape) for n, a in HC.items()}
    O = {}
    O["y_s"] = dt_out("y_s", [TS, D]); O["y_p"] = dt_out("y_p", [NP * TP, D])
    O["o_ckv"] = dt_out("o_ckv", [NP, L, TP, 256]); O["o_kr"] = dt_out("o_kr", [NP, L, TP, 64])
    O["o_s5"] = dt_out("o_s5", [NP, L, 2, 32, 64, 2]); O["o_rw"] = dt_out("o_rw", [NP, L, 2, 8, 64, 64])
    scr_x = dt_scr("scr_x", [D, TT]); scr_p = dt_scr("scr_p", [3200, TT]); scr_m = dt_scr("scr_m", [D, TT])
    scr_r = dt_scr("scr_r", [2, 128, TT])
    if cfg.dbg:
        O["d_x"] = dt_out("d_x", [D, TT]); O["d_p"] = dt_out("d_p", [3200, TT]); O["d_m"] = dt_out("d_m", [D, TT])
        O["d_r"] = dt_out("d_r", [2, 128, TT])

    seqs = [(0, TS, True, -1)] + [(TS + i * TP, TP, False, i) for i in range(NP)]

    st = ExitStack()
    P = Prog(nc)
    sb = lambda n, s, d=F32: st.enter_context(nc.sbuf_tensor(n, list(s), d))
    ident = sb("ident", [128, 128]); blkf = sb("blkf", [128, 128]); ones_bf = sb("ones_bf", [128, 128], BF16)
    blk_bf = sb("blk_bf", [128, 128], BF16); identb = sb("identb", [128, 128], BF16)
    rm = sb("rm", [64, 64]); sel2 = sb("sel2", [2, 128]); mask4 = sb("mask4", [128, 4]); mask8 = sb("mask8", [128, 8])
    msk = {k: sb(k, [128, 128]) for k in ("m_su", "m_iu", "m_sl", "m_il")}
    reset = sb("reset", [128, 512]); twopi = sb("twopi", [128, 64])
    VA = sb("VA", [128, 128]); VB = sb("VB", [128, 128]); MOD = sb("MOD", [128, 96, 2]); AMs = sb("AMs", [128, 2, 16, 2])
    scT = sb("scT", [128, 16, 2]); scTb = sb("scTb", [128, 16, 2], BF16); gfin = sb("gfin", [128, 16]); omm = sb("omm", [128, 14]); hmu = sb("hmu", [128, 14])
    omka = sb("omka", [128, 4])
    wst = [sb(f"wst{i}", [128, 4096]) for i in range(2)]
    wbf = [sb(f"wbf{i}", [128, 4096], BF16) for i in range(2)]
    ARENA = 38500
    arena = sb("arena", [128, ARENA])
    ps = [st.enter_context(nc.psum_tensor(f"ps{i}", [128, 512], F32)) for i in range(8)]
    psk = lambda i: ("ps", i)

    class Arena:
        def __init__(self):
            self.off = 0
        def reset(self):
            self.off = 0
        def get(self, shape, dt=F32):
            n = int(np.prod(shape[1:]))
            nf = n if dt == F32 else (n + 1) // 2
            a = arena[:, self.off:self.off + nf]
            self.off += nf
            assert self.off <= ARENA, (self.off, ARENA)
            if dt != F32:
                a = a.bitcast(BF16)[:, 0:n]
            if len(shape) == 3:
                a = a.rearrange("p (a b) -> p a b", a=shape[1])
            elif len(shape) == 4:
                a = a.rearrange("p (a b c) -> p a b c", a=shape[1], b=shape[2])
            if shape[0] < 128:
                a = a[0:shape[0]]
            return a
    AR = Arena()

    def MM(out, lhsT, rhs, start, stop, R, Wk):
        P.op("pe", lambda e: e.matmul(out, lhsT, rhs, start=start, stop=stop), R, Wk)
    def TRN(out, in_, idt, R, Wk):
        P.op("pe", lambda e: e.transpose(out, in_, idt), R, Wk)
    def ACTF(out, in_, func, R, Wk, bias=0.0, scale=1.0):
        P.op("act", lambda e: e.activation(out, in_, func, bias=bias, scale=scale), R, Wk)
    def TT_(eng, out, a, b, op, R, Wk):
        P.op(eng, lambda e: e.tensor_tensor(out, a, b, op), R, Wk)
    def TS_(eng, out, a, s1, s2, op0, op1, R, Wk):
        if s2 is None and eng == "pool" and op0 in (ALU.mult, ALU.add):
            s2, op1 = (0.0, ALU.add) if op0 == ALU.mult else (1.0, ALU.mult)
        if s2 is None:
            P.op(eng, lambda e: e.tensor_scalar(out, a, s1, None, op0), R, Wk)
        else:
            P.op(eng, lambda e: e.tensor_scalar(out, a, s1, s2, op0, op1), R, Wk)
    def STT(out, a, s, b, op0, op1, R, Wk):
        P.op("dve", lambda e: e.scalar_tensor_tensor(out, a, s, b, op0, op1), R, Wk)
    def CP(eng, out, in_, R, Wk):
        if eng == "act":
            P.op("act", lambda e: e.copy(out, in_), R, Wk)
        else:
            P.op(eng, lambda e: e.tensor_copy(out, in_), R, Wk)
    def MSET(eng, out, v, Wk):
        P.op(eng, lambda e: e.memset(out, v), (), Wk)
    def DMA(out, in_, R, Wk, slow=False, q="sp"):
        if slow:
            P.dma(q, lambda e: e.dma_start(out=out, in_=in_, allow_slow_non_contiguous=True), R, Wk)
        else:
            P.dma(q, lambda e: e.dma_start(out=out, in_=in_), R, Wk)
    def RSTD(out, in_, R, Wk, scale, eps):
        ACTF(out, in_, AF.Ln, R, Wk, bias=eps, scale=scale)
        ACTF(out, out, AF.Exp, Wk, Wk, scale=-0.5)

    wcount = [0]
    def load_w(src_ap, shape3, R=()):
        i = wcount[0] % 2
        wcount[0] += 1
        a, b = shape3
        sv = wst[i][:, 0:a * b].rearrange("p (a b) -> p a b", a=a)
        bv = wbf[i][:, 0:a * b].rearrange("p (a b) -> p a b", a=a)
        DMA(sv, src_ap, R, [("wst", i)])
        CP("act", bv, sv, [("wst", i)], [("wbf", i)])
        return bv, ("wbf", i)

    for n, t_ in (("ident", ident), ("blk", blkf), ("rm", rm), ("sel2", sel2), ("mask4", mask4), ("mask8", mask8),
                  ("reset", reset), ("twopi", twopi)):
        DMA(t_[:], C[n], (), [n])
    for k in msk:
        DMA(msk[k][:], C[k], (), [k])
    MSET("pool", ones_bf[:], 1.0, ["ones_bf"])
    CP("dve", blk_bf[:], blkf[:], ["blk"], ["blk_bf"])
    CP("dve", identb[:], ident[:], ["ident"], ["identb"])
    for j in range(2):
        DMA(scT[:, :, j], I["cvec"][j].rearrange("(c p) -> p c", p=128), (), ["scT"], slow=True)
    ACTF(scT[:], scT[:], AF.Silu, ["scT"], ["scT"])
    CP("dve", scTb[:], scT[:], ["scT"], ["scTb"])
    AR.reset()
    stg = AR.get([128, 128])
    DMA(stg[0:16, :], I["norm_final"].rearrange("(c p) -> c p", p=128), (), ["stg"])
    TRN(ps[0][:, 0:16], stg[0:16, :], ident[0:16, 0:16], ["stg", "ident"], [psk(0)])
    CP("dve", gfin[:], ps[0][:, 0:16], [psk(0)], ["gfin"])
    P.barrier()

    def stage0():
        AR.reset()
        xin = [AR.get([128, D]) for _ in range(2)]
        xTs = [AR.get([128, 16, 128]) for _ in range(2)]
        scr_xv = scr_x.rearrange("(c p) t -> p c t", p=128)
        for n in range(TT // 128):
            b = n % 2
            src = I["xs"][n * 128:(n + 1) * 128, :] if n * 128 < TS else I["xp"][n * 128 - TS:(n + 1) * 128 - TS, :]
            DMA(xin[b], src, (), [("xin", b)])
            for q in range(4):
                pb = q % 2
                for j in range(4):
                    c = 4 * q + j
                    TRN(ps[pb][:, j * 128:(j + 1) * 128], xin[b][:, c * 128:(c + 1) * 128], ident[:],
                        [("xin", b), "ident"], [psk(pb)])
                CP("act" if q % 2 else "dve", xTs[b][:, 4 * q:4 * q + 4, :],
                   ps[pb][:].rearrange("p (a b) -> p a b", a=4), [psk(pb)], [("xTs", b)])
            DMA(scr_xv[:, :, n * 128:(n + 1) * 128], xTs[b], [("xTs", b)], [("scr_x", n)], q="pool")
        P.barrier()

    VBI = {}
    def layer_vectors(l):
        AR.reset()
        sA = AR.get([128, 128]); sB = AR.get([128, 128])
        DMA(sA[0:96, :], W["b_ada"][l].rearrange("(c p) -> c p", p=128), (), ["sA"])
        DMA(sA[96:112, :], W["norm_mix"][l].rearrange("(c p) -> c p", p=128), (), ["sA"])
        DMA(sA[112:128, :], W["norm_mlp"][l].rearrange("(c p) -> c p", p=128), (), ["sA"])
        TRN(ps[0][:, 0:128], sA, ident[:], ["sA", "ident"], [psk(0)])
        CP("dve", VA[:], ps[0][:, 0:128], [psk(0)], ["VA"])
        MSET("pool", sB, 0.0, ["sB"])
        r = 0
        VBI.clear()
        for nm, ap2 in (("s5_d", W["s5_d"][l]), ("s5_on", W["s5_out_norm"][l]), ("mu", W["rwkv_mu"][l]),
                        ("w0_0", W["rwkv_w0"][l, 0]), ("w0_1", W["rwkv_w0"][l, 1]), ("a0_0", W["rwkv_a0"][l, 0]),
                        ("a0_1", W["rwkv_a0"][l, 1]), ("k_k", W["rwkv_k_k"][l]), ("k_a", W["rwkv_k_a"][l]),
                        ("r_k", W["rwkv_r_k"][l].rearrange("h n -> (h n)")), ("ln_w", W["rwkv_ln_w"][l]),
                        ("ln_b", W["rwkv_ln_b"][l]), ("q_n", W["mla_q_norm"][l]), ("kv_n", W["mla_kv_norm"][l]),
                        ("mla_on", W["mla_out_norm"][l])):
            nr = ap2.shape[0] // 128
            DMA(sB[r:r + nr, :], ap2.rearrange("(c p) -> c p", p=128), ["sB"], ["sB"])
            VBI[nm] = r
            r += nr
        assert r <= 128
        TRN(ps[1][:, 0:128], sB, ident[:], ["sB", "ident"], [psk(1)])
        CP("dve", VB[:], ps[1][:, 0:128], [psk(1)], ["VB"])
        m0 = VBI["mu"]
        TS_("dve", omm[:], VB[:, m0:m0 + 14], -1.0, 1.0, ALU.mult, ALU.add, ["VB"], ["omm"])
        TS_("dve", hmu[:], VB[:, m0:m0 + 14], 0.5, None, ALU.mult, None, ["VB"], ["hmu"])
        ka = VBI["k_a"]
        TS_("dve", omka[:], VB[:, ka:ka + 4], -1.0, 1.0, ALU.mult, ALU.add, ["VB"], ["omka"])
        wv = W["w_ada"][l].rearrange("(c p) n -> p c n", p=128)
        for blk in range(6 * D // 256):
            wb_, wk_ = load_w(wv[:, :, blk * 256:(blk + 1) * 256], (16, 256))
            for m in range(2):
                ch = blk * 2 + m
                pb = ch % 2
                for kc in range(16):
                    MM(ps[pb][:, 0:2], wb_[:, kc, m * 128:(m + 1) * 128], scTb[:, kc, :], kc == 0, kc == 15,
                       [wk_, "scTb"], [psk(pb)])
                TS_("dve", MOD[:, ch, :], ps[pb][:, 0:2], VA[:, ch:ch + 1], None, ALU.add, None, [psk(pb), "VA"], ["MOD"])
        for which, (sc0, g0) in enumerate(((16, 96), (64, 112))):
            for j in range(2):
                TS_("dve", AMs[:, which, :, j], MOD[:, sc0:sc0 + 16, j], 1.0, None, ALU.add, None, ["MOD"], ["AMs"])
                TT_("dve", AMs[:, which, :, j], AMs[:, which, :, j], VA[:, g0:g0 + 16], ALU.mult, ["AMs", "VA"], ["AMs"])
        P.barrier()

    def norm_mod(xg, xk, hT_out, hk, which, j, sqb, rstd, tmp):
        sh0 = 0 if which == 0 else 48
        ACTF(sqb, xg, AF.Square, [xk], ["sqb"])
        for c in range(16):
            MM(ps[7][:], ones_bf[:], sqb[:, c, :], c == 0, c == 15, ["sqb", "ones_bf"], [psk(7)])
        RSTD(rstd, ps[7][:], [psk(7)], ["rstd"], 1.0 / D, EPS)
        for c in range(16):
            t2 = tmp[c % 2]
            STT(t2, xg[:, c, :], AMs[:, which, c, j:j + 1], rstd, ALU.mult, ALU.mult, [xk, "AMs", "rstd"], [("tmp", c % 2)])
            ACTF(hT_out[:, c, :], t2, AF.Identity, [("tmp", c % 2), "MOD"], [hk], bias=MOD[:, sh0 + c, j:j + 1])

    scr_xv = scr_x.rearrange("(c p) t -> p c t", p=128)

    def stageA(l):
        wv = W["w_in"][l].rearrange("(c p) n -> p c n", p=128)
        for g0 in range(0, NT, 2):
            tiles = list(range(g0, min(g0 + 2, NT)))
            AR.reset()
            xg = AR.get([128, 16, 512]); sqb = AR.get([128, 16, 512], BF16)
            hT = AR.get([128, 16, 1024], BF16); rstd = AR.get([128, 512]); tmp = [AR.get([128, 512]) for _ in range(2)]
            ost = [AR.get([128, 512]) for _ in range(2)]
            for ti, tl in enumerate(tiles):
                j = 0 if tl * 512 < TS else 1
                DMA(xg, scr_xv[:, :, tl * 512:(tl + 1) * 512], ["scr_x"], ["xg"])
                norm_mod(xg, "xg", hT[:, :, ti * 512:(ti + 1) * 512], ("hT", ti), 0, j, sqb, rstd, tmp)
            cnt = 0
            for blk in range((NCOLS_IN + 255) // 256):
                c0 = blk * 256
                ncol = min(256, NCOLS_IN - c0)
                wb, wk = load_w(wv[:, :, c0:c0 + ncol], (16, ncol))
                for m0 in range(0, ncol, 128):
                    mw = min(128, ncol - m0)
                    for ti, tl in enumerate(tiles):
                        pb = cnt % 2
                        for kc in range(16):
                            MM(ps[pb][0:mw, :], wb[:, kc, m0:m0 + mw], hT[:, kc, ti * 512:(ti + 1) * 512], kc == 0, kc == 15,
                               [wk, ("hT", ti)], [psk(pb)])
                        CP("act" if cnt % 2 else "dve", ost[pb][0:mw, :], ps[pb][0:mw, :], [psk(pb)], [("ost", pb)])
                        DMA(scr_p[c0 + m0:c0 + m0 + mw, tl * 512:(tl + 1) * 512], ost[pb][0:mw, :], [("ost", pb)], [("scr_p", c0 + m0, tl)], q="pool")
                        cnt += 1
            P.barrier()

    def stageA2(l):
        AR.reset()
        pch = [AR.get([128, TS + 2]) for _ in range(2)]
        t1 = [AR.get([128, TS]) for _ in range(2)]
        t2 = [AR.get([128, TS]) for _ in range(2)]
        n = 0
        for (t0, T, smp, pi) in seqs:
            for ch in range(14):
                b = n % 2
                n += 1
                rows = slice(512 + ch * 128, 512 + (ch + 1) * 128)
                MSET("pool", pch[b][:, 0:1], 0.0, [("pch", b)])
                MSET("pool", pch[b][:, T + 1:T + 2], 0.0, [("pch", b)])
                DMA(pch[b][:, 1:T + 1], scr_p[rows, t0:t0 + T], [("scr_p", ch, t0)], [("pch", b)])
                TT_("dve", t1[b][:, 0:T], pch[b][:, 0:T], pch[b][:, 2:T + 2], ALU.add, [("pch", b)], [("t1", b)])
                ACTF(t2[b][:, 0:T], pch[b][:, 1:T + 1], AF.Copy, [("pch", b), "omm"], [("t2", b)], scale=omm[:, ch:ch + 1])
                STT(t1[b][:, 0:T], t1[b][:, 0:T], hmu[:, ch:ch + 1], t2[b][:, 0:T], ALU.mult, ALU.add,
                    [("t1", b), ("t2", b), "hmu"], [("t1", b)])
                DMA(scr_p[rows, t0:t0 + T], t1[b][:, 0:T], [("t1", b)], [("scr_p", ch, t0)], q="pool")
        P.barrier()

    ctx = dict(nc=nc, cfg=cfg, P=P, I=I, W=W, O=O, C=C, AR=AR, ps=ps, psk=psk, seqs=seqs, VB=VB, VBI=VBI, MOD=MOD, AMs=AMs,
               scr_x=scr_x, scr_p=scr_p, scr_m=scr_m, scr_r=scr_r, scr_xv=scr_xv, ident=ident, identb=identb,
               ones_bf=ones_bf, blk_bf=blk_bf, blkf=blkf, rm=rm, sel2=sel2, mask4=mask4, mask8=mask8, msk=msk, reset=reset,
               twopi=twopi, omka=omka, gfin=gfin, VA=VA, load_w=load_w, norm_mod=norm_mod,
               MM=MM, TRN=TRN, ACTF=ACTF, TT_=TT_, TS_=TS_, STT=STT, CP=CP, MSET=MSET, DMA=DMA, RSTD=RSTD, NT=NT)

    stage0()
    for l in range(L):
        layer_vectors(l)
        stageA(l)
        if cfg.upto == "A":
            break
        stageA2(l)
        kmix_s5(ctx, l)
        if cfg.upto == "s5":
            break
        krwkv_rwkv(ctx, l)
        if cfg.upto == "rwkv":
            break
        kmix_mla(ctx, l)
        if cfg.upto == "mla":
            break
        ktail_stageC(ctx, l)
    if cfg.upto == "all":
        ktail_final(ctx)
    if cfg.dbg:
        P.barrier()
        DMA(O["d_x"], scr_x, ["scr_x"], ["d_x"]); DMA(O["d_p"], scr_p, ["scr_p"], ["d_p"])
        DMA(O["d_m"], scr_m, ["scr_m"], ["d_m"]); DMA(O["d_r"], scr_r, ["scr_r"], ["d_r"])
    P.emit(st)
    st.close()
    return nc, HC, P.stats


from concourse.bass_utils import run_bass_kernel_spmd

_CACHE = {}


def kernel(**inp):
    inp = {k: np.asarray(v) for k, v in inp.items()}
    cfg = Cfg(TS=2048, NP=4, TP=256, depth=2, upto="all", dbg=False)
    if "nc" not in _CACHE:
        _CACHE["nc"] = build(cfg)
    nc, HC, stats = _CACHE["nc"]
    L = 2
    in_maps = []
    shared = {n: np.ascontiguousarray(inp[n], dtype=np.float32) for n, s in WSPEC}
    shared["norm_final"] = np.ascontiguousarray(inp["norm_final"], dtype=np.float32)
    for k, v in HC.items():
        shared["k_" + k] = v
    for b in range(8):
        m = dict(shared)
        m["xs"] = np.ascontiguousarray(inp["x_sample"][b])
        m["xp"] = np.ascontiguousarray(inp["x_prompt"][4 * b:4 * b + 4].reshape(1024, 2048))
        m["c_ckv"] = np.ascontiguousarray(inp["cache_mla_ckv"][b]); m["c_kr"] = np.ascontiguousarray(inp["cache_mla_krope"][b])
        m["st_s5"] = np.ascontiguousarray(inp["state_s5"][b]); m["st_rw"] = np.ascontiguousarray(inp["state_rwkv"][b])
        m["cvec"] = np.ascontiguousarray(np.stack([inp["c"][b], inp["c_ctx"]]).astype(np.float32))
        in_maps.append(m)
    res = run_bass_kernel_spmd(nc, in_maps, core_ids=list(range(8)))
    R = res.results
    y_p = np.concatenate([r["y_p"].reshape(4, 256, 2048) for r in R], 0)
    y_s = np.stack([r["y_s"] for r in R], 0)
    o_ckv = np.concatenate([r["o_ckv"] for r in R], 0)
    o_kr = np.concatenate([r["o_kr"] for r in R], 0)
    o_s5 = np.concatenate([r["o_s5"] for r in R], 0)
    o_rw = np.concatenate([r["o_rw"] for r in R], 0)
    return (y_p.astype(np.float32), y_s.astype(np.float32), o_ckv.astype(np.float32), o_kr.astype(np.float32),
            o_s5.astype(np.float32), o_rw.astype(np.float32))
```
